# Optimizing a Trainium2 kernel written in Bass

```python
import jax, jax.numpy as jnp
from jax import lax
import numpy as np

D_MODEL = 2048
BATCH = 2
SEQ = 4096
DEPTH = 1

GRID_W = 64
PLE_DIM = 256
D_ATTN = 1024
N_HEADS = 8
N_KV_HEADS = 2
HEAD_DIM = D_ATTN // N_HEADS
ROPE_THETA = 10000.0
Q_BLOCK = 128
D_SSM = 1024
SSM_GROUP = 16
N_SSM_GROUPS = D_SSM // SSM_GROUP
SSM_STATE = 64
DT_MIN = 0.001
DT_MAX = 0.1
D_MIX = D_ATTN + D_SSM
D_KV = N_KV_HEADS * HEAD_DIM
IN_SPLITS = (D_ATTN, D_KV, D_KV, D_ATTN, D_SSM, D_SSM)
D_IN = sum(IN_SPLITS)
EPS = 1e-6

kernel_name = "hybrid_gqa_axialrope_bis5_block"


def rms_norm(x, g):
    xf = x.astype(jnp.float32)
    y = xf * lax.rsqrt(jnp.mean(xf * xf, axis=-1, keepdims=True) + EPS)
    return (y * g.astype(jnp.float32)).astype(x.dtype)


def _rotate(x, ang):
    x1, x2 = jnp.split(x, 2, axis=-1)
    c = jnp.cos(ang)[None, :, None, :]
    s = jnp.sin(ang)[None, :, None, :]
    return jnp.concatenate([x1 * c - x2 * s, x2 * c + x1 * s], axis=-1)


def axial_rope(x, ang_row, ang_col):
    xf = x.astype(jnp.float32)
    half = x.shape[-1] // 2
    out = jnp.concatenate([_rotate(xf[..., :half], ang_row),
                           _rotate(xf[..., half:], ang_col)], axis=-1)
    return out.astype(x.dtype)


def grid_angles(L):
    rows_n = L // GRID_W
    rows = jnp.repeat(jnp.arange(rows_n), GRID_W).astype(jnp.float32)
    cols = jnp.tile(jnp.arange(GRID_W), rows_n).astype(jnp.float32)
    n_freq = HEAD_DIM // 4
    inv_freq = ROPE_THETA ** (-jnp.arange(n_freq, dtype=jnp.float32) / n_freq)
    return rows[:, None] * inv_freq[None, :], cols[:, None] * inv_freq[None, :]


def block_attention(q, k, v):
    bsz, L, H, Dh = q.shape
    kv = k.shape[2]
    rep = H // kv
    nb = L // Q_BLOCK
    scale = Dh ** -0.5
    qb = q.reshape(bsz, nb, Q_BLOCK, kv, rep, Dh).transpose(1, 0, 2, 3, 4, 5)

    def one_block(qblk):
        s = jnp.einsum('bqkrd,bskd->bkrqs', qblk, k).astype(jnp.float32) * scale
        pr = jax.nn.softmax(s, axis=-1).astype(v.dtype)
        return jnp.einsum('bkrqs,bskd->bqkrd', pr, v)

    o = lax.map(one_block, qb)
    return o.transpose(1, 0, 2, 3, 4, 5).reshape(bsz, L, H * Dh)


def _lin_combine(e1, e2):
    a1, b1 = e1
    a2, b2 = e2
    return a1 * a2, a2 * b1 + b2


def s5_bidirectional(u, a_re, a_im, log_dt, b_re, b_im, c_re, c_im, d):
    bsz, L, _ = u.shape
    ug = u.astype(jnp.float32).reshape(bsz, L, N_SSM_GROUPS, SSM_GROUP)
    uc = ug.astype(jnp.complex64)
    y = d.astype(jnp.float32).reshape(N_SSM_GROUPS, SSM_GROUP) * ug
    for direction in range(2):
        lam = lax.complex(jnp.minimum(a_re[direction].astype(jnp.float32), -1e-4),
                          a_im[direction].astype(jnp.float32))
        dt = jnp.exp(log_dt[direction].astype(jnp.float32))[:, None]
        lam_bar = jnp.exp(lam * dt)
        bmat = lax.complex(b_re[direction].astype(jnp.float32),
                           b_im[direction].astype(jnp.float32))
        b_bar = ((lam_bar - 1.0) / lam)[..., None] * bmat
        bu = jnp.einsum('blgh,gph->blgp', uc, b_bar)
        a = jnp.broadcast_to(lam_bar, bu.shape)
        _, xs = lax.associative_scan(_lin_combine, (a, bu), axis=1,
                                     reverse=(direction == 1))
        cmat = lax.complex(c_re[direction].astype(jnp.float32),
                           c_im[direction].astype(jnp.float32))
        y = y + jnp.real(jnp.einsum('blgp,ghp->blgh', xs, cmat))
    return y.reshape(bsz, L, D_SSM)


def setup_inputs(seed: int = 0) -> dict:
    key = jax.random.key(seed)
    ks = jax.random.split(key, 24)
    f32 = jnp.float32
    G, P, H = N_SSM_GROUPS, SSM_STATE, SSM_GROUP
    nrm = lambda k, shape, s: jax.random.normal(k, shape, f32) * s
    x = jax.random.normal(ks[0], (BATCH, SEQ, D_MODEL), f32)
    p = jax.random.normal(ks[1], (DEPTH, BATCH, SEQ, PLE_DIM), f32)
    norm_mix = 1.0 + nrm(ks[2], (DEPTH, D_MODEL), 0.02)
    w_in = nrm(ks[3], (DEPTH, D_MODEL, D_IN), D_MODEL ** -0.5)
    q_norm = 1.0 + nrm(ks[4], (DEPTH, HEAD_DIM), 0.02)
    k_norm = 1.0 + nrm(ks[5], (DEPTH, HEAD_DIM), 0.02)
    ssm_a_re = -0.5 + nrm(ks[6], (DEPTH, 2, G, P), 0.01)
    ssm_a_im = (np.pi * jnp.arange(P, dtype=f32))[None, None, None, :] + nrm(ks[7], (DEPTH, 2, G, P), 0.01)
    ssm_log_dt = jax.random.uniform(ks[8], (DEPTH, 2, G), f32,
                                    minval=float(np.log(DT_MIN)), maxval=float(np.log(DT_MAX)))
    ssm_b_re = nrm(ks[9], (DEPTH, 2, G, P, H), (0.5 / H) ** 0.5)
    ssm_b_im = nrm(ks[10], (DEPTH, 2, G, P, H), (0.5 / H) ** 0.5)
    ssm_c_re = nrm(ks[11], (DEPTH, 2, G, H, P), (0.5 / P) ** 0.5)
    ssm_c_im = nrm(ks[12], (DEPTH, 2, G, H, P), (0.5 / P) ** 0.5)
    ssm_d = nrm(ks[13], (DEPTH, D_SSM), 1.0)
    w_glu = nrm(ks[14], (DEPTH, D_SSM, 2 * D_SSM), D_SSM ** -0.5)
    b_glu = nrm(ks[15], (DEPTH, 2 * D_SSM), 0.01)
    w_out = nrm(ks[16], (DEPTH, D_MIX, D_MODEL), D_MIX ** -0.5)
    norm_ple = 1.0 + nrm(ks[17], (DEPTH, D_MODEL), 0.02)
    w_ple_gate = nrm(ks[18], (DEPTH, D_MODEL, D_MODEL), D_MODEL ** -0.5)
    w_ple_proj = nrm(ks[19], (DEPTH, PLE_DIM, D_MODEL), PLE_DIM ** -0.5)
    norm_final = 1.0 + nrm(ks[20], (D_MODEL,), 0.02)
    return {"x": x, "p": p, "norm_mix": norm_mix, "w_in": w_in,
            "q_norm": q_norm, "k_norm": k_norm,
            "ssm_a_re": ssm_a_re, "ssm_a_im": ssm_a_im, "ssm_log_dt": ssm_log_dt,
            "ssm_b_re": ssm_b_re, "ssm_b_im": ssm_b_im,
            "ssm_c_re": ssm_c_re, "ssm_c_im": ssm_c_im, "ssm_d": ssm_d,
            "w_glu": w_glu, "b_glu": b_glu, "w_out": w_out,
            "norm_ple": norm_ple, "w_ple_gate": w_ple_gate, "w_ple_proj": w_ple_proj,
            "norm_final": norm_final}


def reference(x, p, norm_mix, w_in, q_norm, k_norm, ssm_a_re, ssm_a_im, ssm_log_dt,
              ssm_b_re, ssm_b_im, ssm_c_re, ssm_c_im, ssm_d, w_glu, b_glu, w_out,
              norm_ple, w_ple_gate, w_ple_proj, norm_final):
    bsz, L, _ = x.shape
    ang_row, ang_col = grid_angles(L)
    split_idx = list(np.cumsum(IN_SPLITS)[:-1])
    h = x
    for i in range(DEPTH):
        hn = rms_norm(h, norm_mix[i])
        z = hn @ w_in[i]
        q, k, v, gate_a, u, gate_s = jnp.split(z, split_idx, axis=-1)

        q = q.reshape(bsz, L, N_HEADS, HEAD_DIM)
        k = k.reshape(bsz, L, N_KV_HEADS, HEAD_DIM)
        v = v.reshape(bsz, L, N_KV_HEADS, HEAD_DIM)
        q = axial_rope(rms_norm(q, q_norm[i]), ang_row, ang_col)
        k = axial_rope(rms_norm(k, k_norm[i]), ang_row, ang_col)
        y_attn = block_attention(q, k, v) * jax.nn.silu(gate_a)

        y_ssm = s5_bidirectional(u, ssm_a_re[i], ssm_a_im[i], ssm_log_dt[i],
                                 ssm_b_re[i], ssm_b_im[i], ssm_c_re[i], ssm_c_im[i],
                                 ssm_d[i])
        y_ssm = jax.nn.gelu(y_ssm).astype(x.dtype)
        glu = y_ssm @ w_glu[i] + b_glu[i]
        y_ssm = glu[..., :D_SSM] * jax.nn.sigmoid(glu[..., D_SSM:])
        y_ssm = y_ssm * jax.nn.silu(gate_s)

        h = h + jnp.concatenate([y_attn, y_ssm], axis=-1) @ w_out[i]

        gate = jax.nn.sigmoid(rms_norm(h, norm_ple[i]) @ w_ple_gate[i])
        h = h + gate * (p[i] @ w_ple_proj[i])
    return rms_norm(h, norm_final)
```

```python
import math
import os
STAGES = os.environ.get('KSTAGES', 'ABCD')
from contextlib import ExitStack
import numpy as np
import concourse.bass as bass
import concourse.mybir as mybir
from concourse.bass_utils import run_bass_kernel_spmd

F32 = mybir.dt.float32
BF16 = mybir.dt.bfloat16
ALU = mybir.AluOpType
AF = mybir.ActivationFunctionType

NDMA = 12
EPS = 1e-6
L = 4096
NCH = 512


class Prog:
    ENGS = ("pe", "act", "dve", "pool", "sp")

    def __init__(self, nc, stack):
        self.nc = nc
        self.sem = {e: stack.enter_context(nc.semaphore("s_" + e)) for e in self.ENGS}
        self.dsem = {e: [stack.enter_context(nc.semaphore("d_%s%d" % (e, i))) for i in range(NDMA)]
                     for e in ("sp", "pool")}
        self.cnt = {e: 0 for e in self.ENGS}
        self.dcnt = {e: [0] * NDMA for e in self.dsem}
        self.dnext = {e: 0 for e in self.dsem}
        self.waited = {e: {} for e in self.ENGS}
        self._reset()

    def _reset(self):
        self.instrs = []
        self.last_writer = {}
        self.readers = {}
        self.last_idx = {}
        self.dmas = []

    def op(self, eng, fn, reads=(), writes=(), dma=False, extra=()):
        idx = len(self.instrs)
        deps = set(extra)
        for k in reads:
            if k in self.last_writer:
                deps.add(self.last_writer[k])
        for k in writes:
            if k in self.last_writer:
                deps.add(self.last_writer[k])
            for r in self.readers.get(k, ()):
                deps.add(r)
        self.instrs.append(dict(eng=eng, fn=fn, deps=deps, dma=dma))
        for k in reads:
            lst = self.readers.setdefault(k, [])
            if not dma:
                lst[:] = [r for r in lst if self.instrs[r]["dma"] or self.instrs[r]["eng"] != eng]
            lst.append(idx)
        for k in writes:
            self.last_writer[k] = idx
            self.readers[k] = []
        self.last_idx[eng] = idx
        if dma:
            self.dmas.append(idx)
        return idx

    def dma(self, eng, out, in_, reads=(), writes=()):
        return self.op(eng, lambda e: e.dma_start(out=out, in_=in_), reads, writes, dma=True)

    def barrier(self):
        deps = set(self.last_idx.values()) | set(self.dmas)
        for e in self.ENGS:
            self.op(e, lambda en: en.nop(), extra=deps)
        self.last_writer = {}
        self.readers = {}
        self.dmas = []

    def emit(self):
        nc = self.nc
        ins = self.instrs
        needed = set()
        for i, it in enumerate(ins):
            for d in it["deps"]:
                if ins[d]["eng"] == "pe" and it["eng"] == "pe" and not ins[d]["dma"]:
                    continue
                needed.add(d)
        for i, it in enumerate(ins):
            e = it["eng"]
            if it["dma"]:
                s = self.dnext[e]
                self.dnext[e] = (s + 1) % NDMA
                self.dcnt[e][s] += 16
                it["tok"] = (self.dsem[e][s], self.dcnt[e][s], "d_%s%d" % (e, s))
                it["inc"] = 16
            elif i in needed:
                self.cnt[e] += 1
                it["tok"] = (self.sem[e], self.cnt[e], "s_" + e)
                it["inc"] = 1
            else:
                it["tok"] = None
        per = {e: [] for e in self.ENGS}
        for i, it in enumerate(ins):
            per[it["eng"]].append(i)

        def replay(ename, eng):
            w = self.waited[ename]
            for i in per[ename]:
                it = ins[i]
                waits = {}
                for d in it["deps"]:
                    t = ins[d]["tok"]
                    if t is None:
                        continue
                    if w.get(t[2], 0) < t[1] and waits.get(t[2], (None, 0))[1] < t[1]:
                        waits[t[2]] = (t[0], t[1])
                if it["dma"]:
                    t = it["tok"]
                    prev = t[1] - 16
                    if prev > 0 and w.get(t[2], 0) < prev and waits.get(t[2], (None, 0))[1] < prev:
                        waits[t[2]] = (t[0], prev)
                for name, (s, v) in waits.items():
                    eng.wait_ge(s, v)
                    w[name] = v
                bi = it["fn"](eng)
                if it["tok"] is not None:
                    bi.then_inc(it["tok"][0], it["inc"])

        with nc.Block() as block:
            @block.tensor
            def _(e):
                replay("pe", e)

            @block.scalar
            def _(e):
                replay("act", e)

            @block.vector
            def _(e):
                replay("dve", e)

            @block.gpsimd
            def _(e):
                replay("pool", e)

            @block.sync
            def _(e):
                replay("sp", e)
        self._reset()


def _din(nc, name, shape):
    return nc.dram_tensor(name, list(shape), F32, kind="ExternalInput").ap()


def build_phase1():
    nc = bass.Bass("TRN2", target_bir_lowering=False)
    xT = _din(nc, "xT", [16, 128, 16, 256])
    w1 = _din(nc, "w1", [8, 128, 16, 128])
    nmix_d = _din(nc, "nmix", [128, 16])
    qkn_d = _din(nc, "qkn", [128, 2])
    cs_d = _din(nc, "csT", [16, 128, 2, 256])
    c128_d = _din(nc, "c128", [128, 4, 128])
    sp_d = _din(nc, "ssmp", [128, 6, 16])
    bc_d = _din(nc, "ssmbc", [128, 4, 16, 16])
    EX = nc.dram_tensor("EX", [512, L], F32, kind="ExternalOutput").ap()

    with ExitStack() as st0:
        P = Prog(nc, st0)
        sb0 = lambda name, shape, dt=F32: st0.enter_context(nc.sbuf_tensor(name, list(shape), dt))
        ps = [st0.enter_context(nc.psum_tensor("ps%d" % i, [128, 512], F32)) for i in range(8)]
        UT = sb0("UT", [128, 2, L], BF16)
        c128 = sb0("c128s", [128, 4, 128])
        Rm = sb0("Rm", [128, 128], BF16)
        identb = sb0("identb", [128, 128], BF16)
        ones = sb0("ones", [128, 128], BF16)
        qkn = sb0("qkns", [128, 2])
        P.dma("sp", c128[:], c128_d, writes=["c128"])
        P.dma("sp", qkn[:], qkn_d, writes=["qkn"])
        P.op("dve", lambda e: e.tensor_copy(Rm[:], c128[:, 0, :]), reads=["c128"], writes=["Rm"])
        P.op("dve", lambda e: e.tensor_copy(identb[:], c128[:, 1, :]), reads=["c128"], writes=["identb"])
        P.op("dve", lambda e: e.memset(ones[:], 1.0), writes=["ones"])

        with ExitStack() as st1:
            sb1 = lambda name, shape, dt=F32: st1.enter_context(nc.sbuf_tensor(name, list(shape), dt))
            QT = sb1("QT", [128, 2, L], BF16)
            KT = sb1("KT", [128, L], BF16)
            Vt = sb1("Vt", [128, 32, 128], BF16)
            SG = sb1("SG", [128, 2, L], BF16)
            with ExitStack() as st:
                sb = lambda name, shape, dt=F32: st.enter_context(nc.sbuf_tensor(name, list(shape), dt))
                W1b = sb("W1b", [128, 16, 1024], BF16)
                wst = [sb("wst%d" % i, [128, 16, 128]) for i in range(1)]
                xblk = [sb("xblk%d" % i, [128, 16, 256]) for i in range(2)]
                sq = sb("sq", [128, 16, 256], BF16)
                xn = [sb("xn%d" % i, [128, 16, 512], BF16) for i in range(2)]
                nmix = sb("nmixs", [128, 16])
                rt = [sb("rt%d" % i, [128, 256]) for i in range(2)]
                rstd = [sb("rstd%d" % i, [128, 256]) for i in range(2)]
                qw = [sb("qw%d" % i, [128, 512], BF16) for i in range(3)]
                qsq = [sb("qsq%d" % i, [128, 512], BF16) for i in range(3)]
                rq = [sb("rq%d" % i, [128, 512]) for i in range(3)]
                t1 = [sb("t1%d" % i, [128, 512]) for i in range(3)]
                t2 = [sb("t2s", [128, 512])] * 3
                csb = [sb("csbs", [128, 2, 512])] * 2
                vT = sb("vT", [128, 512], BF16)
                P.dma("sp", nmix[:], nmix_d, writes=["nmix"])
                for pc in range(8):
                    P.dma("sp", wst[0][:], w1[pc], writes=[("wst", 0)])
                    for kc in range(16):
                        if kc % 2 == 0:
                            P.op("dve", lambda e, kc=kc, pc=pc: e.tensor_scalar(
                                W1b[:, kc, pc * 128:(pc + 1) * 128], wst[0][:, kc, :], nmix[:, kc:kc + 1], None, ALU.mult),
                                reads=[("wst", 0), "nmix"], writes=[("W1b", pc, kc)])
                        else:
                            P.op("act", lambda e, kc=kc, pc=pc: e.activation(
                                out=W1b[:, kc, pc * 128:(pc + 1) * 128], in_=wst[0][:, kc, :], func=AF.Copy, scale=nmix[:, kc:kc + 1]),
                                reads=[("wst", 0), "nmix"], writes=[("W1b", pc, kc)])
                ctiles = [(0, "q", 0), (128, "q", 1), (256, "k", 2), (384, "v", 0), (512, "g", 0), (640, "g", 1), (768, "u", 0), (896, "u", 1)]
                sbank = [2, 6]

                def prep(tb):
                    s = tb % 2
                    pr = (tb // 2) % 2
                    hf = tb % 2
                    tok = slice(tb * 256, (tb + 1) * 256)
                    P.dma("sp", xblk[s][:, 0:8, :], xT[tb, :, 0:8, :], writes=[("xblk", s, 0)])
                    P.dma("sp", xblk[s][:, 8:16, :], xT[tb, :, 8:16, :], writes=[("xblk", s, 1)])
                    P.op("act", lambda e: e.activation(out=sq[:], in_=xblk[s][:], func=AF.Square),
                         reads=[("xblk", s, 0), ("xblk", s, 1)], writes=["sq"])
                    bk = ps[sbank[s]]
                    for kc in range(16):
                        P.op("pe", lambda e, kc=kc: e.matmul(bk[:, 0:256], lhsT=ones[:], rhs=sq[:, kc, :], start=(kc == 0), stop=(kc == 15)),
                             reads=["sq", "ones"], writes=[("ps", sbank[s])])
                    P.op("act", lambda e: e.activation(out=rt[s][:], in_=bk[:, 0:256], func=AF.Sqrt, scale=1.0 / 2048, bias=EPS),
                         reads=[("ps", sbank[s])], writes=[("rt", s)])
                    P.op("dve", lambda e: e.reciprocal(rstd[s][:], rt[s][:]), reads=[("rt", s)], writes=[("rstd", s)])
                    P.op("dve", lambda e: e.tensor_tensor(xn[pr][:, :, hf * 256:(hf + 1) * 256], xblk[s][:], rstd[s][:].unsqueeze(1).to_broadcast([128, 16, 256]), ALU.mult),
                         reads=[("xblk", s, 0), ("xblk", s, 1), ("rstd", s)], writes=[("xn", pr, hf)])

                def qk_post(pp, kind, idx, j):
                    pr = pp % 2
                    tok = slice(pp * 512, (pp + 1) * 512)
                    b1, b2 = [(3, 4), (7, 3), (4, 7)][j]
                    bq, br = ps[b1], ps[b2]
                    P.op("pe", lambda e: e.matmul(bq[:, :], lhsT=ones[:], rhs=qsq[j][:], start=True, stop=True),
                         reads=[("qsq", j), "ones"], writes=[("ps", b1)])
                    P.op("pe", lambda e: e.matmul(br[:, :], lhsT=Rm[:], rhs=qw[j][:], start=True, stop=True),
                         reads=[("qw", j), "Rm"], writes=[("ps", b2)])
                    P.op("act", lambda e: e.activation(out=rq[j][:], in_=bq[:, :], func=AF.Sqrt, scale=1.0 / 128, bias=EPS),
                         reads=[("ps", b1)], writes=[("rq", j)])
                    P.op("dve", lambda e: e.tensor_tensor(t1[j][:], qw[j][:], csb[pr][:, 0, :], ALU.mult),
                         reads=[("qw", j)] + [("csb", 0, 0, hf) for hf in range(2)], writes=[("t1", j)])
                    P.op("dve", lambda e: e.tensor_tensor(t2[j][:], br[:, :], csb[pr][:, 1, :], ALU.mult),
                         reads=[("ps", b2)] + [("csb", 0, 1, hf) for hf in range(2)], writes=[("t2", 0)])
                    P.op("dve", lambda e: e.tensor_tensor(t1[j][:], t1[j][:], t2[j][:], ALU.add), reads=[("t1", j), ("t2", 0)], writes=[("t1", j)])
                    dst = QT[:, idx, tok] if kind == "q" else KT[:, tok]
                    P.op("dve", lambda e: e.reciprocal(rq[j][:], rq[j][:]), reads=[("rq", j)], writes=[("rq", j)])
                    P.op("dve", lambda e: e.tensor_tensor(dst, t1[j][:], rq[j][:], ALU.mult),
                         reads=[("t1", j), ("rq", j)], writes=[("QK", kind, idx, pp)])

                def main(pp):
                    pr = pp % 2
                    tok = slice(pp * 512, (pp + 1) * 512)
                    pending = None
                    for hf in range(2):
                        P.dma("sp", csb[0][:, :, hf * 256:(hf + 1) * 256], cs_d[2 * pp + hf], writes=[("csb", 0, 0, hf), ("csb", 0, 1, hf)])
                    for ci, (c0, kind, idx) in enumerate(ctiles):
                        pb = ps[ci % 2]
                        pk = ("ps", ci % 2)
                        for kc in range(16):
                            P.op("pe", lambda e, pb=pb, kc=kc, c0=c0: e.matmul(pb[:, :], lhsT=W1b[:, kc, c0:c0 + 128], rhs=xn[pr][:, kc, :],
                                                                         start=(kc == 0), stop=(kc == 15)),
                                 reads=[("xn", pr, 0), ("xn", pr, 1)] + ([("W1b", c0 // 128, kc)] if pp == 0 else []), writes=[pk])
                        if pending is not None:
                            qk_post(*pending)
                            pending = None
                        if kind in ("q", "k"):
                            j = idx
                            col = 0 if kind == "q" else 1
                            P.op("act", lambda e, pb=pb, j=j, col=col: e.activation(out=qw[j][:], in_=pb[:, :], func=AF.Copy, scale=qkn[:, col:col + 1]),
                                 reads=[pk, "qkn"], writes=[("qw", j)])
                            P.op("act", lambda e, pb=pb, j=j: e.activation(out=qsq[j][:], in_=pb[:, :], func=AF.Square),
                                 reads=[pk], writes=[("qsq", j)])
                            pending = (pp, kind, idx, j)
                        elif kind == "v":
                            P.op("act", lambda e, pb=pb: e.activation(out=vT[:], in_=pb[:, :], func=AF.Copy), reads=[pk], writes=["vT"])
                            for q4 in range(4):
                                P.op("pe", lambda e, q4=q4: e.matmul(ps[5][:, q4 * 128:(q4 + 1) * 128], lhsT=vT[:, q4 * 128:(q4 + 1) * 128], rhs=identb[:], start=True, stop=True),
                                     reads=["vT", "identb"], writes=[("ps", 5)])
                            P.op("dve", lambda e: e.tensor_copy(Vt[:, pp * 4:pp * 4 + 4, :], ps[5][:, :].rearrange("p (a b) -> p a b", a=4)),
                                 reads=[("ps", 5)], writes=[("Vt", pp)])
                        elif kind == "g":
                            P.op("act", lambda e, pb=pb, idx=idx: e.activation(out=SG[:, idx, tok], in_=pb[:, :], func=AF.Silu),
                                 reads=[pk], writes=[("SG", idx, pp)])
                        else:
                            dstv = UT[:, idx, :].rearrange("p (s c) -> p s c", s=8)[:, :, pp * 64:(pp + 1) * 64]
                            srcv = pb[:, :].rearrange("p (c s) -> p s c", s=8)
                            P.op("dve", lambda e, dstv=dstv, srcv=srcv: e.tensor_copy(dstv, srcv), reads=[pk], writes=[("UT", idx, pp)])

                prep(0)
                prep(1)
                for pp in range(8):
                    if pp + 1 < 8:
                        prep(2 * pp + 2)
                        prep(2 * pp + 3)
                    main(pp)
                P.barrier()
                P.emit()

            stB = st1
            sbB = lambda name, shape, dt=F32: stB.enter_context(nc.sbuf_tensor(name, list(shape), dt))
            pT = [sbB("pT%d" % i, [128, 512], BF16) for i in range(4)]
            rden = sbB("rden", [128, 512])
            ot = sbB("ot", [128, 512])
            YA = [sbB("YA%d" % i, [128, 8, 64]) for i in range(2)]
            acnt = [0]

            ait = [0]
            scale_att = 128 ** -0.5

            def a_qk(i):
                blk, kt = divmod(i, 32)
                h, qb = blk // 8, blk % 8
                sp_ = ps[i % 2]
                P.op("pe", lambda e: e.matmul(sp_[:, :], lhsT=KT[:, kt * 128:(kt + 1) * 128], rhs=QT[:, h, qb * 512:(qb + 1) * 512], start=True, stop=True),
                     reads=[], writes=[("ps", i % 2)])
                P.op("act", lambda e: e.activation(out=pT[i % 4][:], in_=sp_[:, :], func=AF.Exp, scale=scale_att),
                     reads=[("ps", i % 2)], writes=[("pT", i % 4)])

            def a_pv(i):
                blk, kt = divmod(i, 32)
                P.op("pe", lambda e: e.matmul(ps[4][:, :], lhsT=Vt[:, kt, :], rhs=pT[i % 4][:], start=(kt == 0), stop=(kt == 31)),
                     reads=[("pT", i % 4)], writes=[("ps", 4)])
                P.op("pe", lambda e: e.matmul(ps[5][:, :], lhsT=ones[:], rhs=pT[i % 4][:], start=(kt == 0), stop=(kt == 31)),
                     reads=[("pT", i % 4)], writes=[("ps", 5)])

            def attn_norm(blk):
                h, qb = blk // 8, blk % 8
                bo, bd = 4, 5
                qs = slice(qb * 512, (qb + 1) * 512)
                P.op("act", lambda e: e.activation(out=rden[:], in_=ps[bd][:, :], func=AF.Ln), reads=[("ps", bd)], writes=["lnd"])
                P.op("act", lambda e: e.activation(out=rden[:], in_=rden[:], func=AF.Exp, scale=-1.0), reads=["lnd"], writes=["rden"])
                P.op("dve", lambda e: e.tensor_tensor(ot[:], ps[bo][:, :], rden[:], ALU.mult), reads=[("ps", bo), "rden"], writes=["ot"])
                ya = YA[blk % 2]
                P.op("dve", lambda e: e.tensor_tensor(ya[:], ot[:].rearrange("p (c s) -> p s c", s=8),
                                                     SG[:, h, qs].rearrange("p (c s) -> p s c", s=8), ALU.mult),
                     reads=["ot"], writes=[("YA", blk % 2)])
                P.dma("sp", EX[h * 128:(h + 1) * 128, :].rearrange("p (s c) -> p s c", s=8)[:, :, qb * 64:(qb + 1) * 64], ya[:], reads=[("YA", blk % 2)])

            def attn_step(n=8):
                if 'B' not in STAGES:
                    return
                for _ in range(n):
                    i = ait[0]
                    if i >= 512:
                        return
                    ait[0] += 1
                    blk, kt = divmod(i, 32)
                    if i == 0:
                        a_qk(0)
                    if i + 1 < 512:
                        a_qk(i + 1)
                    if kt == 0 and blk >= 1:
                        attn_norm(blk - 1)
                    a_pv(i)

            def attn_flush():
                if 'B' not in STAGES:
                    return
                attn_step(512)
                attn_norm(15)

            with ExitStack() as st:
              if 'C' in STAGES:
                  sb = lambda name, shape, dt=F32: st.enter_context(nc.sbuf_tensor(name, list(shape), dt))
                  spm = sb("spm", [128, 6, 16])
                  P.dma("sp", spm[:], sp_d, writes=["spm"])
                  rho = sb("rho", [128, 16]); ur = sb("ur", [128, 16]); ui = sb("ui", [128, 16])
                  WSr = sb("WSr", [128, 16, 128], BF16); WSi = sb("WSi", [128, 16, 128], BF16); M0 = sb("M0", [128, 16, 128], BF16)
                  E4 = [128, 16, 8, 16]
                  Cfr = sb("Cfr", E4, BF16); Cfi = sb("Cfi", E4, BF16); Cbr = sb("Cbr", E4, BF16); Cbi = sb("Cbi", E4, BF16)
                  fl = lambda a, g: a[:, g, :, :].rearrange("p s h -> p (s h)")

                  def dv(fn, reads, writes, eng="dve"):
                      P.op(eng, fn, reads=reads, writes=writes)

                  def tt(out, a, b, op, reads, writes):
                      dv(lambda e: e.tensor_tensor(out, a, b, op), reads, writes)

                  def cmul(or_, oi_, ar, ai, br, bi, reads, okr, oki, ta, tb, tk):
                      ka, kb = tk + "a", tk + "b"
                      tt(ta, ar, br, ALU.mult, reads, [ka])
                      tt(tb, ai, bi, ALU.mult, reads, [kb])
                      tt(or_, ta, tb, ALU.subtract, [ka, kb], [okr])
                      tt(ta, ar, bi, ALU.mult, reads + [okr], [ka])
                      tt(tb, ai, br, ALU.mult, reads + [okr], [kb])
                      tt(oi_, ta, tb, ALU.add, [ka, kb], [oki])

                  with ExitStack() as stg:
                      cnt = [0]

                      def tmp(shape, dt=F32):
                          cnt[0] += 1
                          return stg.enter_context(nc.sbuf_tensor("tmp%d" % cnt[0], list(shape), dt))

                      bcm = tmp([128, 4, 16, 16])
                      P.dma("sp", bcm[:], bc_d, writes=["bcm"])
                      are, aim, ldt, dtile = spm[:, 0, :], spm[:, 1, :], spm[:, 2, :], spm[:, 3, :]
                      mf, mb = spm[:, 4, 0:1], spm[:, 5, 0:1]
                      S = [128, 16]
                      lre = tmp(S); dt_ = tmp(S); al = tmp(S); th = tmp(S); mag = tmp(S)
                      sn = tmp(S); cs = tmp(S); a1 = tmp(S); a2 = tmp(S); a3 = tmp(S)
                      dv(lambda e: e.tensor_scalar(lre[:], are, -1e-4, None, ALU.min), ["spm"], ["lre"])
                      dv(lambda e: e.activation(out=dt_[:], in_=ldt, func=AF.Exp), ["spm"], ["dt"], "act")
                      tt(al[:], lre[:], dt_[:], ALU.mult, ["lre", "dt"], ["al"])
                      tt(th[:], aim, dt_[:], ALU.mult, ["spm", "dt"], ["th"])
                      dv(lambda e: e.activation(out=mag[:], in_=al[:], func=AF.Exp), ["al"], ["mag"], "act")
                      halfpi = tmp([128, 1])
                      dv(lambda e: e.memset(halfpi[:], math.pi / 2), [], ["halfpi"])
                      dv(lambda e: e.activation(out=sn[:], in_=th[:], func=AF.Sin, scale=1.0 / 32), ["th"], ["sn"], "act")
                      dv(lambda e: e.activation(out=cs[:], in_=th[:], func=AF.Sin, scale=1.0 / 32, bias=halfpi[:]), ["th", "halfpi"], ["cs"], "act")
                      for it in range(5):
                          tt(a1[:], sn[:], cs[:], ALU.mult, ["sn", "cs"], ["a1"])
                          tt(a2[:], cs[:], cs[:], ALU.mult, ["cs"], ["a2"])
                          tt(a3[:], sn[:], sn[:], ALU.mult, ["sn"], ["a3"])
                          tt(cs[:], a2[:], a3[:], ALU.subtract, ["a2", "a3"], ["cs"])
                          dv(lambda e: e.tensor_scalar(sn[:], a1[:], 2.0, None, ALU.mult), ["a1"], ["sn"])
                      attn_step()
                      lbr = tmp(S); lbi = tmp(S)
                      tt(lbr[:], mag[:], cs[:], ALU.mult, ["mag", "cs"], ["lbr"])
                      tt(lbi[:], mag[:], sn[:], ALU.mult, ["mag", "sn"], ["lbi"])
                      nr = tmp(S); den = tmp(S); bfr = tmp(S); bfi = tmp(S)
                      dv(lambda e: e.tensor_scalar(nr[:], lbr[:], -1.0, None, ALU.add), ["lbr"], ["nr"])
                      tt(a1[:], lre[:], lre[:], ALU.mult, ["lre"], ["a1"])
                      tt(a2[:], aim, aim, ALU.mult, ["spm"], ["a2"])
                      tt(den[:], a1[:], a2[:], ALU.add, ["a1", "a2"], ["den"])
                      dv(lambda e: e.reciprocal(den[:], den[:]), ["den"], ["den"])
                      tt(a1[:], nr[:], lre[:], ALU.mult, ["nr", "lre"], ["a1"])
                      tt(a2[:], lbi[:], aim, ALU.mult, ["lbi", "spm"], ["a2"])
                      tt(a3[:], a1[:], a2[:], ALU.add, ["a1", "a2"], ["a3"])
                      tt(bfr[:], a3[:], den[:], ALU.mult, ["a3", "den"], ["bfr"])
                      tt(a1[:], lbi[:], lre[:], ALU.mult, ["lbi", "lre"], ["a1"])
                      tt(a2[:], nr[:], aim, ALU.mult, ["nr", "spm"], ["a2"])
                      tt(a3[:], a1[:], a2[:], ALU.subtract, ["a1", "a2"], ["a3"])
                      tt(bfi[:], a3[:], den[:], ALU.mult, ["a3", "den"], ["bfi"])
                      Pwr = tmp([128, 16, 9]); Pwi = tmp([128, 16, 9])
                      dv(lambda e: e.memset(Pwr[:, :, 0], 1.0), [], ["Pw"])
                      dv(lambda e: e.memset(Pwi[:, :, 0], 0.0), ["Pw"], ["Pw"])
                      for k in range(1, 9):
                          cmul(Pwr[:, :, k], Pwi[:, :, k], Pwr[:, :, k - 1], Pwi[:, :, k - 1], lbr[:], lbi[:], ["lbr", "lbi", "Pw"], "Pw", "Pw", a1[:], a2[:], "a12")
                      rinv = tmp(S); i8r = tmp(S); i8i = tmp(S)
                      dv(lambda e: e.activation(out=rho[:], in_=al[:], func=AF.Exp, scale=8.0), ["al"], ["rho"], "act")
                      dv(lambda e: e.reciprocal(rinv[:], rho[:]), ["rho"], ["rinv"])
                      tt(ur[:], Pwr[:, :, 8], rinv[:], ALU.mult, ["Pw", "rinv"], ["ur"])
                      tt(ui[:], Pwi[:, :, 8], rinv[:], ALU.mult, ["Pw", "rinv"], ["ui"])
                      tt(i8r[:], ur[:], rinv[:], ALU.mult, ["ur", "rinv"], ["i8r"])
                      tt(a3[:], ui[:], rinv[:], ALU.mult, ["ui", "rinv"], ["a3"])
                      dv(lambda e: e.tensor_scalar(i8i[:], a3[:], -1.0, None, ALU.mult), ["a3"], ["i8i"])
                      PAr = tmp([128, 16, 8]); PAi = tmp([128, 16, 8]); PCr = tmp([128, 16, 8]); PCi = tmp([128, 16, 8])
                      for (dst, src) in ((PAr, Pwr), (PAi, Pwi)):
                          for k in range(8):
                              dv(lambda e, dst=dst, src=src, k=k: e.tensor_copy(dst[0:64, :, k:k + 1], src[0:64, :, 7 - k:8 - k]), ["Pw", "PA"], ["PA"])
                          dv(lambda e, dst=dst, src=src: e.tensor_copy(dst[64:128, :, :], src[64:128, :, 0:8]), ["Pw", "PA"], ["PA"])
                      for (dst, src) in ((PCr, Pwr), (PCi, Pwi)):
                          dv(lambda e, dst=dst, src=src: e.tensor_copy(dst[0:64, :, :], src[0:64, :, 1:9]), ["Pw", "PC"], ["PC"])
                          for k in range(8):
                              dv(lambda e, dst=dst, src=src, k=k: e.tensor_copy(dst[64:128, :, k:k + 1], src[64:128, :, 8 - k:9 - k]), ["Pw", "PC"], ["PC"])
                      PA2r = tmp([128, 16, 8]); PA2i = tmp([128, 16, 8]); t8a = tmp([128, 16, 8]); t8b = tmp([128, 16, 8])
                      bc8 = lambda a: a.unsqueeze(2).to_broadcast([128, 16, 8])
                      cmul(PA2r[:], PA2i[:], PAr[:], PAi[:], bc8(i8r[:]), bc8(i8i[:]), ["PA", "i8r", "i8i"], "PA2r", "PA2i", t8a[:], t8b[:], "t8")
                      Bbr = tmp([128, 16, 16]); Bbi = tmp([128, 16, 16]); t16a = tmp([128, 16, 16]); t16b = tmp([128, 16, 16])
                      bc16 = lambda a: a.unsqueeze(2).to_broadcast([128, 16, 16])
                      cmul(Bbr[:], Bbi[:], bcm[:, 0, :, :], bcm[:, 1, :, :], bc16(bfr[:]), bc16(bfi[:]), ["bcm", "bfr", "bfi"], "Bbr", "Bbi", t16a[:], t16b[:], "t16")
                      exa = tmp(E4); exb = tmp(E4); Er_ = tmp(E4); Ei_ = tmp(E4)

                      def cexp(pr, pi, mr, mi, reads):
                          for s_ in range(8):
                              b3 = lambda a, s_=s_: a[:, :, s_:s_ + 1].to_broadcast([128, 16, 16])
                              cmul(Er_[:, :, s_, :], Ei_[:, :, s_, :], b3(pr), b3(pi), mr, mi, reads + ["Er", "Ei"], "Er", "Ei", exa[:, :, s_, :], exb[:, :, s_, :], "ex")
                              if s_ % 2 == 1:
                                  attn_step()
                      Ar = tmp(E4, BF16); Ai = tmp(E4, BF16)
                      A2fr = tmp(E4, BF16); A2fi = tmp(E4, BF16); A2br = tmp(E4, BF16); A2bi = tmp(E4, BF16)
                      Ccr = tmp(E4, BF16); nCci = tmp(E4, BF16)
                      attn_step()
                      cexp(PAr, PAi, Bbr[:], Bbi[:], ["PA", "Bbr", "Bbi"])
                      attn_step()
                      dv(lambda e: e.tensor_copy(Ar[:], Er_[:]), ["Er"], ["Ar"])
                      dv(lambda e: e.tensor_copy(Ai[:], Ei_[:]), ["Ei"], ["Ai"])
                      cexp(PA2r, PA2i, Bbr[:], Bbi[:], ["PA2r", "PA2i", "Bbr", "Bbi", "Ar", "Ai"])
                      attn_step()
                      dv(lambda e: e.tensor_scalar(A2fr[:], Er_[:], mf, None, ALU.mult), ["Er", "spm"], ["A2fr"])
                      dv(lambda e: e.tensor_scalar(A2br[:], Er_[:], mb, None, ALU.mult), ["Er", "spm"], ["A2br"])
                      dv(lambda e: e.tensor_scalar(A2fi[:], Ei_[:], mf, None, ALU.mult), ["Ei", "spm"], ["A2fi"])
                      dv(lambda e: e.tensor_scalar(A2bi[:], Ei_[:], mb, None, ALU.mult), ["Ei", "spm"], ["A2bi"])
                      cexp(PCr, PCi, bcm[:, 2, :, :], bcm[:, 3, :, :], ["PC", "bcm", "A2fr", "A2br", "A2fi", "A2bi"])
                      attn_step()
                      dv(lambda e: e.tensor_copy(Ccr[:], Er_[:]), ["Er"], ["Ccr"])
                      dv(lambda e: e.tensor_scalar(nCci[:], Ei_[:], -1.0, None, ALU.mult), ["Ei"], ["nCci"])
                      dv(lambda e: e.tensor_scalar(Cfr[:], Er_[:], mf, None, ALU.mult), ["Er", "spm"], ["Cfr"])
                      dv(lambda e: e.tensor_scalar(Cbr[:], Er_[:], mb, None, ALU.mult), ["Er", "spm"], ["Cbr"])
                      dv(lambda e: e.tensor_scalar(Cfi[:], nCci[:], mf, None, ALU.mult), ["nCci", "spm"], ["Cfi"])
                      dv(lambda e: e.tensor_scalar(Cbi[:], nCci[:], mb, None, ALU.mult), ["nCci", "spm"], ["Cbi"])
                      mt1 = tmp([128, 128]); mt2 = tmp([128, 128])
                      for g in range(16):
                          attn_step()
                          P.op("pe", lambda e, g=g: e.matmul(ps[2][:, 0:128], lhsT=fl(Ar, g), rhs=identb[:], start=True, stop=True), reads=["Ar", "identb"], writes=[("ps", 2)])
                          P.op("pe", lambda e, g=g: e.matmul(ps[3][:, 0:128], lhsT=fl(Ai, g), rhs=identb[:], start=True, stop=True), reads=["Ai", "identb"], writes=[("ps", 3)])
                          P.op("act", lambda e, g=g: e.activation(out=WSr[:, g, :], in_=ps[2][:, 0:128], func=AF.Copy), reads=[("ps", 2)], writes=[("WSr", g)])
                          P.op("act", lambda e, g=g: e.activation(out=WSi[:, g, :], in_=ps[3][:, 0:128], func=AF.Copy), reads=[("ps", 3)], writes=[("WSi", g)])
                          P.op("pe", lambda e, g=g: e.matmul(ps[2][:, 0:128], lhsT=fl(A2fr, g), rhs=fl(Ccr, g), start=True, stop=False), reads=["A2fr", "Ccr"], writes=[("ps", 2)])
                          P.op("pe", lambda e, g=g: e.matmul(ps[2][:, 0:128], lhsT=fl(A2fi, g), rhs=fl(nCci, g), start=False, stop=True), reads=["A2fi", "nCci"], writes=[("ps", 2)])
                          P.op("pe", lambda e, g=g: e.matmul(ps[3][:, 0:128], lhsT=fl(A2br, g), rhs=fl(Ccr, g), start=True, stop=False), reads=["A2br", "Ccr"], writes=[("ps", 3)])
                          P.op("pe", lambda e, g=g: e.matmul(ps[3][:, 0:128], lhsT=fl(A2bi, g), rhs=fl(nCci, g), start=False, stop=True), reads=["A2bi", "nCci"], writes=[("ps", 3)])
                          tt(mt1[:], ps[2][:, 0:128], c128[:, 2, :], ALU.mult, [("ps", 2), "c128"], ["mt1"])
                          tt(mt2[:], ps[3][:, 0:128], c128[:, 3, :], ALU.mult, [("ps", 3), "c128"], ["mt2"])
                          tt(mt1[:], mt1[:], mt2[:], ALU.add, ["mt1", "mt2"], ["mt1"])
                          dv(lambda e, g=g: e.scalar_tensor_tensor(M0[:, g, :], c128[:, 1, :], dtile[:, g:g + 1], mt1[:], ALU.mult, ALU.add),
                             ["mt1", "c128", "spm"], [("M0", g)])
                      P.barrier()
                      P.emit()
                  if 'D' in STAGES:
                    U = sb("U", [128, 16, NCH], BF16)
                    for j in range(2):
                        for g8 in range(8):
                            for s in range(8):
                                P.dma("sp" if (g8 + s) % 2 == 0 else "pool", U[s * 16:(s + 1) * 16, j * 8 + g8, :],
                                      UT[g8 * 16:(g8 + 1) * 16, j, s * NCH:(s + 1) * NCH], writes=[("U", j * 8 + g8, s)])
                    Ukeys = lambda g: [("U", g, s) for s in range(8)]
                    Xr = sb("Xr", [128, 8, NCH + 2], BF16)
                    Xi = sb("Xi", [128, 8, NCH + 2], BF16)
                    YG = [sb("YG%d" % i, [128, NCH]) for i in range(2)]
                    Tr = sb("Tr", [128, 8, NCH]); Ti = sb("Ti", [128, 8, NCH])
                    Sr = sb("Sr", [128, 8, NCH], BF16); Si = sb("Si", [128, 8, NCH], BF16)
                    ta = sb("ta", [128, 8, 256]); tb_ = sb("tbb", [128, 8, 256])
                    ta2 = ta[:, 0:2, :].rearrange("p a b -> p (a b)"); tb2 = tb_[:, 0:2, :].rearrange("p a b -> p (a b)")
                    sgnv = sb("sgnv", [128, 1]); u2r = sb("u2r", [128, 8]); u2i = sb("u2i", [128, 8]); u3r = sb("u3r", [128, 8]); u3i = sb("u3i", [128, 8])
                    for h in range(2):
                        G8 = range(h * 8, h * 8 + 8)
                        gs = slice(h * 8, h * 8 + 8)
                        dv(lambda e: e.memset(Xr[:], 0.0), [("Xr", gl) for gl in range(8)], [("Xr", gl) for gl in range(8)])
                        dv(lambda e: e.memset(Xi[:], 0.0), [("Xi", gl) for gl in range(8)], [("Xi", gl) for gl in range(8)])
                        dv(lambda e: e.memset(Tr[:, :, 0:1], 1.0), ["T"], ["T"])
                        dv(lambda e: e.memset(Ti[:, :, 0:1], 0.0), ["T"], ["T"])
                        if h == 0:
                            tt(sgnv[:], spm[:, 4, 0:1], spm[:, 5, 0:1], ALU.subtract, ["spm"], ["sgnv"])
                        dv(lambda e, gs=gs: e.tensor_copy(u2r[:], ur[:, gs]), ["ur", "u2"], ["u2"])
                        dv(lambda e, gs=gs: e.tensor_scalar(u2i[:], ui[:, gs], sgnv[:, 0:1], None, ALU.mult), ["ui", "u2", "sgnv"], ["u2"])
                        m = 1
                        while m < NCH:
                            bq = lambda a, m=m: a[:, :].unsqueeze(2).to_broadcast([128, 8, m])
                            cmul(Tr[:, :, m:2 * m], Ti[:, :, m:2 * m], Tr[:, :, 0:m], Ti[:, :, 0:m], bq(u2r), bq(u2i), ["T", "u2"], "T", "T",
                                 ta[:, :, 0:m], tb_[:, :, 0:m], "tab")
                            cmul(u3r[:], u3i[:], u2r[:], u2i[:], u2r[:], u2i[:], ["u2", "T"], "u3", "u3", ta[:, :, 0], tb_[:, :, 0], "tab")
                            dv(lambda e: e.tensor_copy(u2r[:], u3r[:]), ["u3", "T"], ["u2"])
                            dv(lambda e: e.tensor_copy(u2i[:], u3i[:]), ["u3", "T", "u2"], ["u2"])
                            attn_step()
                            m *= 2
                        pa2 = ta[:, 2:4, :].rearrange("p a b -> p (a b)"); pb2 = tb_[:, 2:4, :].rearrange("p a b -> p (a b)")

                        def tp(out, a, b, op, reads, writes):
                            P.op("pool", lambda e: e.tensor_tensor(out, a, b, op), reads=reads, writes=writes)

                        def s_mm(g):
                            b_re, b_im = (2, 3) if g % 2 == 0 else (6, 7)
                            P.op("pe", lambda e: e.matmul(ps[b_re][:, :], lhsT=WSr[:, g, :], rhs=U[:, g, :], start=True, stop=True),
                                 reads=Ukeys(g), writes=[("ps", b_re)])
                            P.op("pe", lambda e: e.matmul(ps[b_im][:, :], lhsT=WSi[:, g, :], rhs=U[:, g, :], start=True, stop=True),
                                 reads=Ukeys(g), writes=[("ps", b_im)])

                        if h == 0:
                            s_mm(0)
                        for g in G8:
                            gl = g - h * 8
                            b_re, b_im = (2, 3) if g % 2 == 0 else (6, 7)
                            tt(ta2, Tr[:, gl, :], ps[b_re][:, :], ALU.mult, ["T", ("ps", b_re)], ["ta2"])
                            tt(tb2, Ti[:, gl, :], ps[b_im][:, :], ALU.mult, ["T", ("ps", b_im)], ["tb2"])
                            tt(Sr[:, gl, :], ta2, tb2, ALU.add, ["ta2", "tb2"], [("Sr", gl)])
                            tt(ta2, Tr[:, gl, :], ps[b_im][:, :], ALU.mult, ["T", ("ps", b_im)], ["ta2"])
                            tt(tb2, Ti[:, gl, :], ps[b_re][:, :], ALU.mult, ["T", ("ps", b_re)], ["tb2"])
                            tt(Si[:, gl, :], ta2, tb2, ALU.subtract, ["ta2", "tb2"], [("Si", gl)])
                            for (Sx, nm_) in ((Sr, "Sr"), (Si, "Si")):
                                dv(lambda e, Sx=Sx, g=g, gl=gl: e.tensor_tensor_scan(Sx[0:64, gl, :], rho[0:64, g:g + 1].to_broadcast([64, NCH]), Sx[0:64, gl, :], 0.0, ALU.mult, ALU.add),
                                   [(nm_, gl), "rho"], [(nm_, gl)])
                                dv(lambda e, Sx=Sx, g=g, gl=gl: e.tensor_tensor_scan(Sx[64:128, gl, ::-1], rho[64:128, g:g + 1].to_broadcast([64, NCH]), Sx[64:128, gl, ::-1], 0.0, ALU.mult, ALU.add),
                                   [(nm_, gl), "rho"], [(nm_, gl)])
                            tp(pa2, Tr[:, gl, :], Sr[:, gl, :], ALU.mult, ["T", ("Sr", gl)], ["pa2"])
                            tp(pb2, Ti[:, gl, :], Si[:, gl, :], ALU.mult, ["T", ("Si", gl)], ["pb2"])
                            tp(Xr[:, gl, 1:NCH + 1], pa2, pb2, ALU.subtract, ["pa2", "pb2"], [("Xr", gl)])
                            tp(pa2, Tr[:, gl, :], Si[:, gl, :], ALU.mult, ["T", ("Si", gl)], ["pa2"])
                            tp(pb2, Ti[:, gl, :], Sr[:, gl, :], ALU.mult, ["T", ("Sr", gl)], ["pb2"])
                            tp(Xi[:, gl, 1:NCH + 1], pa2, pb2, ALU.add, ["pa2", "pb2"], [("Xi", gl)])
                            if g + 1 < 16:
                                s_mm(g + 1)
                            attn_step()
                            pk = ("ps", b_re)
                            mm = [(M0[:, g, :], U[:, g, :], Ukeys(g)),
                                  (fl(Cfr, g), Xr[:, gl, 0:NCH], [("Xr", gl)]),
                                  (fl(Cfi, g), Xi[:, gl, 0:NCH], [("Xi", gl)]),
                                  (fl(Cbr, g), Xr[:, gl, 2:NCH + 2], [("Xr", gl)]),
                                  (fl(Cbi, g), Xi[:, gl, 2:NCH + 2], [("Xi", gl)])]
                            for i, (lt, rh, rk) in enumerate(mm):
                                P.op("pe", lambda e, lt=lt, rh=rh, i=i, b_re=b_re: e.matmul(ps[b_re][:, :], lhsT=lt, rhs=rh, start=(i == 0), stop=(i == 4)), reads=rk, writes=[pk])
                            yg = YG[g % 2]
                            P.op("act", lambda e, yg=yg, b_re=b_re: e.activation(out=yg[:], in_=ps[b_re][:, :], func=AF.Gelu_apprx_tanh), reads=[pk], writes=[("YG", g % 2)])
                            for t in range(8):
                                P.dma("sp" if t % 2 == 0 else "pool", EX[256 + g * 16:256 + (g + 1) * 16, t * NCH:(t + 1) * NCH], yg[t * 16:(t + 1) * 16, :], reads=[("YG", g % 2)])
                        if h == 1:
                            attn_flush()
                        if h == 0:
                            dv(lambda e: e.nop(), ["pa2", "pb2", "ta2", "tb2"], ["taba", "tabb"])
                    P.barrier()
                    P.emit()
    return nc


def build_phase2():
    nc = bass.Bass("TRN2", target_bir_lowering=False)
    mixin = _din(nc, "mixin", [128, 16, 1024])
    x2 = _din(nc, "x2", [2, 128, 16, 512])
    pTd = _din(nc, "pT", [128, 2, 1024])
    wgs = _din(nc, "wgs", [8, 128, 16, 128])
    wglu = _din(nc, "wglu", [16, 128, 8, 128])
    wout = _din(nc, "wout", [16, 128, 16, 128])
    wpg = _din(nc, "wpg", [16, 128, 16, 128])
    wpp = _din(nc, "wpp", [16, 128, 2, 128])
    vec_d = _din(nc, "vecs", [128, 4, 16])
    outT = nc.dram_tensor("outT", [128, 16, 1024], F32, kind="ExternalOutput").ap()
    with ExitStack() as st:
        P = Prog(nc, st)
        phase2_body(nc, P, st, dict(mixin=mixin, x2=x2, pTd=pTd, wgs=wgs, wglu=wglu, wout=wout, wpg=wpg, wpp=wpp, vec_d=vec_d, outT=outT), None, None)
    return nc


def phase2_body(nc, P, st, D, mixA, ysA):
    if True:
        sb = lambda name, shape, dt=F32: st.enter_context(nc.sbuf_tensor(name, list(shape), dt))
        ps = [st.enter_context(nc.psum_tensor("qs%d" % i, [128, 512], F32)) for i in range(8)]
        vecs = sb("vecss", [128, 4, 16])
        ones = sb("ones2", [128, 128], BF16)
        P.dma("sp", vecs[:], D["vec_d"], writes=["vecs"])
        P.op("dve", lambda e: e.memset(ones[:], 1.0), writes=["ones2"])
        NW = 4
        xn = sb("xn2", [128, 16, 1024], BF16)
        mix = sb("mix2", [128, 16, 1024], BF16)
        pTb = sb("pTb", [128, 2, 1024], BF16)
        rt = sb("rt2", [128, 512])
        rstd = sb("rstd2", [128, 1024])
        ga = sb("ga2", [128, 512]); sgb = sb("sgb2", [128, 512]); gsl = sb("gsl2", [128, 512])
        ot = [sb("ot2_0", [128, 512])] * 2
        xres = [sb("xres0", [128, 1024])] * 2
        wcount = [0]
        casters = ["act", "dve"]
        WS = {}

        def load_w(src_tile, K):
            wst, wb = WS["wst"], WS["wb"]
            s = wcount[0] % len(wb)
            s2 = wcount[0] % len(wst)
            eng = casters[wcount[0] % len(casters)]
            wcount[0] += 1
            P.dma("sp", wst[s2][:, 0:K, :], src_tile, writes=[("wst", s2)])
            if eng == "act":
                P.op("act", lambda e: e.activation(out=wb[s][:, 0:K, :], in_=wst[s2][:, 0:K, :], func=AF.Copy), reads=[("wst", s2)], writes=[("wb", s)])
            else:
                P.op(eng, lambda e: e.tensor_copy(wb[s][:, 0:K, :], wst[s2][:, 0:K, :]), reads=[("wst", s2)], writes=[("wb", s)])
            return wb[s], ("wb", s)

        def rms_bcast(src_blk, key, sqt, blk):
            for hh in range(2):
                P.op("act", lambda e, hh=hh: e.activation(out=sqt[:], in_=src_blk[:, hh * 8:(hh + 1) * 8, :], func=AF.Square), reads=[key], writes=["sq"])
                for kc in range(8):
                    P.op("pe", lambda e, kc=kc, hh=hh: e.matmul(ps[7][:, :], lhsT=ones[:], rhs=sqt[:, kc, :], start=(hh == 0 and kc == 0), stop=(hh == 1 and kc == 7)),
                         reads=["sq", "ones2"], writes=[("ps", 7)])
            P.op("act", lambda e: e.activation(out=rt[:], in_=ps[7][:, :], func=AF.Sqrt, scale=1.0 / 2048, bias=EPS), reads=[("ps", 7)], writes=["rt"])
            P.op("dve", lambda e: e.reciprocal(rstd[:, blk * 512:(blk + 1) * 512], rt[:]), reads=["rt"], writes=[("rstd", blk)])

        def norm_to_bf16(dst_blk, src_blk, srckey, row, blk, dkey):
            for k in range(16):
                P.op("dve", lambda e, k=k: e.scalar_tensor_tensor(dst_blk[:, k, :], src_blk[:, k, :], vecs[:, row, k:k + 1], rstd[:, blk * 512:(blk + 1) * 512], ALU.mult, ALU.mult)
                     if True else None, reads=[srckey, ("rstd", blk), "vecs"], writes=[(dkey, blk, k)])

        sG = ExitStack()
        ys = sG.enter_context(nc.sbuf_tensor("ys2", [128, 8, 1024], BF16)) if ysA is None else ysA
        with ExitStack() as s1:
            sb1 = lambda name, shape, dt=F32: s1.enter_context(nc.sbuf_tensor(name, list(shape), dt))
            xb = sb1("xb2", [128, 16, 512])
            sq = sb1("sq2a", [128, 8, 512], BF16)
            mixf = sb1("mixf", [128, 8, 512])
            pTf = sb1("pTfs", [128, 2, 1024])
            P.dma("sp", pTf[:], D["pTd"], writes=["pTf"])
            P.op("dve", lambda e: e.tensor_copy(pTb[:], pTf[:]), reads=["pTf"], writes=["pTb"])
            WS["wst"] = [sb1("wstA%d" % i, [128, 8, 128]) for i in range(4)]
            WS["wb"] = [sb1("wbA%d" % i, [128, 8, 128], BF16) for i in range(4)]
            for blk in range(2):
                tk = slice(blk * 512, (blk + 1) * 512)
                P.dma("sp", mixf[:], D["mixin"][:, 8:16, tk], writes=["mixf"])
                P.op("act", lambda e, tk=tk: e.activation(out=ys[:, :, tk], in_=mixf[:], func=AF.Copy), reads=["mixf"], writes=[("ys", blk)])
            for blk in range(2):
                tk = slice(blk * 512, (blk + 1) * 512)
                P.dma("sp", mixf[:], D["mixin"][:, 0:8, tk], writes=["mixf"])
                P.op("dve", lambda e, tk=tk: e.tensor_copy(mix[:, 0:8, tk], mixf[:]), reads=["mixf"], writes=[("mixa", blk)])

            def glu_ab(m):
                wa, ka = load_w(D["wglu"][m], 8)
                wbb, kb = load_w(D["wglu"][8 + m], 8)
                for blk in range(2):
                    tk = slice(blk * 512, (blk + 1) * 512)
                    for k in range(8):
                        P.op("pe", lambda e, k=k, tk=tk, wa=wa: e.matmul(ps[0][:, :], lhsT=wa[:, k, :], rhs=ys[:, k, tk], start=(k == 0), stop=(k == 7)), reads=[ka, ("ys", blk)], writes=[("ps", 0)])
                    for k in range(8):
                        P.op("pe", lambda e, k=k, tk=tk, wbb=wbb: e.matmul(ps[1][:, :], lhsT=wbb[:, k, :], rhs=ys[:, k, tk], start=(k == 0), stop=(k == 7)), reads=[kb, ("ys", blk)], writes=[("ps", 1)])
                    P.op("act", lambda e: e.activation(out=ga[:], in_=ps[0][:, :], func=AF.Identity, bias=vecs[:, 3, m:m + 1]), reads=[("ps", 0), "vecs"], writes=["ga"])
                    P.op("act", lambda e: e.activation(out=sgb[:], in_=ps[1][:, :], func=AF.Sigmoid, bias=vecs[:, 3, 8 + m:9 + m]), reads=[("ps", 1), "vecs"], writes=["sgb"])
                    P.op("dve", lambda e, tk=tk: e.tensor_tensor(mix[:, 8 + m, tk], ga[:], sgb[:], ALU.mult), reads=["ga", "sgb"], writes=[("mixs", m, blk)])

            for blk in range(2):
                tk = slice(blk * 512, (blk + 1) * 512)
                P.dma("sp", xb[:], D["x2"][blk], writes=["xb"])
                for m in range(blk * 4, blk * 4 + 4):
                    glu_ab(m)
                rms_bcast(xb[:], "xb", sq, blk)
                for k in range(16):
                    P.op("dve", lambda e, k=k, tk=tk, blk=blk: e.scalar_tensor_tensor(xn[:, k, tk], xb[:, k, :], vecs[:, 0, k:k + 1], rstd[:, tk], ALU.mult, ALU.mult),
                         reads=["xb", ("rstd", blk), "vecs"], writes=[("xn", blk, k)])
            P.barrier()
            P.emit()
        wcount[0] = 0
        WS["wst"] = [sG.enter_context(nc.sbuf_tensor("wstG%d" % i, [128, 16, 128], F32)) for i in range(6)]
        WS["wb"] = [sG.enter_context(nc.sbuf_tensor("wbG%d" % i, [128, 16, 128], BF16)) for i in range(6)]
        for m in range(8):
            wg, kg = load_w(D["wgs"][m], 16)
            for blk in range(2):
                tk = slice(blk * 512, (blk + 1) * 512)
                pb = ps[2 + (2 * m + blk) % 2]
                pk = ("ps", 2 + (2 * m + blk) % 2)
                for k in range(16):
                    P.op("pe", lambda e, k=k, tk=tk, wg=wg, pb=pb: e.matmul(pb[:, :], lhsT=wg[:, k, :], rhs=xn[:, k, tk], start=(k == 0), stop=(k == 15)), reads=[kg], writes=[pk])
                P.op("act", lambda e, pb=pb: e.activation(out=gsl[:], in_=pb[:, :], func=AF.Silu), reads=[pk], writes=["gsl"])
                P.op("dve", lambda e, m=m, tk=tk: e.tensor_tensor(mix[:, 8 + m, tk], mix[:, 8 + m, tk], gsl[:], ALU.mult), reads=["gsl"], writes=[("mixs", m, blk)])
        P.barrier()
        P.emit()
        sG.close()
        wcount[0] = 0
        H = sb("H2", [128, 16, 1024])
        sq = sb("sq2b", [128, 8, 512], BF16)
        WS["wst"] = [sb("wstO%d" % i, [128, 16, 128]) for i in range(4)]
        WS["wb"] = [sb("wbO%d" % i, [128, 16, 128], BF16) for i in range(4)]
        for m in range(16):
            wo, ko = load_w(D["wout"][m], 16)
            xr = xres[m % 2]
            P.dma("sp", xr[:].rearrange("p (b t) -> p b t", b=2), D["x2"][:, :, m, :].rearrange("b p t -> p b t"), writes=[("xres", 0)])
            for blk in range(2):
                tk = slice(blk * 512, (blk + 1) * 512)
                pb = ps[3 + (2 * m + blk) % 3]
                pk = ("ps", 3 + (2 * m + blk) % 3)
                for k in range(16):
                    P.op("pe", lambda e, k=k, tk=tk, pb=pb, wo=wo: e.matmul(pb[:, :], lhsT=wo[:, k, :], rhs=mix[:, k, tk], start=(k == 0), stop=(k == 15)),
                         reads=[ko] + ([("mixs", mm, blk) for mm in range(8)] if m == 0 else []), writes=[pk])
                P.op("dve", lambda e, m=m, tk=tk, pb=pb, xr=xr: e.tensor_tensor(H[:, m, tk], pb[:, :], xr[:, tk], ALU.add), reads=[pk, ("xres", 0)], writes=[("H", m, blk)])
        for blk in range(2):
            tk = slice(blk * 512, (blk + 1) * 512)
            P.op("dve", lambda e: e.nop(), reads=[("H", m, blk) for m in range(16)], writes=[("Hall", blk)])
            rms_bcast(H[:, :, tk], ("Hall", blk), sq, blk)
            for k in range(16):
                P.op("dve", lambda e, k=k, tk=tk: e.scalar_tensor_tensor(xn[:, k, tk], H[:, k, tk], vecs[:, 1, k:k + 1], rstd[:, tk], ALU.mult, ALU.mult),
                     reads=[("Hall", blk), ("rstd", blk), "vecs"], writes=[("hn", blk, k)])
        for m in range(16):
            wq, kq = load_w(D["wpg"][m], 16)
            wp_, kp = load_w(D["wpp"][m], 2)
            for blk in range(2):
                tk = slice(blk * 512, (blk + 1) * 512)
                pb = ps[(2 * m + blk) % 2]
                pk = ("ps", (2 * m + blk) % 2)
                for k in range(16):
                    P.op("pe", lambda e, k=k, tk=tk, pb=pb, wq=wq: e.matmul(pb[:, :], lhsT=wq[:, k, :], rhs=xn[:, k, tk], start=(k == 0), stop=(k == 15)),
                         reads=[kq] + ([("hn", blk, kk) for kk in range(16)] if m == 0 else []), writes=[pk])
                pb2 = ps[2 + (2 * m + blk) % 2]
                pk2 = ("ps", 2 + (2 * m + blk) % 2)
                for k in range(2):
                    P.op("pe", lambda e, k=k, tk=tk, pb2=pb2, wp_=wp_: e.matmul(pb2[:, :], lhsT=wp_[:, k, :], rhs=pTb[:, k, tk], start=(k == 0), stop=(k == 1)), reads=[kp, "pTb"], writes=[pk2])
                P.op("act", lambda e, pb=pb: e.activation(out=ga[:], in_=pb[:, :], func=AF.Sigmoid), reads=[pk], writes=["ga"])
                P.op("dve", lambda e, pb2=pb2: e.tensor_tensor(sgb[:], ga[:], pb2[:, :], ALU.mult), reads=["ga", pk2], writes=["sgb"])
                P.op("dve", lambda e, m=m, tk=tk: e.tensor_tensor(H[:, m, tk], H[:, m, tk], sgb[:], ALU.add),
                     reads=["sgb"] + [("hn", blk, kk) for kk in range(16)], writes=[("H2", m, blk)])
        for blk in range(2):
            tk = slice(blk * 512, (blk + 1) * 512)
            P.op("dve", lambda e: e.nop(), reads=[("H2", m, blk) for m in range(16)], writes=[("H2all", blk)])
            rms_bcast(H[:, :, tk], ("H2all", blk), sq, blk)
            for m in range(16):
                o_ = ot[m % 2]
                P.op("dve", lambda e, m=m, tk=tk, o_=o_: e.scalar_tensor_tensor(o_[:], H[:, m, tk], vecs[:, 2, m:m + 1], rstd[:, tk], ALU.mult, ALU.mult),
                     reads=[("H2all", blk), ("rstd", blk), "vecs"], writes=[("ot", 0)])
                P.dma("sp", D["outT"][:, m, tk], o_[:], reads=[("ot", 0)], writes=[("out", m, blk)])
        P.barrier()
        P.emit()


_CACHE = {}


def _perm(r):
    return np.array([8 * (128 * r + cl) + t for t in range(8) for cl in range(128)], dtype=np.int64)


def _fm(a):
    F_, T_ = a.shape
    return np.ascontiguousarray(a.reshape(F_ // 128, 128, T_).transpose(1, 0, 2))


def kernel(x, p, norm_mix, w_in, q_norm, k_norm, ssm_a_re, ssm_a_im, ssm_log_dt,
           ssm_b_re, ssm_b_im, ssm_c_re, ssm_c_im, ssm_d, w_glu, b_glu, w_out,
           norm_ple, w_ple_gate, w_ple_proj, norm_final):
    f32 = np.float32
    x = np.asarray(x, f32); p = np.asarray(p, f32)
    if "p1" not in _CACHE:
        _CACHE["p1"] = build_phase1()
        _CACHE["p2"] = build_phase2()
    inv_freq = (np.float32(10000.0) ** (-np.arange(32, dtype=f32) / np.float32(32))).astype(f32)
    t = np.arange(L)
    rows = (t // 64).astype(f32); cols = (t % 64).astype(f32)
    cosT = np.zeros((128, L), f32); sinT = np.zeros((128, L), f32)
    for d in range(128):
        pos = rows if d < 64 else cols
        ang = (pos * inv_freq[d % 32]).astype(f32)
        cosT[d] = np.cos(ang); sinT[d] = np.sin(ang)
    csT = np.ascontiguousarray(np.stack([cosT, sinT], 1).reshape(128, 2, 16, 256).transpose(2, 0, 1, 3))
    Rm = np.zeros((128, 128), f32)
    for d in range(128):
        if d % 64 < 32:
            Rm[d + 32, d] = -1.0
        else:
            Rm[d - 32, d] = 1.0
    ident = np.eye(128, dtype=f32)
    si = np.arange(128) // 16
    maskF = (si[None, :] >= si[:, None]).astype(f32)
    maskB = (si[:, None] >= si[None, :]).astype(f32)
    c128 = np.ascontiguousarray(np.stack([Rm, ident, maskF, maskB], 1))
    nm = lambda v: np.ascontiguousarray(np.asarray(v, f32).reshape(-1, 128).T)
    w_in0 = np.asarray(w_in[0], f32)
    in1, in2 = [], []
    for c in range(8):
        b, r = c // 4, c % 4
        kv = r // 2
        G0 = 16 * r
        colsel = np.concatenate([np.arange(256 * r, 256 * r + 256), np.arange(1024 + 128 * kv, 1024 + 128 * kv + 128),
                                 np.arange(1280 + 128 * kv, 1280 + 128 * kv + 128), np.arange(1536 + 256 * r, 1536 + 256 * r + 256),
                                 np.arange(2560 + 256 * r, 2560 + 256 * r + 256)])
        w1 = np.ascontiguousarray(_fm(w_in0[:, colsel]).reshape(128, 16, 8, 128).transpose(2, 0, 1, 3))
        if r == 0:
            xTb = np.ascontiguousarray(_fm(np.ascontiguousarray(x[b].T)).reshape(128, 16, 16, 256).transpose(2, 0, 1, 3))
        qkn = np.ascontiguousarray(np.stack([q_norm[0], k_norm[0]], 1).astype(f32))
        dp = lambda a: np.ascontiguousarray(np.asarray(a[0], f32)[:, G0:G0 + 16, :].transpose(0, 2, 1).reshape(128, 16))
        are = dp(ssm_a_re); aim = dp(ssm_a_im)
        ldt = np.ascontiguousarray(np.repeat(np.asarray(ssm_log_dt[0], f32)[:, None, G0:G0 + 16], 64, 1).reshape(128, 16))
        dt_ = np.asarray(ssm_d[0], f32)[G0 * 16:(G0 + 16) * 16].reshape(16, 16)
        dtile = np.ascontiguousarray(np.tile(dt_.T, (8, 1)))
        mfv = np.zeros((128, 16), f32); mfv[:64] = 1.0
        mbv = np.zeros((128, 16), f32); mbv[64:] = 1.0
        ssmp = np.ascontiguousarray(np.stack([are, aim, ldt, dtile, mfv, mbv], 1))
        Bl = lambda a: np.asarray(a[0], f32)[:, G0:G0 + 16].transpose(0, 2, 1, 3).reshape(128, 16, 16)
        Cl = lambda a: np.asarray(a[0], f32)[:, G0:G0 + 16].transpose(0, 3, 1, 2).reshape(128, 16, 16)
        ssmbc = np.ascontiguousarray(np.stack([Bl(ssm_b_re), Bl(ssm_b_im), Cl(ssm_c_re), Cl(ssm_c_im)], 1))
        in1.append({"xT": xTb, "w1": w1, "nmix": nm(norm_mix[0]), "qkn": qkn, "csT": csT, "c128": c128,
                    "ssmp": ssmp, "ssmbc": ssmbc})
    res1 = run_bass_kernel_spmd(_CACHE["p1"], in1, core_ids=list(range(8)))
    EXs = [np.asarray(r["EX"], f32) for r in res1.results]
    if os.environ.get("KP1ONLY"):
        return EXs
    vecs = np.ascontiguousarray(np.stack([nm(norm_mix[0]), nm(norm_ple[0]), nm(norm_final), nm(b_glu[0])], 1))
    tl = lambda w: np.ascontiguousarray(_fm(w).reshape(128, w.shape[0] // 128, w.shape[1] // 128, 128).transpose(2, 0, 1, 3))
    wgs = tl(w_in0[:, 3584:4608]); wglu = tl(np.asarray(w_glu[0], f32)); wout = tl(np.asarray(w_out[0], f32))
    wpg = tl(np.asarray(w_ple_gate[0], f32)); wpp = tl(np.asarray(w_ple_proj[0], f32))
    for c in range(8):
        b, r = c // 4, c % 4
        pr = _perm(r)
        sel = np.concatenate([np.arange(tt * NCH + 128 * r, tt * NCH + 128 * r + 128) for tt in range(8)])
        attn = np.concatenate([EXs[b * 4 + q][0:256][:, sel] for q in range(4)], 0)
        ssm = np.concatenate([EXs[b * 4 + q][256:512][:, sel] for q in range(4)], 0)
        mixin = _fm(np.concatenate([attn, ssm], 0))
        x2 = np.ascontiguousarray(_fm(np.ascontiguousarray(x[b].T[:, pr])).reshape(128, 16, 2, 512).transpose(2, 0, 1, 3))
        pT = _fm(np.ascontiguousarray(p[0, b].T[:, pr]))
        in2.append({"mixin": mixin, "x2": x2, "pT": pT, "wgs": wgs, "wglu": wglu, "wout": wout, "wpg": wpg, "wpp": wpp, "vecs": vecs})
    res2 = run_bass_kernel_spmd(_CACHE["p2"], in2, core_ids=list(range(8)))
    out = np.zeros((2, L, 2048), f32)
    for c in range(8):
        b, r = c // 4, c % 4
        o = np.asarray(res2.results[c]["outT"], f32)
        o = o.transpose(1, 0, 2).reshape(2048, 1024)
        out[b, _perm(r), :] = o.T
    return out
```

```python
import math
import os
STAGES = os.environ.get('KSTAGES', 'ABCD')
from contextlib import ExitStack
import numpy as np
import concourse.bass as bass
import concourse.mybir as mybir
from concourse.bass_utils import run_bass_kernel_spmd

F32 = mybir.dt.float32
BF16 = mybir.dt.bfloat16
ALU = mybir.AluOpType
AF = mybir.ActivationFunctionType

NDMA = 12
EPS = 1e-6
L = 4096
NCH = 512


class Prog:
    ENGS = ("pe", "act", "dve", "pool", "sp")

    def __init__(self, nc, stack):
        self.nc = nc
        self.sem = {e: stack.enter_context(nc.semaphore("s_" + e)) for e in self.ENGS}
        self.dsem = {e: [stack.enter_context(nc.semaphore("d_%s%d" % (e, i))) for i in range(NDMA)]
                     for e in ("sp", "pool")}
        self.cnt = {e: 0 for e in self.ENGS}
        self.dcnt = {e: [0] * NDMA for e in self.dsem}
        self.dnext = {e: 0 for e in self.dsem}
        self.waited = {e: {} for e in self.ENGS}
        self._reset()

    def _reset(self):
        self.instrs = []
        self.last_writer = {}
        self.readers = {}
        self.last_idx = {}
        self.dmas = []

    def op(self, eng, fn, reads=(), writes=(), dma=False, extra=()):
        idx = len(self.instrs)
        deps = set(extra)
        for k in reads:
            if k in self.last_writer:
                deps.add(self.last_writer[k])
        for k in writes:
            if k in self.last_writer:
                deps.add(self.last_writer[k])
            for r in self.readers.get(k, ()):
                deps.add(r)
        self.instrs.append(dict(eng=eng, fn=fn, deps=deps, dma=dma))
        for k in reads:
            lst = self.readers.setdefault(k, [])
            if not dma:
                lst[:] = [r for r in lst if self.instrs[r]["dma"] or self.instrs[r]["eng"] != eng]
            lst.append(idx)
        for k in writes:
            self.last_writer[k] = idx
            self.readers[k] = []
        self.last_idx[eng] = idx
        if dma:
            self.dmas.append(idx)
        return idx

    def dma(self, eng, out, in_, reads=(), writes=()):
        return self.op(eng, lambda e: e.dma_start(out=out, in_=in_), reads, writes, dma=True)

    def barrier(self):
        deps = set(self.last_idx.values()) | set(self.dmas)
        for e in self.ENGS:
            self.op(e, lambda en: en.nop(), extra=deps)
        self.last_writer = {}
        self.readers = {}
        self.dmas = []

    def emit(self):
        nc = self.nc
        ins = self.instrs
        needed = set()
        for i, it in enumerate(ins):
            for d in it["deps"]:
                if ins[d]["eng"] == "pe" and it["eng"] == "pe" and not ins[d]["dma"]:
                    continue
                needed.add(d)
        for i, it in enumerate(ins):
            e = it["eng"]
            if it["dma"]:
                s = self.dnext[e]
                self.dnext[e] = (s + 1) % NDMA
                self.dcnt[e][s] += 16
                it["tok"] = (self.dsem[e][s], self.dcnt[e][s], "d_%s%d" % (e, s))
                it["inc"] = 16
            elif i in needed:
                self.cnt[e] += 1
                it["tok"] = (self.sem[e], self.cnt[e], "s_" + e)
                it["inc"] = 1
            else:
                it["tok"] = None
        per = {e: [] for e in self.ENGS}
        for i, it in enumerate(ins):
            per[it["eng"]].append(i)

        def replay(ename, eng):
            w = self.waited[ename]
            for i in per[ename]:
                it = ins[i]
                waits = {}
                for d in it["deps"]:
                    t = ins[d]["tok"]
                    if t is None:
                        continue
                    if w.get(t[2], 0) < t[1] and waits.get(t[2], (None, 0))[1] < t[1]:
                        waits[t[2]] = (t[0], t[1])
                if it["dma"]:
                    t = it["tok"]
                    prev = t[1] - 16
                    if prev > 0 and w.get(t[2], 0) < prev and waits.get(t[2], (None, 0))[1] < prev:
                        waits[t[2]] = (t[0], prev)
                for name, (s, v) in waits.items():
                    eng.wait_ge(s, v)
                    w[name] = v
                bi = it["fn"](eng)
                if it["tok"] is not None:
                    bi.then_inc(it["tok"][0], it["inc"])

        with nc.Block() as block:
            @block.tensor
            def _(e):
                replay("pe", e)

            @block.scalar
            def _(e):
                replay("act", e)

            @block.vector
            def _(e):
                replay("dve", e)

            @block.gpsimd
            def _(e):
                replay("pool", e)

            @block.sync
            def _(e):
                replay("sp", e)
        self._reset()


def _din(nc, name, shape):
    return nc.dram_tensor(name, list(shape), F32, kind="ExternalInput").ap()


def build_phase1():
    nc = bass.Bass("TRN2", target_bir_lowering=False)
    xT = _din(nc, "xT", [16, 128, 16, 256])
    w1 = _din(nc, "w1", [8, 128, 16, 128])
    nmix_d = _din(nc, "nmix", [128, 16])
    qkn_d = _din(nc, "qkn", [128, 2])
    cs_d = _din(nc, "csT", [16, 128, 2, 256])
    c128_d = _din(nc, "c128", [128, 4, 128])
    sp_d = _din(nc, "ssmp", [128, 6, 16])
    bc_d = _din(nc, "ssmbc", [128, 4, 16, 16])
    EX = nc.dram_tensor("EX", [512, L], F32, kind="ExternalOutput").ap()

    with ExitStack() as st0:
        P = Prog(nc, st0)
        sb0 = lambda name, shape, dt=F32: st0.enter_context(nc.sbuf_tensor(name, list(shape), dt))
        ps = [st0.enter_context(nc.psum_tensor("ps%d" % i, [128, 512], F32)) for i in range(8)]
        UT = sb0("UT", [128, 2, L], BF16)
        c128 = sb0("c128s", [128, 4, 128])
        Rm = sb0("Rm", [128, 128], BF16)
        identb = sb0("identb", [128, 128], BF16)
        ones = sb0("ones", [128, 128], BF16)
        qkn = sb0("qkns", [128, 2])
        P.dma("sp", c128[:], c128_d, writes=["c128"])
        P.dma("sp", qkn[:], qkn_d, writes=["qkn"])
        P.op("dve", lambda e: e.tensor_copy(Rm[:], c128[:, 0, :]), reads=["c128"], writes=["Rm"])
        P.op("dve", lambda e: e.tensor_copy(identb[:], c128[:, 1, :]), reads=["c128"], writes=["identb"])
        P.op("dve", lambda e: e.memset(ones[:], 1.0), writes=["ones"])

        with ExitStack() as st1:
            sb1 = lambda name, shape, dt=F32: st1.enter_context(nc.sbuf_tensor(name, list(shape), dt))
            QT = sb1("QT", [128, 2, L], BF16)
            KT = sb1("KT", [128, L], BF16)
            Vt = sb1("Vt", [128, 32, 128], BF16)
            SG = sb1("SG", [128, 2, L], BF16)
            with ExitStack() as st:
                sb = lambda name, shape, dt=F32: st.enter_context(nc.sbuf_tensor(name, list(shape), dt))
                W1b = sb("W1b", [128, 16, 1024], BF16)
                wst = [sb("wst%d" % i, [128, 16, 128]) for i in range(1)]
                xblk = [sb("xblk%d" % i, [128, 16, 256]) for i in range(2)]
                sq = sb("sq", [128, 16, 256], BF16)
                xn = [sb("xn%d" % i, [128, 16, 512], BF16) for i in range(2)]
                nmix = sb("nmixs", [128, 16])
                rt = [sb("rt%d" % i, [128, 256]) for i in range(2)]
                rstd = [sb("rstd%d" % i, [128, 256]) for i in range(2)]
                qw = [sb("qw%d" % i, [128, 512], BF16) for i in range(3)]
                qsq = [sb("qsq%d" % i, [128, 512], BF16) for i in range(3)]
                rq = [sb("rq%d" % i, [128, 512]) for i in range(3)]
                t1 = [sb("t1%d" % i, [128, 512]) for i in range(3)]
                t2 = [sb("t2s", [128, 512])] * 3
                csb = [sb("csbs", [128, 2, 512])] * 2
                vT = sb("vT", [128, 512], BF16)
                P.dma("sp", nmix[:], nmix_d, writes=["nmix"])
                for pc in range(8):
                    P.dma("sp", wst[0][:], w1[pc], writes=[("wst", 0)])
                    for kc in range(16):
                        if kc % 2 == 0:
                            P.op("dve", lambda e, kc=kc, pc=pc: e.tensor_scalar(
                                W1b[:, kc, pc * 128:(pc + 1) * 128], wst[0][:, kc, :], nmix[:, kc:kc + 1], None, ALU.mult),
                                reads=[("wst", 0), "nmix"], writes=[("W1b", pc, kc)])
                        else:
                            P.op("act", lambda e, kc=kc, pc=pc: e.activation(
                                out=W1b[:, kc, pc * 128:(pc + 1) * 128], in_=wst[0][:, kc, :], func=AF.Copy, scale=nmix[:, kc:kc + 1]),
                                reads=[("wst", 0), "nmix"], writes=[("W1b", pc, kc)])
                ctiles = [(0, "q", 0), (128, "q", 1), (256, "k", 2), (384, "v", 0), (512, "g", 0), (640, "g", 1), (768, "u", 0), (896, "u", 1)]
                sbank = [2, 6]

                def prep(tb):
                    s = tb % 2
                    pr = (tb // 2) % 2
                    hf = tb % 2
                    tok = slice(tb * 256, (tb + 1) * 256)
                    P.dma("sp", xblk[s][:, 0:8, :], xT[tb, :, 0:8, :], writes=[("xblk", s, 0)])
                    P.dma("sp", xblk[s][:, 8:16, :], xT[tb, :, 8:16, :], writes=[("xblk", s, 1)])
                    P.op("act", lambda e: e.activation(out=sq[:], in_=xblk[s][:], func=AF.Square),
                         reads=[("xblk", s, 0), ("xblk", s, 1)], writes=["sq"])
                    bk = ps[sbank[s]]
                    for kc in range(16):
                        P.op("pe", lambda e, kc=kc: e.matmul(bk[:, 0:256], lhsT=ones[:], rhs=sq[:, kc, :], start=(kc == 0), stop=(kc == 15)),
                             reads=["sq", "ones"], writes=[("ps", sbank[s])])
                    P.op("act", lambda e: e.activation(out=rt[s][:], in_=bk[:, 0:256], func=AF.Sqrt, scale=1.0 / 2048, bias=EPS),
                         reads=[("ps", sbank[s])], writes=[("rt", s)])
                    P.op("dve", lambda e: e.reciprocal(rstd[s][:], rt[s][:]), reads=[("rt", s)], writes=[("rstd", s)])
                    P.op("dve", lambda e: e.tensor_tensor(xn[pr][:, :, hf * 256:(hf + 1) * 256], xblk[s][:], rstd[s][:].unsqueeze(1).to_broadcast([128, 16, 256]), ALU.mult),
                         reads=[("xblk", s, 0), ("xblk", s, 1), ("rstd", s)], writes=[("xn", pr, hf)])

                def qk_post(pp, kind, idx, j):
                    pr = pp % 2
                    tok = slice(pp * 512, (pp + 1) * 512)
                    b1, b2 = [(3, 4), (7, 3), (4, 7)][j]
                    bq, br = ps[b1], ps[b2]
                    P.op("pe", lambda e: e.matmul(bq[:, :], lhsT=ones[:], rhs=qsq[j][:], start=True, stop=True),
                         reads=[("qsq", j), "ones"], writes=[("ps", b1)])
                    P.op("pe", lambda e: e.matmul(br[:, :], lhsT=Rm[:], rhs=qw[j][:], start=True, stop=True),
                         reads=[("qw", j), "Rm"], writes=[("ps", b2)])
                    P.op("act", lambda e: e.activation(out=rq[j][:], in_=bq[:, :], func=AF.Sqrt, scale=1.0 / 128, bias=EPS),
                         reads=[("ps", b1)], writes=[("rq", j)])
                    P.op("dve", lambda e: e.tensor_tensor(t1[j][:], qw[j][:], csb[pr][:, 0, :], ALU.mult),
                         reads=[("qw", j)] + [("csb", 0, 0, hf) for hf in range(2)], writes=[("t1", j)])
                    P.op("dve", lambda e: e.tensor_tensor(t2[j][:], br[:, :], csb[pr][:, 1, :], ALU.mult),
                         reads=[("ps", b2)] + [("csb", 0, 1, hf) for hf in range(2)], writes=[("t2", 0)])
                    P.op("dve", lambda e: e.tensor_tensor(t1[j][:], t1[j][:], t2[j][:], ALU.add), reads=[("t1", j), ("t2", 0)], writes=[("t1", j)])
                    dst = QT[:, idx, tok] if kind == "q" else KT[:, tok]
                    P.op("dve", lambda e: e.reciprocal(rq[j][:], rq[j][:]), reads=[("rq", j)], writes=[("rq", j)])
                    P.op("dve", lambda e: e.tensor_tensor(dst, t1[j][:], rq[j][:], ALU.mult),
                         reads=[("t1", j), ("rq", j)], writes=[("QK", kind, idx, pp)])

                def main(pp):
                    pr = pp % 2
                    tok = slice(pp * 512, (pp + 1) * 512)
                    pending = None
                    for hf in range(2):
                        P.dma("sp", csb[0][:, :, hf * 256:(hf + 1) * 256], cs_d[2 * pp + hf], writes=[("csb", 0, 0, hf), ("csb", 0, 1, hf)])
                    for ci, (c0, kind, idx) in enumerate(ctiles):
                        pb = ps[ci % 2]
                        pk = ("ps", ci % 2)
                        for kc in range(16):
                            P.op("pe", lambda e, pb=pb, kc=kc, c0=c0: e.matmul(pb[:, :], lhsT=W1b[:, kc, c0:c0 + 128], rhs=xn[pr][:, kc, :],
                                                                         start=(kc == 0), stop=(kc == 15)),
                                 reads=[("xn", pr, 0), ("xn", pr, 1)] + ([("W1b", c0 // 128, kc)] if pp == 0 else []), writes=[pk])
                        if pending is not None:
                            qk_post(*pending)
                            pending = None
                        if kind in ("q", "k"):
                            j = idx
                            col = 0 if kind == "q" else 1
                            P.op("act", lambda e, pb=pb, j=j, col=col: e.activation(out=qw[j][:], in_=pb[:, :], func=AF.Copy, scale=qkn[:, col:col + 1]),
                                 reads=[pk, "qkn"], writes=[("qw", j)])
                            P.op("act", lambda e, pb=pb, j=j: e.activation(out=qsq[j][:], in_=pb[:, :], func=AF.Square),
                                 reads=[pk], writes=[("qsq", j)])
                            pending = (pp, kind, idx, j)
                        elif kind == "v":
                            P.op("act", lambda e, pb=pb: e.activation(out=vT[:], in_=pb[:, :], func=AF.Copy), reads=[pk], writes=["vT"])
                            for q4 in range(4):
                                P.op("pe", lambda e, q4=q4: e.matmul(ps[5][:, q4 * 128:(q4 + 1) * 128], lhsT=vT[:, q4 * 128:(q4 + 1) * 128], rhs=identb[:], start=True, stop=True),
                                     reads=["vT", "identb"], writes=[("ps", 5)])
                            P.op("dve", lambda e: e.tensor_copy(Vt[:, pp * 4:pp * 4 + 4, :], ps[5][:, :].rearrange("p (a b) -> p a b", a=4)),
                                 reads=[("ps", 5)], writes=[("Vt", pp)])
                        elif kind == "g":
                            P.op("act", lambda e, pb=pb, idx=idx: e.activation(out=SG[:, idx, tok], in_=pb[:, :], func=AF.Silu),
                                 reads=[pk], writes=[("SG", idx, pp)])
                        else:
                            dstv = UT[:, idx, :].rearrange("p (s c) -> p s c", s=8)[:, :, pp * 64:(pp + 1) * 64]
                            srcv = pb[:, :].rearrange("p (c s) -> p s c", s=8)
                            P.op("dve", lambda e, dstv=dstv, srcv=srcv: e.tensor_copy(dstv, srcv), reads=[pk], writes=[("UT", idx, pp)])

                prep(0)
                prep(1)
                for pp in range(8):
                    if pp + 1 < 8:
                        prep(2 * pp + 2)
                        prep(2 * pp + 3)
                    main(pp)
                P.barrier()
                P.emit()

            stB = st1
            sbB = lambda name, shape, dt=F32: stB.enter_context(nc.sbuf_tensor(name, list(shape), dt))
            pT = [sbB("pT%d" % i, [128, 512], BF16) for i in range(4)]
            rden = sbB("rden", [128, 512])
            ot = sbB("ot", [128, 512])
            YA = [sbB("YA%d" % i, [128, 8, 64]) for i in range(2)]
            acnt = [0]

            ait = [0]
            scale_att = 128 ** -0.5

            def a_qk(i):
                blk, kt = divmod(i, 32)
                h, qb = blk // 8, blk % 8
                sp_ = ps[i % 2]
                P.op("pe", lambda e: e.matmul(sp_[:, :], lhsT=KT[:, kt * 128:(kt + 1) * 128], rhs=QT[:, h, qb * 512:(qb + 1) * 512], start=True, stop=True),
                     reads=[], writes=[("ps", i % 2)])
                P.op("act", lambda e: e.activation(out=pT[i % 4][:], in_=sp_[:, :], func=AF.Exp, scale=scale_att),
                     reads=[("ps", i % 2)], writes=[("pT", i % 4)])

            def a_pv(i):
                blk, kt = divmod(i, 32)
                P.op("pe", lambda e: e.matmul(ps[4][:, :], lhsT=Vt[:, kt, :], rhs=pT[i % 4][:], start=(kt == 0), stop=(kt == 31)),
                     reads=[("pT", i % 4)], writes=[("ps", 4)])
                P.op("pe", lambda e: e.matmul(ps[5][:, :], lhsT=ones[:], rhs=pT[i % 4][:], start=(kt == 0), stop=(kt == 31)),
                     reads=[("pT", i % 4)], writes=[("ps", 5)])

            def attn_norm(blk):
                h, qb = blk // 8, blk % 8
                bo, bd = 4, 5
                qs = slice(qb * 512, (qb + 1) * 512)
                P.op("act", lambda e: e.activation(out=rden[:], in_=ps[bd][:, :], func=AF.Ln), reads=[("ps", bd)], writes=["lnd"])
                P.op("act", lambda e: e.activation(out=rden[:], in_=rden[:], func=AF.Exp, scale=-1.0), reads=["lnd"], writes=["rden"])
                P.op("dve", lambda e: e.tensor_tensor(ot[:], ps[bo][:, :], rden[:], ALU.mult), reads=[("ps", bo), "rden"], writes=["ot"])
                ya = YA[blk % 2]
                P.op("dve", lambda e: e.tensor_tensor(ya[:], ot[:].rearrange("p (c s) -> p s c", s=8),
                                                     SG[:, h, qs].rearrange("p (c s) -> p s c", s=8), ALU.mult),
                     reads=["ot"], writes=[("YA", blk % 2)])
                P.dma("sp", EX[h * 128:(h + 1) * 128, :].rearrange("p (s c) -> p s c", s=8)[:, :, qb * 64:(qb + 1) * 64], ya[:], reads=[("YA", blk % 2)])

            def attn_step(n=8):
                if 'B' not in STAGES:
                    return
                for _ in range(n):
                    i = ait[0]
                    if i >= 512:
                        return
                    ait[0] += 1
                    blk, kt = divmod(i, 32)
                    if i == 0:
                        a_qk(0)
                    if i + 1 < 512:
                        a_qk(i + 1)
                    if kt == 0 and blk >= 1:
                        attn_norm(blk - 1)
                    a_pv(i)

            def attn_flush():
                if 'B' not in STAGES:
                    return
                attn_step(512)
                attn_norm(15)

            with ExitStack() as st:
              if 'C' in STAGES:
                  sb = lambda name, shape, dt=F32: st.enter_context(nc.sbuf_tensor(name, list(shape), dt))
                  spm = sb("spm", [128, 6, 16])
                  P.dma("sp", spm[:], sp_d, writes=["spm"])
                  rho = sb("rho", [128, 16]); ur = sb("ur", [128, 16]); ui = sb("ui", [128, 16])
                  WSr = sb("WSr", [128, 16, 128], BF16); WSi = sb("WSi", [128, 16, 128], BF16); M0 = sb("M0", [128, 16, 128], BF16)
                  E4 = [128, 16, 8, 16]
                  Cfr = sb("Cfr", E4, BF16); Cfi = sb("Cfi", E4, BF16); Cbr = sb("Cbr", E4, BF16); Cbi = sb("Cbi", E4, BF16)
                  fl = lambda a, g: a[:, g, :, :].rearrange("p s h -> p (s h)")

                  def dv(fn, reads, writes, eng="dve"):
                      P.op(eng, fn, reads=reads, writes=writes)

                  def tt(out, a, b, op, reads, writes):
                      dv(lambda e: e.tensor_tensor(out, a, b, op), reads, writes)

                  def cmul(or_, oi_, ar, ai, br, bi, reads, okr, oki, ta, tb, tk):
                      ka, kb = tk + "a", tk + "b"
                      tt(ta, ar, br, ALU.mult, reads, [ka])
                      tt(tb, ai, bi, ALU.mult, reads, [kb])
                      tt(or_, ta, tb, ALU.subtract, [ka, kb], [okr])
                      tt(ta, ar, bi, ALU.mult, reads + [okr], [ka])
                      tt(tb, ai, br, ALU.mult, reads + [okr], [kb])
                      tt(oi_, ta, tb, ALU.add, [ka, kb], [oki])

                  with ExitStack() as stg:
                      cnt = [0]

                      def tmp(shape, dt=F32):
                          cnt[0] += 1
                          return stg.enter_context(nc.sbuf_tensor("tmp%d" % cnt[0], list(shape), dt))

                      bcm = tmp([128, 4, 16, 16])
                      P.dma("sp", bcm[:], bc_d, writes=["bcm"])
                      are, aim, ldt, dtile = spm[:, 0, :], spm[:, 1, :], spm[:, 2, :], spm[:, 3, :]
                      mf, mb = spm[:, 4, 0:1], spm[:, 5, 0:1]
                      S = [128, 16]
                      lre = tmp(S); dt_ = tmp(S); al = tmp(S); th = tmp(S); mag = tmp(S)
                      sn = tmp(S); cs = tmp(S); a1 = tmp(S); a2 = tmp(S); a3 = tmp(S)
                      dv(lambda e: e.tensor_scalar(lre[:], are, -1e-4, None, ALU.min), ["spm"], ["lre"])
                      dv(lambda e: e.activation(out=dt_[:], in_=ldt, func=AF.Exp), ["spm"], ["dt"], "act")
                      tt(al[:], lre[:], dt_[:], ALU.mult, ["lre", "dt"], ["al"])
                      tt(th[:], aim, dt_[:], ALU.mult, ["spm", "dt"], ["th"])
                      dv(lambda e: e.activation(out=mag[:], in_=al[:], func=AF.Exp), ["al"], ["mag"], "act")
                      halfpi = tmp([128, 1])
                      dv(lambda e: e.memset(halfpi[:], math.pi / 2), [], ["halfpi"])
                      dv(lambda e: e.activation(out=sn[:], in_=th[:], func=AF.Sin, scale=1.0 / 32), ["th"], ["sn"], "act")
                      dv(lambda e: e.activation(out=cs[:], in_=th[:], func=AF.Sin, scale=1.0 / 32, bias=halfpi[:]), ["th", "halfpi"], ["cs"], "act")
                      for it in range(5):
                          tt(a1[:], sn[:], cs[:], ALU.mult, ["sn", "cs"], ["a1"])
                          tt(a2[:], cs[:], cs[:], ALU.mult, ["cs"], ["a2"])
                          tt(a3[:], sn[:], sn[:], ALU.mult, ["sn"], ["a3"])
                          tt(cs[:], a2[:], a3[:], ALU.subtract, ["a2", "a3"], ["cs"])
                          dv(lambda e: e.tensor_scalar(sn[:], a1[:], 2.0, None, ALU.mult), ["a1"], ["sn"])
                      attn_step()
                      lbr = tmp(S); lbi = tmp(S)
                      tt(lbr[:], mag[:], cs[:], ALU.mult, ["mag", "cs"], ["lbr"])
                      tt(lbi[:], mag[:], sn[:], ALU.mult, ["mag", "sn"], ["lbi"])
                      nr = tmp(S); den = tmp(S); bfr = tmp(S); bfi = tmp(S)
                      dv(lambda e: e.tensor_scalar(nr[:], lbr[:], -1.0, None, ALU.add), ["lbr"], ["nr"])
                      tt(a1[:], lre[:], lre[:], ALU.mult, ["lre"], ["a1"])
                      tt(a2[:], aim, aim, ALU.mult, ["spm"], ["a2"])
                      tt(den[:], a1[:], a2[:], ALU.add, ["a1", "a2"], ["den"])
                      dv(lambda e: e.reciprocal(den[:], den[:]), ["den"], ["den"])
                      tt(a1[:], nr[:], lre[:], ALU.mult, ["nr", "lre"], ["a1"])
                      tt(a2[:], lbi[:], aim, ALU.mult, ["lbi", "spm"], ["a2"])
                      tt(a3[:], a1[:], a2[:], ALU.add, ["a1", "a2"], ["a3"])
                      tt(bfr[:], a3[:], den[:], ALU.mult, ["a3", "den"], ["bfr"])
                      tt(a1[:], lbi[:], lre[:], ALU.mult, ["lbi", "lre"], ["a1"])
                      tt(a2[:], nr[:], aim, ALU.mult, ["nr", "spm"], ["a2"])
                      tt(a3[:], a1[:], a2[:], ALU.subtract, ["a1", "a2"], ["a3"])
                      tt(bfi[:], a3[:], den[:], ALU.mult, ["a3", "den"], ["bfi"])
                      Pwr = tmp([128, 16, 9]); Pwi = tmp([128, 16, 9])
                      dv(lambda e: e.memset(Pwr[:, :, 0], 1.0), [], ["Pw"])
                      dv(lambda e: e.memset(Pwi[:, :, 0], 0.0), ["Pw"], ["Pw"])
                      for k in range(1, 9):
                          cmul(Pwr[:, :, k], Pwi[:, :, k], Pwr[:, :, k - 1], Pwi[:, :, k - 1], lbr[:], lbi[:], ["lbr", "lbi", "Pw"], "Pw", "Pw", a1[:], a2[:], "a12")
                      rinv = tmp(S); i8r = tmp(S); i8i = tmp(S)
                      dv(lambda e: e.activation(out=rho[:], in_=al[:], func=AF.Exp, scale=8.0), ["al"], ["rho"], "act")
                      dv(lambda e: e.reciprocal(rinv[:], rho[:]), ["rho"], ["rinv"])
                      tt(ur[:], Pwr[:, :, 8], rinv[:], ALU.mult, ["Pw", "rinv"], ["ur"])
                      tt(ui[:], Pwi[:, :, 8], rinv[:], ALU.mult, ["Pw", "rinv"], ["ui"])
                      tt(i8r[:], ur[:], rinv[:], ALU.mult, ["ur", "rinv"], ["i8r"])
                      tt(a3[:], ui[:], rinv[:], ALU.mult, ["ui", "rinv"], ["a3"])
                      dv(lambda e: e.tensor_scalar(i8i[:], a3[:], -1.0, None, ALU.mult), ["a3"], ["i8i"])
                      PAr = tmp([128, 16, 8]); PAi = tmp([128, 16, 8]); PCr = tmp([128, 16, 8]); PCi = tmp([128, 16, 8])
                      for (dst, src) in ((PAr, Pwr), (PAi, Pwi)):
                          for k in range(8):
                              dv(lambda e, dst=dst, src=src, k=k: e.tensor_copy(dst[0:64, :, k:k + 1], src[0:64, :, 7 - k:8 - k]), ["Pw", "PA"], ["PA"])
                          dv(lambda e, dst=dst, src=src: e.tensor_copy(dst[64:128, :, :], src[64:128, :, 0:8]), ["Pw", "PA"], ["PA"])
                      for (dst, src) in ((PCr, Pwr), (PCi, Pwi)):
                          dv(lambda e, dst=dst, src=src: e.tensor_copy(dst[0:64, :, :], src[0:64, :, 1:9]), ["Pw", "PC"], ["PC"])
                          for k in range(8):
                              dv(lambda e, dst=dst, src=src, k=k: e.tensor_copy(dst[64:128, :, k:k + 1], src[64:128, :, 8 - k:9 - k]), ["Pw", "PC"], ["PC"])
                      PA2r = tmp([128, 16, 8]); PA2i = tmp([128, 16, 8]); t8a = tmp([128, 16, 8]); t8b = tmp([128, 16, 8])
                      bc8 = lambda a: a.unsqueeze(2).to_broadcast([128, 16, 8])
                      cmul(PA2r[:], PA2i[:], PAr[:], PAi[:], bc8(i8r[:]), bc8(i8i[:]), ["PA", "i8r", "i8i"], "PA2r", "PA2i", t8a[:], t8b[:], "t8")
                      Bbr = tmp([128, 16, 16]); Bbi = tmp([128, 16, 16]); t16a = tmp([128, 16, 16]); t16b = tmp([128, 16, 16])
                      bc16 = lambda a: a.unsqueeze(2).to_broadcast([128, 16, 16])
                      cmul(Bbr[:], Bbi[:], bcm[:, 0, :, :], bcm[:, 1, :, :], bc16(bfr[:]), bc16(bfi[:]), ["bcm", "bfr", "bfi"], "Bbr", "Bbi", t16a[:], t16b[:], "t16")
                      exa = tmp(E4); exb = tmp(E4); Er_ = tmp(E4); Ei_ = tmp(E4)

                      def cexp(pr, pi, mr, mi, reads):
                          for s_ in range(8):
                              b3 = lambda a, s_=s_: a[:, :, s_:s_ + 1].to_broadcast([128, 16, 16])
                              cmul(Er_[:, :, s_, :], Ei_[:, :, s_, :], b3(pr), b3(pi), mr, mi, reads + ["Er", "Ei"], "Er", "Ei", exa[:, :, s_, :], exb[:, :, s_, :], "ex")
                              if s_ % 2 == 1:
                                  attn_step()
                      Ar = tmp(E4, BF16); Ai = tmp(E4, BF16)
                      A2fr = tmp(E4, BF16); A2fi = tmp(E4, BF16); A2br = tmp(E4, BF16); A2bi = tmp(E4, BF16)
                      Ccr = tmp(E4, BF16); nCci = tmp(E4, BF16)
                      attn_step()
                      cexp(PAr, PAi, Bbr[:], Bbi[:], ["PA", "Bbr", "Bbi"])
                      attn_step()
                      dv(lambda e: e.tensor_copy(Ar[:], Er_[:]), ["Er"], ["Ar"])
                      dv(lambda e: e.tensor_copy(Ai[:], Ei_[:]), ["Ei"], ["Ai"])
                      cexp(PA2r, PA2i, Bbr[:], Bbi[:], ["PA2r", "PA2i", "Bbr", "Bbi", "Ar", "Ai"])
                      attn_step()
                      dv(lambda e: e.tensor_scalar(A2fr[:], Er_[:], mf, None, ALU.mult), ["Er", "spm"], ["A2fr"])
                      dv(lambda e: e.tensor_scalar(A2br[:], Er_[:], mb, None, ALU.mult), ["Er", "spm"], ["A2br"])
                      dv(lambda e: e.tensor_scalar(A2fi[:], Ei_[:], mf, None, ALU.mult), ["Ei", "spm"], ["A2fi"])
                      dv(lambda e: e.tensor_scalar(A2bi[:], Ei_[:], mb, None, ALU.mult), ["Ei", "spm"], ["A2bi"])
                      cexp(PCr, PCi, bcm[:, 2, :, :], bcm[:, 3, :, :], ["PC", "bcm", "A2fr", "A2br", "A2fi", "A2bi"])
                      attn_step()
                      dv(lambda e: e.tensor_copy(Ccr[:], Er_[:]), ["Er"], ["Ccr"])
                      dv(lambda e: e.tensor_scalar(nCci[:], Ei_[:], -1.0, None, ALU.mult), ["Ei"], ["nCci"])
                      dv(lambda e: e.tensor_scalar(Cfr[:], Er_[:], mf, None, ALU.mult), ["Er", "spm"], ["Cfr"])
                      dv(lambda e: e.tensor_scalar(Cbr[:], Er_[:], mb, None, ALU.mult), ["Er", "spm"], ["Cbr"])
                      dv(lambda e: e.tensor_scalar(Cfi[:], nCci[:], mf, None, ALU.mult), ["nCci", "spm"], ["Cfi"])
                      dv(lambda e: e.tensor_scalar(Cbi[:], nCci[:], mb, None, ALU.mult), ["nCci", "spm"], ["Cbi"])
                      mt1 = tmp([128, 128]); mt2 = tmp([128, 128])
                      for g in range(16):
                          attn_step()
                          P.op("pe", lambda e, g=g: e.matmul(ps[2][:, 0:128], lhsT=fl(Ar, g), rhs=identb[:], start=True, stop=True), reads=["Ar", "identb"], writes=[("ps", 2)])
                          P.op("pe", lambda e, g=g: e.matmul(ps[3][:, 0:128], lhsT=fl(Ai, g), rhs=identb[:], start=True, stop=True), reads=["Ai", "identb"], writes=[("ps", 3)])
                          P.op("act", lambda e, g=g: e.activation(out=WSr[:, g, :], in_=ps[2][:, 0:128], func=AF.Copy), reads=[("ps", 2)], writes=[("WSr", g)])
                          P.op("act", lambda e, g=g: e.activation(out=WSi[:, g, :], in_=ps[3][:, 0:128], func=AF.Copy), reads=[("ps", 3)], writes=[("WSi", g)])
                          P.op("pe", lambda e, g=g: e.matmul(ps[2][:, 0:128], lhsT=fl(A2fr, g), rhs=fl(Ccr, g), start=True, stop=False), reads=["A2fr", "Ccr"], writes=[("ps", 2)])
                          P.op("pe", lambda e, g=g: e.matmul(ps[2][:, 0:128], lhsT=fl(A2fi, g), rhs=fl(nCci, g), start=False, stop=True), reads=["A2fi", "nCci"], writes=[("ps", 2)])
                          P.op("pe", lambda e, g=g: e.matmul(ps[3][:, 0:128], lhsT=fl(A2br, g), rhs=fl(Ccr, g), start=True, stop=False), reads=["A2br", "Ccr"], writes=[("ps", 3)])
                          P.op("pe", lambda e, g=g: e.matmul(ps[3][:, 0:128], lhsT=fl(A2bi, g), rhs=fl(nCci, g), start=False, stop=True), reads=["A2bi", "nCci"], writes=[("ps", 3)])
                          tt(mt1[:], ps[2][:, 0:128], c128[:, 2, :], ALU.mult, [("ps", 2), "c128"], ["mt1"])
                          tt(mt2[:], ps[3][:, 0:128], c128[:, 3, :], ALU.mult, [("ps", 3), "c128"], ["mt2"])
                          tt(mt1[:], mt1[:], mt2[:], ALU.add, ["mt1", "mt2"], ["mt1"])
                          dv(lambda e, g=g: e.scalar_tensor_tensor(M0[:, g, :], c128[:, 1, :], dtile[:, g:g + 1], mt1[:], ALU.mult, ALU.add),
                             ["mt1", "c128", "spm"], [("M0", g)])
                      P.barrier()
                      P.emit()
                  if 'D' in STAGES:
                    U = sb("U", [128, 16, NCH], BF16)
                    for j in range(2):
                        for g8 in range(8):
                            for s in range(8):
                                P.dma("sp" if (g8 + s) % 2 == 0 else "pool", U[s * 16:(s + 1) * 16, j * 8 + g8, :],
                                      UT[g8 * 16:(g8 + 1) * 16, j, s * NCH:(s + 1) * NCH], writes=[("U", j * 8 + g8, s)])
                    Ukeys = lambda g: [("U", g, s) for s in range(8)]
                    Xr = sb("Xr", [128, 8, NCH + 2], BF16)
                    Xi = sb("Xi", [128, 8, NCH + 2], BF16)
                    YG = [sb("YG%d" % i, [128, NCH]) for i in range(2)]
                    Tr = sb("Tr", [128, 8, NCH]); Ti = sb("Ti", [128, 8, NCH])
                    Sr = sb("Sr", [128, 8, NCH], BF16); Si = sb("Si", [128, 8, NCH], BF16)
                    ta = sb("ta", [128, 8, 256]); tb_ = sb("tbb", [128, 8, 256])
                    ta2 = ta[:, 0:2, :].rearrange("p a b -> p (a b)"); tb2 = tb_[:, 0:2, :].rearrange("p a b -> p (a b)")
                    sgnv = sb("sgnv", [128, 1]); u2r = sb("u2r", [128, 8]); u2i = sb("u2i", [128, 8]); u3r = sb("u3r", [128, 8]); u3i = sb("u3i", [128, 8])
                    for h in range(2):
                        G8 = range(h * 8, h * 8 + 8)
                        gs = slice(h * 8, h * 8 + 8)
                        dv(lambda e: e.memset(Xr[:], 0.0), [("Xr", gl) for gl in range(8)], [("Xr", gl) for gl in range(8)])
                        dv(lambda e: e.memset(Xi[:], 0.0), [("Xi", gl) for gl in range(8)], [("Xi", gl) for gl in range(8)])
                        dv(lambda e: e.memset(Tr[:, :, 0:1], 1.0), ["T"], ["T"])
                        dv(lambda e: e.memset(Ti[:, :, 0:1], 0.0), ["T"], ["T"])
                        if h == 0:
                            tt(sgnv[:], spm[:, 4, 0:1], spm[:, 5, 0:1], ALU.subtract, ["spm"], ["sgnv"])
                        dv(lambda e, gs=gs: e.tensor_copy(u2r[:], ur[:, gs]), ["ur", "u2"], ["u2"])
                        dv(lambda e, gs=gs: e.tensor_scalar(u2i[:], ui[:, gs], sgnv[:, 0:1], None, ALU.mult), ["ui", "u2", "sgnv"], ["u2"])
                        m = 1
                        while m < NCH:
                            bq = lambda a, m=m: a[:, :].unsqueeze(2).to_broadcast([128, 8, m])
                            cmul(Tr[:, :, m:2 * m], Ti[:, :, m:2 * m], Tr[:, :, 0:m], Ti[:, :, 0:m], bq(u2r), bq(u2i), ["T", "u2"], "T", "T",
                                 ta[:, :, 0:m], tb_[:, :, 0:m], "tab")
                            cmul(u3r[:], u3i[:], u2r[:], u2i[:], u2r[:], u2i[:], ["u2", "T"], "u3", "u3", ta[:, :, 0], tb_[:, :, 0], "tab")
                            dv(lambda e: e.tensor_copy(u2r[:], u3r[:]), ["u3", "T"], ["u2"])
                            dv(lambda e: e.tensor_copy(u2i[:], u3i[:]), ["u3", "T", "u2"], ["u2"])
                            attn_step()
                            m *= 2
                        pa2 = ta[:, 2:4, :].rearrange("p a b -> p (a b)"); pb2 = tb_[:, 2:4, :].rearrange("p a b -> p (a b)")

                        def tp(out, a, b, op, reads, writes):
                            P.op("pool", lambda e: e.tensor_tensor(out, a, b, op), reads=reads, writes=writes)

                        def s_mm(g):
                            b_re, b_im = (2, 3) if g % 2 == 0 else (6, 7)
                            P.op("pe", lambda e: e.matmul(ps[b_re][:, :], lhsT=WSr[:, g, :], rhs=U[:, g, :], start=True, stop=True),
                                 reads=Ukeys(g), writes=[("ps", b_re)])
                            P.op("pe", lambda e: e.matmul(ps[b_im][:, :], lhsT=WSi[:, g, :], rhs=U[:, g, :], start=True, stop=True),
                                 reads=Ukeys(g), writes=[("ps", b_im)])

                        if h == 0:
                            s_mm(0)
                        for g in G8:
                            gl = g - h * 8
                            b_re, b_im = (2, 3) if g % 2 == 0 else (6, 7)
                            tt(ta2, Tr[:, gl, :], ps[b_re][:, :], ALU.mult, ["T", ("ps", b_re)], ["ta2"])
                            tt(tb2, Ti[:, gl, :], ps[b_im][:, :], ALU.mult, ["T", ("ps", b_im)], ["tb2"])
                            tt(Sr[:, gl, :], ta2, tb2, ALU.add, ["ta2", "tb2"], [("Sr", gl)])
                            tt(ta2, Tr[:, gl, :], ps[b_im][:, :], ALU.mult, ["T", ("ps", b_im)], ["ta2"])
                            tt(tb2, Ti[:, gl, :], ps[b_re][:, :], ALU.mult, ["T", ("ps", b_re)], ["tb2"])
                            tt(Si[:, gl, :], ta2, tb2, ALU.subtract, ["ta2", "tb2"], [("Si", gl)])
                            for (Sx, nm_) in ((Sr, "Sr"), (Si, "Si")):
                                dv(lambda e, Sx=Sx, g=g, gl=gl: e.tensor_tensor_scan(Sx[0:64, gl, :], rho[0:64, g:g + 1].to_broadcast([64, NCH]), Sx[0:64, gl, :], 0.0, ALU.mult, ALU.add),
                                   [(nm_, gl), "rho"], [(nm_, gl)])
                                dv(lambda e, Sx=Sx, g=g, gl=gl: e.tensor_tensor_scan(Sx[64:128, gl, ::-1], rho[64:128, g:g + 1].to_broadcast([64, NCH]), Sx[64:128, gl, ::-1], 0.0, ALU.mult, ALU.add),
                                   [(nm_, gl), "rho"], [(nm_, gl)])
                            tp(pa2, Tr[:, gl, :], Sr[:, gl, :], ALU.mult, ["T", ("Sr", gl)], ["pa2"])
                            tp(pb2, Ti[:, gl, :], Si[:, gl, :], ALU.mult, ["T", ("Si", gl)], ["pb2"])
                            tp(Xr[:, gl, 1:NCH + 1], pa2, pb2, ALU.subtract, ["pa2", "pb2"], [("Xr", gl)])
                            tp(pa2, Tr[:, gl, :], Si[:, gl, :], ALU.mult, ["T", ("Si", gl)], ["pa2"])
                            tp(pb2, Ti[:, gl, :], Sr[:, gl, :], ALU.mult, ["T", ("Sr", gl)], ["pb2"])
                            tp(Xi[:, gl, 1:NCH + 1], pa2, pb2, ALU.add, ["pa2", "pb2"], [("Xi", gl)])
                            if g + 1 < 16:
                                s_mm(g + 1)
                            attn_step()
                            pk = ("ps", b_re)
                            mm = [(M0[:, g, :], U[:, g, :], Ukeys(g)),
                                  (fl(Cfr, g), Xr[:, gl, 0:NCH], [("Xr", gl)]),
                                  (fl(Cfi, g), Xi[:, gl, 0:NCH], [("Xi", gl)]),
                                  (fl(Cbr, g), Xr[:, gl, 2:NCH + 2], [("Xr", gl)]),
                                  (fl(Cbi, g), Xi[:, gl, 2:NCH + 2], [("Xi", gl)])]
                            for i, (lt, rh, rk) in enumerate(mm):
                                P.op("pe", lambda e, lt=lt, rh=rh, i=i, b_re=b_re: e.matmul(ps[b_re][:, :], lhsT=lt, rhs=rh, start=(i == 0), stop=(i == 4)), reads=rk, writes=[pk])
                            yg = YG[g % 2]
                            P.op("act", lambda e, yg=yg, b_re=b_re: e.activation(out=yg[:], in_=ps[b_re][:, :], func=AF.Gelu_apprx_tanh), reads=[pk], writes=[("YG", g % 2)])
                            for t in range(8):
                                P.dma("sp" if t % 2 == 0 else "pool", EX[256 + g * 16:256 + (g + 1) * 16, t * NCH:(t + 1) * NCH], yg[t * 16:(t + 1) * 16, :], reads=[("YG", g % 2)])
                        if h == 1:
                            attn_flush()
                        if h == 0:
                            dv(lambda e: e.nop(), ["pa2", "pb2", "ta2", "tb2"], ["taba", "tabb"])
                    P.barrier()
                    P.emit()
    return nc


def build_phase2():
    nc = bass.Bass("TRN2", target_bir_lowering=False)
    mixin = _din(nc, "mixin", [128, 16, 1024])
    x2 = _din(nc, "x2", [2, 128, 16, 512])
    pTd = _din(nc, "pT", [128, 2, 1024])
    wgs = _din(nc, "wgs", [8, 128, 16, 128])
    wglu = _din(nc, "wglu", [16, 128, 8, 128])
    wout = _din(nc, "wout", [16, 128, 16, 128])
    wpg = _din(nc, "wpg", [16, 128, 16, 128])
    wpp = _din(nc, "wpp", [16, 128, 2, 128])
    vec_d = _din(nc, "vecs", [128, 4, 16])
    outT = nc.dram_tensor("outT", [128, 16, 1024], F32, kind="ExternalOutput").ap()
    with ExitStack() as st:
        P = Prog(nc, st)
        phase2_body(nc, P, st, dict(mixin=mixin, x2=x2, pTd=pTd, wgs=wgs, wglu=wglu, wout=wout, wpg=wpg, wpp=wpp, vec_d=vec_d, outT=outT), None, None)
    return nc


def phase2_body(nc, P, st, D, mixA, ysA):
    if True:
        sb = lambda name, shape, dt=F32: st.enter_context(nc.sbuf_tensor(name, list(shape), dt))
        ps = [st.enter_context(nc.psum_tensor("qs%d" % i, [128, 512], F32)) for i in range(8)]
        vecs = sb("vecss", [128, 4, 16])
        ones = sb("ones2", [128, 128], BF16)
        P.dma("sp", vecs[:], D["vec_d"], writes=["vecs"])
        P.op("dve", lambda e: e.memset(ones[:], 1.0), writes=["ones2"])
        NW = 4
        xn = sb("xn2", [128, 16, 1024], BF16)
        mix = sb("mix2", [128, 16, 1024], BF16)
        pTb = sb("pTb", [128, 2, 1024], BF16)
        rt = sb("rt2", [128, 512])
        rstd = sb("rstd2", [128, 1024])
        ga = sb("ga2", [128, 512]); sgb = sb("sgb2", [128, 512]); gsl = sb("gsl2", [128, 512])
        ot = [sb("ot2_0", [128, 512])] * 2
        xres = [sb("xres0", [128, 1024])] * 2
        wcount = [0]
        casters = ["act", "dve"]
        WS = {}

        def load_w(src_tile, K):
            wst, wb = WS["wst"], WS["wb"]
            s = wcount[0] % len(wb)
            s2 = wcount[0] % len(wst)
            eng = casters[wcount[0] % len(casters)]
            wcount[0] += 1
            P.dma("sp", wst[s2][:, 0:K, :], src_tile, writes=[("wst", s2)])
            if eng == "act":
                P.op("act", lambda e: e.activation(out=wb[s][:, 0:K, :], in_=wst[s2][:, 0:K, :], func=AF.Copy), reads=[("wst", s2)], writes=[("wb", s)])
            else:
                P.op(eng, lambda e: e.tensor_copy(wb[s][:, 0:K, :], wst[s2][:, 0:K, :]), reads=[("wst", s2)], writes=[("wb", s)])
            return wb[s], ("wb", s)

        def rms_bcast(src_blk, key, sqt, blk):
            for hh in range(2):
                P.op("act", lambda e, hh=hh: e.activation(out=sqt[:], in_=src_blk[:, hh * 8:(hh + 1) * 8, :], func=AF.Square), reads=[key], writes=["sq"])
                for kc in range(8):
                    P.op("pe", lambda e, kc=kc, hh=hh: e.matmul(ps[7][:, :], lhsT=ones[:], rhs=sqt[:, kc, :], start=(hh == 0 and kc == 0), stop=(hh == 1 and kc == 7)),
                         reads=["sq", "ones2"], writes=[("ps", 7)])
            P.op("act", lambda e: e.activation(out=rt[:], in_=ps[7][:, :], func=AF.Sqrt, scale=1.0 / 2048, bias=EPS), reads=[("ps", 7)], writes=["rt"])
            P.op("dve", lambda e: e.reciprocal(rstd[:, blk * 512:(blk + 1) * 512], rt[:]), reads=["rt"], writes=[("rstd", blk)])

        def norm_to_bf16(dst_blk, src_blk, srckey, row, blk, dkey):
            for k in range(16):
                P.op("dve", lambda e, k=k: e.scalar_tensor_tensor(dst_blk[:, k, :], src_blk[:, k, :], vecs[:, row, k:k + 1], rstd[:, blk * 512:(blk + 1) * 512], ALU.mult, ALU.mult)
                     if True else None, reads=[srckey, ("rstd", blk), "vecs"], writes=[(dkey, blk, k)])

        sG = ExitStack()
        ys = sG.enter_context(nc.sbuf_tensor("ys2", [128, 8, 1024], BF16)) if ysA is None else ysA
        with ExitStack() as s1:
            sb1 = lambda name, shape, dt=F32: s1.enter_context(nc.sbuf_tensor(name, list(shape), dt))
            xb = sb1("xb2", [128, 16, 512])
            sq = sb1("sq2a", [128, 8, 512], BF16)
            mixf = sb1("mixf", [128, 8, 512])
            pTf = sb1("pTfs", [128, 2, 1024])
            P.dma("sp", pTf[:], D["pTd"], writes=["pTf"])
            P.op("dve", lambda e: e.tensor_copy(pTb[:], pTf[:]), reads=["pTf"], writes=["pTb"])
            for blk in range(2):
                tk = slice(blk * 512, (blk + 1) * 512)
                P.dma("sp", xb[:], D["x2"][blk], writes=["xb"])
                if mixA is None:
                    P.dma("sp", mixf[:], D["mixin"][:, 8:16, tk], writes=["mixf"])
                    P.op("act", lambda e, tk=tk: e.activation(out=ys[:, :, tk], in_=mixf[:], func=AF.Copy), reads=["mixf"], writes=[("ys", blk)])
                    P.dma("sp", mixf[:], D["mixin"][:, 0:8, tk], writes=["mixf"])
                    P.op("dve", lambda e, tk=tk: e.tensor_copy(mix[:, 0:8, tk], mixf[:]), reads=["mixf"], writes=[("mixa", blk)])
                rms_bcast(xb[:], "xb", sq, blk)
                for k in range(16):
                    P.op("dve", lambda e, k=k, tk=tk, blk=blk: e.scalar_tensor_tensor(xn[:, k, tk], xb[:, k, :], vecs[:, 0, k:k + 1], rstd[:, tk], ALU.mult, ALU.mult),
                         reads=["xb", ("rstd", blk), "vecs"], writes=[("xn", blk, k)])
            if mixA is not None:
                P.op("pool", lambda e: e.tensor_copy(mix[:, 0:8, :], mixA[:]), reads=[], writes=[("mixa", 0)])
            P.barrier()
            P.emit()
        WS["wst"] = [sG.enter_context(nc.sbuf_tensor("wstG%d" % i, [128, 16, 128], F32)) for i in range(6)]
        WS["wb"] = [sG.enter_context(nc.sbuf_tensor("wbG%d" % i, [128, 16, 128], BF16)) for i in range(6)]
        for m in range(8):
            wa, ka = load_w(D["wglu"][m], 8)
            wbb, kb = load_w(D["wglu"][8 + m], 8)
            wg, kg = load_w(D["wgs"][m], 16)
            for blk in range(2):
                tk = slice(blk * 512, (blk + 1) * 512)
                for k in range(8):
                    P.op("pe", lambda e, k=k, tk=tk, wa=wa: e.matmul(ps[0][:, :], lhsT=wa[:, k, :], rhs=ys[:, k, tk], start=(k == 0), stop=(k == 7)), reads=[ka], writes=[("ps", 0)])
                for k in range(8):
                    P.op("pe", lambda e, k=k, tk=tk, wbb=wbb: e.matmul(ps[1][:, :], lhsT=wbb[:, k, :], rhs=ys[:, k, tk], start=(k == 0), stop=(k == 7)), reads=[kb], writes=[("ps", 1)])
                for k in range(16):
                    P.op("pe", lambda e, k=k, tk=tk, wg=wg: e.matmul(ps[2][:, :], lhsT=wg[:, k, :], rhs=xn[:, k, tk], start=(k == 0), stop=(k == 15)), reads=[kg], writes=[("ps", 2)])
                P.op("act", lambda e, m=m: e.activation(out=ga[:], in_=ps[0][:, :], func=AF.Identity, bias=vecs[:, 3, m:m + 1]), reads=[("ps", 0), "vecs"], writes=["ga"])
                P.op("act", lambda e, m=m: e.activation(out=sgb[:], in_=ps[1][:, :], func=AF.Sigmoid, bias=vecs[:, 3, 8 + m:9 + m]), reads=[("ps", 1), "vecs"], writes=["sgb"])
                P.op("act", lambda e: e.activation(out=gsl[:], in_=ps[2][:, :], func=AF.Silu), reads=[("ps", 2)], writes=["gsl"])
                P.op("dve", lambda e: e.tensor_tensor(ga[:], ga[:], sgb[:], ALU.mult), reads=["ga", "sgb"], writes=["ga"])
                P.op("dve", lambda e, m=m, tk=tk: e.tensor_tensor(mix[:, 8 + m, tk], ga[:], gsl[:], ALU.mult), reads=["ga", "gsl"], writes=[("mixs", m, blk)])
        P.barrier()
        P.emit()
        sG.close()
        wcount[0] = 0
        H = sb("H2", [128, 16, 1024])
        sq = sb("sq2b", [128, 8, 512], BF16)
        WS["wst"] = [sb("wstO%d" % i, [128, 16, 128]) for i in range(4)]
        WS["wb"] = [sb("wbO%d" % i, [128, 16, 128], BF16) for i in range(4)]
        for m in range(16):
            wo, ko = load_w(D["wout"][m], 16)
            xr = xres[m % 2]
            P.dma("sp", xr[:].rearrange("p (b t) -> p b t", b=2), D["x2"][:, :, m, :].rearrange("b p t -> p b t"), writes=[("xres", 0)])
            for blk in range(2):
                tk = slice(blk * 512, (blk + 1) * 512)
                pb = ps[3 + (2 * m + blk) % 3]
                pk = ("ps", 3 + (2 * m + blk) % 3)
                for k in range(16):
                    P.op("pe", lambda e, k=k, tk=tk, pb=pb, wo=wo: e.matmul(pb[:, :], lhsT=wo[:, k, :], rhs=mix[:, k, tk], start=(k == 0), stop=(k == 15)),
                         reads=[ko] + ([("mixs", mm, blk) for mm in range(8)] if m == 0 else []), writes=[pk])
                P.op("dve", lambda e, m=m, tk=tk, pb=pb, xr=xr: e.tensor_tensor(H[:, m, tk], pb[:, :], xr[:, tk], ALU.add), reads=[pk, ("xres", 0)], writes=[("H", m, blk)])
        for blk in range(2):
            tk = slice(blk * 512, (blk + 1) * 512)
            P.op("dve", lambda e: e.nop(), reads=[("H", m, blk) for m in range(16)], writes=[("Hall", blk)])
            rms_bcast(H[:, :, tk], ("Hall", blk), sq, blk)
            for k in range(16):
                P.op("dve", lambda e, k=k, tk=tk: e.scalar_tensor_tensor(xn[:, k, tk], H[:, k, tk], vecs[:, 1, k:k + 1], rstd[:, tk], ALU.mult, ALU.mult),
                     reads=[("Hall", blk), ("rstd", blk), "vecs"], writes=[("hn", blk, k)])
        for m in range(16):
            wq, kq = load_w(D["wpg"][m], 16)
            wp_, kp = load_w(D["wpp"][m], 2)
            for blk in range(2):
                tk = slice(blk * 512, (blk + 1) * 512)
                pb = ps[(2 * m + blk) % 2]
                pk = ("ps", (2 * m + blk) % 2)
                for k in range(16):
                    P.op("pe", lambda e, k=k, tk=tk, pb=pb, wq=wq: e.matmul(pb[:, :], lhsT=wq[:, k, :], rhs=xn[:, k, tk], start=(k == 0), stop=(k == 15)),
                         reads=[kq] + ([("hn", blk, kk) for kk in range(16)] if m == 0 else []), writes=[pk])
                pb2 = ps[2 + (2 * m + blk) % 2]
                pk2 = ("ps", 2 + (2 * m + blk) % 2)
                for k in range(2):
                    P.op("pe", lambda e, k=k, tk=tk, pb2=pb2, wp_=wp_: e.matmul(pb2[:, :], lhsT=wp_[:, k, :], rhs=pTb[:, k, tk], start=(k == 0), stop=(k == 1)), reads=[kp, "pTb"], writes=[pk2])
                P.op("act", lambda e, pb=pb: e.activation(out=ga[:], in_=pb[:, :], func=AF.Sigmoid), reads=[pk], writes=["ga"])
                P.op("dve", lambda e, pb2=pb2: e.tensor_tensor(sgb[:], ga[:], pb2[:, :], ALU.mult), reads=["ga", pk2], writes=["sgb"])
                P.op("dve", lambda e, m=m, tk=tk: e.tensor_tensor(H[:, m, tk], H[:, m, tk], sgb[:], ALU.add),
                     reads=["sgb"] + [("hn", blk, kk) for kk in range(16)], writes=[("H2", m, blk)])
        for blk in range(2):
            tk = slice(blk * 512, (blk + 1) * 512)
            P.op("dve", lambda e: e.nop(), reads=[("H2", m, blk) for m in range(16)], writes=[("H2all", blk)])
            rms_bcast(H[:, :, tk], ("H2all", blk), sq, blk)
            for m in range(16):
                o_ = ot[m % 2]
                P.op("dve", lambda e, m=m, tk=tk, o_=o_: e.scalar_tensor_tensor(o_[:], H[:, m, tk], vecs[:, 2, m:m + 1], rstd[:, tk], ALU.mult, ALU.mult),
                     reads=[("H2all", blk), ("rstd", blk), "vecs"], writes=[("ot", 0)])
                P.dma("sp", D["outT"][:, m, tk], o_[:], reads=[("ot", 0)], writes=[("out", m, blk)])
        P.barrier()
        P.emit()


_CACHE = {}


def _perm(r):
    return np.array([8 * (128 * r + cl) + t for t in range(8) for cl in range(128)], dtype=np.int64)


def _fm(a):
    F_, T_ = a.shape
    return np.ascontiguousarray(a.reshape(F_ // 128, 128, T_).transpose(1, 0, 2))


def kernel(x, p, norm_mix, w_in, q_norm, k_norm, ssm_a_re, ssm_a_im, ssm_log_dt,
           ssm_b_re, ssm_b_im, ssm_c_re, ssm_c_im, ssm_d, w_glu, b_glu, w_out,
           norm_ple, w_ple_gate, w_ple_proj, norm_final):
    f32 = np.float32
    x = np.asarray(x, f32); p = np.asarray(p, f32)
    if "p1" not in _CACHE:
        _CACHE["p1"] = build_phase1()
        _CACHE["p2"] = build_phase2()
    inv_freq = (np.float32(10000.0) ** (-np.arange(32, dtype=f32) / np.float32(32))).astype(f32)
    t = np.arange(L)
    rows = (t // 64).astype(f32); cols = (t % 64).astype(f32)
    cosT = np.zeros((128, L), f32); sinT = np.zeros((128, L), f32)
    for d in range(128):
        pos = rows if d < 64 else cols
        ang = (pos * inv_freq[d % 32]).astype(f32)
        cosT[d] = np.cos(ang); sinT[d] = np.sin(ang)
    csT = np.ascontiguousarray(np.stack([cosT, sinT], 1).reshape(128, 2, 16, 256).transpose(2, 0, 1, 3))
    Rm = np.zeros((128, 128), f32)
    for d in range(128):
        if d % 64 < 32:
            Rm[d + 32, d] = -1.0
        else:
            Rm[d - 32, d] = 1.0
    ident = np.eye(128, dtype=f32)
    si = np.arange(128) // 16
    maskF = (si[None, :] >= si[:, None]).astype(f32)
    maskB = (si[:, None] >= si[None, :]).astype(f32)
    c128 = np.ascontiguousarray(np.stack([Rm, ident, maskF, maskB], 1))
    nm = lambda v: np.ascontiguousarray(np.asarray(v, f32).reshape(-1, 128).T)
    w_in0 = np.asarray(w_in[0], f32)
    in1, in2 = [], []
    for c in range(8):
        b, r = c // 4, c % 4
        kv = r // 2
        G0 = 16 * r
        colsel = np.concatenate([np.arange(256 * r, 256 * r + 256), np.arange(1024 + 128 * kv, 1024 + 128 * kv + 128),
                                 np.arange(1280 + 128 * kv, 1280 + 128 * kv + 128), np.arange(1536 + 256 * r, 1536 + 256 * r + 256),
                                 np.arange(2560 + 256 * r, 2560 + 256 * r + 256)])
        w1 = np.ascontiguousarray(_fm(w_in0[:, colsel]).reshape(128, 16, 8, 128).transpose(2, 0, 1, 3))
        if r == 0:
            xTb = np.ascontiguousarray(_fm(np.ascontiguousarray(x[b].T)).reshape(128, 16, 16, 256).transpose(2, 0, 1, 3))
        qkn = np.ascontiguousarray(np.stack([q_norm[0], k_norm[0]], 1).astype(f32))
        dp = lambda a: np.ascontiguousarray(np.asarray(a[0], f32)[:, G0:G0 + 16, :].transpose(0, 2, 1).reshape(128, 16))
        are = dp(ssm_a_re); aim = dp(ssm_a_im)
        ldt = np.ascontiguousarray(np.repeat(np.asarray(ssm_log_dt[0], f32)[:, None, G0:G0 + 16], 64, 1).reshape(128, 16))
        dt_ = np.asarray(ssm_d[0], f32)[G0 * 16:(G0 + 16) * 16].reshape(16, 16)
        dtile = np.ascontiguousarray(np.tile(dt_.T, (8, 1)))
        mfv = np.zeros((128, 16), f32); mfv[:64] = 1.0
        mbv = np.zeros((128, 16), f32); mbv[64:] = 1.0
        ssmp = np.ascontiguousarray(np.stack([are, aim, ldt, dtile, mfv, mbv], 1))
        Bl = lambda a: np.asarray(a[0], f32)[:, G0:G0 + 16].transpose(0, 2, 1, 3).reshape(128, 16, 16)
        Cl = lambda a: np.asarray(a[0], f32)[:, G0:G0 + 16].transpose(0, 3, 1, 2).reshape(128, 16, 16)
        ssmbc = np.ascontiguousarray(np.stack([Bl(ssm_b_re), Bl(ssm_b_im), Cl(ssm_c_re), Cl(ssm_c_im)], 1))
        in1.append({"xT": xTb, "w1": w1, "nmix": nm(norm_mix[0]), "qkn": qkn, "csT": csT, "c128": c128,
                    "ssmp": ssmp, "ssmbc": ssmbc})
    res1 = run_bass_kernel_spmd(_CACHE["p1"], in1, core_ids=list(range(8)))
    EXs = [np.asarray(r["EX"], f32) for r in res1.results]
    if os.environ.get("KP1ONLY"):
        return EXs
    vecs = np.ascontiguousarray(np.stack([nm(norm_mix[0]), nm(norm_ple[0]), nm(norm_final), nm(b_glu[0])], 1))
    tl = lambda w: np.ascontiguousarray(_fm(w).reshape(128, w.shape[0] // 128, w.shape[1] // 128, 128).transpose(2, 0, 1, 3))
    wgs = tl(w_in0[:, 3584:4608]); wglu = tl(np.asarray(w_glu[0], f32)); wout = tl(np.asarray(w_out[0], f32))
    wpg = tl(np.asarray(w_ple_gate[0], f32)); wpp = tl(np.asarray(w_ple_proj[0], f32))
    for c in range(8):
        b, r = c // 4, c % 4
        pr = _perm(r)
        sel = np.concatenate([np.arange(tt * NCH + 128 * r, tt * NCH + 128 * r + 128) for tt in range(8)])
        attn = np.concatenate([EXs[b * 4 + q][0:256][:, sel] for q in range(4)], 0)
        ssm = np.concatenate([EXs[b * 4 + q][256:512][:, sel] for q in range(4)], 0)
        mixin = _fm(np.concatenate([attn, ssm], 0))
        x2 = np.ascontiguousarray(_fm(np.ascontiguousarray(x[b].T[:, pr])).reshape(128, 16, 2, 512).transpose(2, 0, 1, 3))
        pT = _fm(np.ascontiguousarray(p[0, b].T[:, pr]))
        in2.append({"mixin": mixin, "x2": x2, "pT": pT, "wgs": wgs, "wglu": wglu, "wout": wout, "wpg": wpg, "wpp": wpp, "vecs": vecs})
    res2 = run_bass_kernel_spmd(_CACHE["p2"], in2, core_ids=list(range(8)))
    out = np.zeros((2, L, 2048), f32)
    for c in range(8):
        b, r = c // 4, c % 4
        o = np.asarray(res2.results[c]["outT"], f32)
        o = o.transpose(1, 0, 2).reshape(2048, 1024)
        out[b, _perm(r), :] = o.T
    return out
```

```python
import math
import os
STAGES = os.environ.get('KSTAGES', 'ABCD')
from contextlib import ExitStack
import numpy as np
import concourse.bass as bass
import concourse.mybir as mybir
from concourse.bass_utils import run_bass_kernel_spmd

F32 = mybir.dt.float32
BF16 = mybir.dt.bfloat16
ALU = mybir.AluOpType
AF = mybir.ActivationFunctionType

NDMA = 12
EPS = 1e-6
L = 4096
NCH = 512


class Prog:
    ENGS = ("pe", "act", "dve", "pool", "sp")

    def __init__(self, nc, stack):
        self.nc = nc
        self.sem = {e: stack.enter_context(nc.semaphore("s_" + e)) for e in self.ENGS}
        self.dsem = {e: [stack.enter_context(nc.semaphore("d_%s%d" % (e, i))) for i in range(NDMA)]
                     for e in ("sp", "pool")}
        self.cnt = {e: 0 for e in self.ENGS}
        self.dcnt = {e: [0] * NDMA for e in self.dsem}
        self.dnext = {e: 0 for e in self.dsem}
        self.waited = {e: {} for e in self.ENGS}
        self._reset()

    def _reset(self):
        self.instrs = []
        self.last_writer = {}
        self.readers = {}
        self.last_idx = {}
        self.dmas = []

    def op(self, eng, fn, reads=(), writes=(), dma=False, extra=()):
        idx = len(self.instrs)
        deps = set(extra)
        for k in reads:
            if k in self.last_writer:
                deps.add(self.last_writer[k])
        for k in writes:
            if k in self.last_writer:
                deps.add(self.last_writer[k])
            for r in self.readers.get(k, ()):
                deps.add(r)
        self.instrs.append(dict(eng=eng, fn=fn, deps=deps, dma=dma))
        for k in reads:
            lst = self.readers.setdefault(k, [])
            if not dma:
                lst[:] = [r for r in lst if self.instrs[r]["dma"] or self.instrs[r]["eng"] != eng]
            lst.append(idx)
        for k in writes:
            self.last_writer[k] = idx
            self.readers[k] = []
        self.last_idx[eng] = idx
        if dma:
            self.dmas.append(idx)
        return idx

    def dma(self, eng, out, in_, reads=(), writes=()):
        return self.op(eng, lambda e: e.dma_start(out=out, in_=in_), reads, writes, dma=True)

    def barrier(self):
        deps = set(self.last_idx.values()) | set(self.dmas)
        for e in self.ENGS:
            self.op(e, lambda en: en.nop(), extra=deps)
        self.last_writer = {}
        self.readers = {}
        self.dmas = []

    def emit(self):
        nc = self.nc
        ins = self.instrs
        needed = set()
        for i, it in enumerate(ins):
            for d in it["deps"]:
                if ins[d]["eng"] == "pe" and it["eng"] == "pe" and not ins[d]["dma"]:
                    continue
                needed.add(d)
        for i, it in enumerate(ins):
            e = it["eng"]
            if it["dma"]:
                s = self.dnext[e]
                self.dnext[e] = (s + 1) % NDMA
                self.dcnt[e][s] += 16
                it["tok"] = (self.dsem[e][s], self.dcnt[e][s], "d_%s%d" % (e, s))
                it["inc"] = 16
            elif i in needed:
                self.cnt[e] += 1
                it["tok"] = (self.sem[e], self.cnt[e], "s_" + e)
                it["inc"] = 1
            else:
                it["tok"] = None
        per = {e: [] for e in self.ENGS}
        for i, it in enumerate(ins):
            per[it["eng"]].append(i)

        def replay(ename, eng):
            w = self.waited[ename]
            for i in per[ename]:
                it = ins[i]
                waits = {}
                for d in it["deps"]:
                    t = ins[d]["tok"]
                    if t is None:
                        continue
                    if w.get(t[2], 0) < t[1] and waits.get(t[2], (None, 0))[1] < t[1]:
                        waits[t[2]] = (t[0], t[1])
                if it["dma"]:
                    t = it["tok"]
                    prev = t[1] - 16
                    if prev > 0 and w.get(t[2], 0) < prev and waits.get(t[2], (None, 0))[1] < prev:
                        waits[t[2]] = (t[0], prev)
                for name, (s, v) in waits.items():
                    eng.wait_ge(s, v)
                    w[name] = v
                bi = it["fn"](eng)
                if it["tok"] is not None:
                    bi.then_inc(it["tok"][0], it["inc"])

        with nc.Block() as block:
            @block.tensor
            def _(e):
                replay("pe", e)

            @block.scalar
            def _(e):
                replay("act", e)

            @block.vector
            def _(e):
                replay("dve", e)

            @block.gpsimd
            def _(e):
                replay("pool", e)

            @block.sync
            def _(e):
                replay("sp", e)
        self._reset()


def _din(nc, name, shape):
    return nc.dram_tensor(name, list(shape), F32, kind="ExternalInput").ap()


def build_phase1():
    nc = bass.Bass("TRN2", target_bir_lowering=False)
    xT = _din(nc, "xT", [16, 128, 16, 256])
    w1 = _din(nc, "w1", [8, 128, 16, 128])
    nmix_d = _din(nc, "nmix", [128, 16])
    qkn_d = _din(nc, "qkn", [128, 2])
    cs_d = _din(nc, "csT", [16, 128, 2, 256])
    c128_d = _din(nc, "c128", [128, 4, 128])
    sp_d = _din(nc, "ssmp", [128, 6, 16])
    bc_d = _din(nc, "ssmbc", [128, 4, 16, 16])
    EX = nc.dram_tensor("EX", [512, L], F32, kind="ExternalOutput").ap()

    with ExitStack() as st0:
        P = Prog(nc, st0)
        sb0 = lambda name, shape, dt=F32: st0.enter_context(nc.sbuf_tensor(name, list(shape), dt))
        ps = [st0.enter_context(nc.psum_tensor("ps%d" % i, [128, 512], F32)) for i in range(8)]
        UT = sb0("UT", [128, 2, L], BF16)
        c128 = sb0("c128s", [128, 4, 128])
        Rm = sb0("Rm", [128, 128], BF16)
        identb = sb0("identb", [128, 128], BF16)
        ones = sb0("ones", [128, 128], BF16)
        qkn = sb0("qkns", [128, 2])
        P.dma("sp", c128[:], c128_d, writes=["c128"])
        P.dma("sp", qkn[:], qkn_d, writes=["qkn"])
        P.op("dve", lambda e: e.tensor_copy(Rm[:], c128[:, 0, :]), reads=["c128"], writes=["Rm"])
        P.op("dve", lambda e: e.tensor_copy(identb[:], c128[:, 1, :]), reads=["c128"], writes=["identb"])
        P.op("dve", lambda e: e.memset(ones[:], 1.0), writes=["ones"])

        with ExitStack() as st1:
            sb1 = lambda name, shape, dt=F32: st1.enter_context(nc.sbuf_tensor(name, list(shape), dt))
            QT = sb1("QT", [128, 2, L], BF16)
            KT = sb1("KT", [128, L], BF16)
            Vt = sb1("Vt", [128, 32, 128], BF16)
            SG = sb1("SG", [128, 2, L], BF16)
            with ExitStack() as st:
                sb = lambda name, shape, dt=F32: st.enter_context(nc.sbuf_tensor(name, list(shape), dt))
                W1b = sb("W1b", [128, 16, 1024], BF16)
                wst = [sb("wst%d" % i, [128, 16, 128]) for i in range(1)]
                xblk = [sb("xblk%d" % i, [128, 16, 256]) for i in range(2)]
                sq = sb("sq", [128, 16, 256], BF16)
                xn = [sb("xn%d" % i, [128, 16, 512], BF16) for i in range(2)]
                nmix = sb("nmixs", [128, 16])
                rt = [sb("rt%d" % i, [128, 256]) for i in range(2)]
                rstd = [sb("rstd%d" % i, [128, 256]) for i in range(2)]
                qw = [sb("qw%d" % i, [128, 512], BF16) for i in range(3)]
                qsq = [sb("qsq%d" % i, [128, 512], BF16) for i in range(3)]
                rq = [sb("rq%d" % i, [128, 512]) for i in range(3)]
                t1 = [sb("t1%d" % i, [128, 512]) for i in range(3)]
                t2 = [sb("t2s", [128, 512])] * 3
                csb = [sb("csbs", [128, 2, 512])] * 2
                vT = sb("vT", [128, 512], BF16)
                P.dma("sp", nmix[:], nmix_d, writes=["nmix"])
                for pc in range(8):
                    P.dma("sp", wst[0][:], w1[pc], writes=[("wst", 0)])
                    for kc in range(16):
                        if kc % 2 == 0:
                            P.op("dve", lambda e, kc=kc, pc=pc: e.tensor_scalar(
                                W1b[:, kc, pc * 128:(pc + 1) * 128], wst[0][:, kc, :], nmix[:, kc:kc + 1], None, ALU.mult),
                                reads=[("wst", 0), "nmix"], writes=[("W1b", pc, kc)])
                        else:
                            P.op("act", lambda e, kc=kc, pc=pc: e.activation(
                                out=W1b[:, kc, pc * 128:(pc + 1) * 128], in_=wst[0][:, kc, :], func=AF.Copy, scale=nmix[:, kc:kc + 1]),
                                reads=[("wst", 0), "nmix"], writes=[("W1b", pc, kc)])
                ctiles = [(0, "q", 0), (128, "q", 1), (256, "k", 2), (384, "v", 0), (512, "g", 0), (640, "g", 1), (768, "u", 0), (896, "u", 1)]
                sbank = [2, 6]

                def prep(tb):
                    s = tb % 2
                    pr = (tb // 2) % 2
                    hf = tb % 2
                    tok = slice(tb * 256, (tb + 1) * 256)
                    P.dma("sp", xblk[s][:, 0:8, :], xT[tb, :, 0:8, :], writes=[("xblk", s, 0)])
                    P.dma("sp", xblk[s][:, 8:16, :], xT[tb, :, 8:16, :], writes=[("xblk", s, 1)])
                    P.op("act", lambda e: e.activation(out=sq[:], in_=xblk[s][:], func=AF.Square),
                         reads=[("xblk", s, 0), ("xblk", s, 1)], writes=["sq"])
                    bk = ps[sbank[s]]
                    for kc in range(16):
                        P.op("pe", lambda e, kc=kc: e.matmul(bk[:, 0:256], lhsT=ones[:], rhs=sq[:, kc, :], start=(kc == 0), stop=(kc == 15)),
                             reads=["sq", "ones"], writes=[("ps", sbank[s])])
                    P.op("act", lambda e: e.activation(out=rt[s][:], in_=bk[:, 0:256], func=AF.Sqrt, scale=1.0 / 2048, bias=EPS),
                         reads=[("ps", sbank[s])], writes=[("rt", s)])
                    P.op("dve", lambda e: e.reciprocal(rstd[s][:], rt[s][:]), reads=[("rt", s)], writes=[("rstd", s)])
                    P.op("dve", lambda e: e.tensor_tensor(xn[pr][:, :, hf * 256:(hf + 1) * 256], xblk[s][:], rstd[s][:].unsqueeze(1).to_broadcast([128, 16, 256]), ALU.mult),
                         reads=[("xblk", s, 0), ("xblk", s, 1), ("rstd", s)], writes=[("xn", pr, hf)])

                def qk_post(pp, kind, idx, j):
                    pr = pp % 2
                    tok = slice(pp * 512, (pp + 1) * 512)
                    b1, b2 = [(3, 4), (7, 3), (4, 7)][j]
                    bq, br = ps[b1], ps[b2]
                    P.op("pe", lambda e: e.matmul(bq[:, :], lhsT=ones[:], rhs=qsq[j][:], start=True, stop=True),
                         reads=[("qsq", j), "ones"], writes=[("ps", b1)])
                    P.op("pe", lambda e: e.matmul(br[:, :], lhsT=Rm[:], rhs=qw[j][:], start=True, stop=True),
                         reads=[("qw", j), "Rm"], writes=[("ps", b2)])
                    P.op("act", lambda e: e.activation(out=rq[j][:], in_=bq[:, :], func=AF.Sqrt, scale=1.0 / 128, bias=EPS),
                         reads=[("ps", b1)], writes=[("rq", j)])
                    P.op("dve", lambda e: e.tensor_tensor(t1[j][:], qw[j][:], csb[pr][:, 0, :], ALU.mult),
                         reads=[("qw", j)] + [("csb", 0, 0, hf) for hf in range(2)], writes=[("t1", j)])
                    P.op("dve", lambda e: e.tensor_tensor(t2[j][:], br[:, :], csb[pr][:, 1, :], ALU.mult),
                         reads=[("ps", b2)] + [("csb", 0, 1, hf) for hf in range(2)], writes=[("t2", 0)])
                    P.op("dve", lambda e: e.tensor_tensor(t1[j][:], t1[j][:], t2[j][:], ALU.add), reads=[("t1", j), ("t2", 0)], writes=[("t1", j)])
                    dst = QT[:, idx, tok] if kind == "q" else KT[:, tok]
                    P.op("dve", lambda e: e.reciprocal(rq[j][:], rq[j][:]), reads=[("rq", j)], writes=[("rq", j)])
                    P.op("dve", lambda e: e.tensor_tensor(dst, t1[j][:], rq[j][:], ALU.mult),
                         reads=[("t1", j), ("rq", j)], writes=[("QK", kind, idx, pp)])

                def main(pp):
                    pr = pp % 2
                    tok = slice(pp * 512, (pp + 1) * 512)
                    pending = None
                    for hf in range(2):
                        P.dma("sp", csb[0][:, :, hf * 256:(hf + 1) * 256], cs_d[2 * pp + hf], writes=[("csb", 0, 0, hf), ("csb", 0, 1, hf)])
                    for ci, (c0, kind, idx) in enumerate(ctiles):
                        pb = ps[ci % 2]
                        pk = ("ps", ci % 2)
                        for kc in range(16):
                            P.op("pe", lambda e, pb=pb, kc=kc, c0=c0: e.matmul(pb[:, :], lhsT=W1b[:, kc, c0:c0 + 128], rhs=xn[pr][:, kc, :],
                                                                         start=(kc == 0), stop=(kc == 15)),
                                 reads=[("xn", pr, 0), ("xn", pr, 1)] + ([("W1b", c0 // 128, kc)] if pp == 0 else []), writes=[pk])
                        if pending is not None:
                            qk_post(*pending)
                            pending = None
                        if kind in ("q", "k"):
                            j = idx
                            col = 0 if kind == "q" else 1
                            P.op("act", lambda e, pb=pb, j=j, col=col: e.activation(out=qw[j][:], in_=pb[:, :], func=AF.Copy, scale=qkn[:, col:col + 1]),
                                 reads=[pk, "qkn"], writes=[("qw", j)])
                            P.op("act", lambda e, pb=pb, j=j: e.activation(out=qsq[j][:], in_=pb[:, :], func=AF.Square),
                                 reads=[pk], writes=[("qsq", j)])
                            pending = (pp, kind, idx, j)
                        elif kind == "v":
                            P.op("act", lambda e, pb=pb: e.activation(out=vT[:], in_=pb[:, :], func=AF.Copy), reads=[pk], writes=["vT"])
                            for q4 in range(4):
                                P.op("pe", lambda e, q4=q4: e.matmul(ps[5][:, q4 * 128:(q4 + 1) * 128], lhsT=vT[:, q4 * 128:(q4 + 1) * 128], rhs=identb[:], start=True, stop=True),
                                     reads=["vT", "identb"], writes=[("ps", 5)])
                            P.op("dve", lambda e: e.tensor_copy(Vt[:, pp * 4:pp * 4 + 4, :], ps[5][:, :].rearrange("p (a b) -> p a b", a=4)),
                                 reads=[("ps", 5)], writes=[("Vt", pp)])
                        elif kind == "g":
                            P.op("act", lambda e, pb=pb, idx=idx: e.activation(out=SG[:, idx, tok], in_=pb[:, :], func=AF.Silu),
                                 reads=[pk], writes=[("SG", idx, pp)])
                        else:
                            dstv = UT[:, idx, :].rearrange("p (s c) -> p s c", s=8)[:, :, pp * 64:(pp + 1) * 64]
                            srcv = pb[:, :].rearrange("p (c s) -> p s c", s=8)
                            P.op("dve", lambda e, dstv=dstv, srcv=srcv: e.tensor_copy(dstv, srcv), reads=[pk], writes=[("UT", idx, pp)])

                prep(0)
                prep(1)
                for pp in range(8):
                    if pp + 1 < 8:
                        prep(2 * pp + 2)
                        prep(2 * pp + 3)
                    main(pp)
                P.barrier()
                P.emit()

            stB = st1
            sbB = lambda name, shape, dt=F32: stB.enter_context(nc.sbuf_tensor(name, list(shape), dt))
            pT = [sbB("pT%d" % i, [128, 512], BF16) for i in range(4)]
            rden = sbB("rden", [128, 512])
            ot = sbB("ot", [128, 512])
            YA = [sbB("YA%d" % i, [128, 8, 64]) for i in range(2)]
            acnt = [0]

            ait = [0]
            scale_att = 128 ** -0.5

            def a_qk(i):
                blk, kt = divmod(i, 32)
                h, qb = blk // 8, blk % 8
                sp_ = ps[i % 2]
                P.op("pe", lambda e: e.matmul(sp_[:, :], lhsT=KT[:, kt * 128:(kt + 1) * 128], rhs=QT[:, h, qb * 512:(qb + 1) * 512], start=True, stop=True),
                     reads=[], writes=[("ps", i % 2)])
                P.op("act", lambda e: e.activation(out=pT[i % 4][:], in_=sp_[:, :], func=AF.Exp, scale=scale_att),
                     reads=[("ps", i % 2)], writes=[("pT", i % 4)])

            def a_pv(i):
                blk, kt = divmod(i, 32)
                P.op("pe", lambda e: e.matmul(ps[4][:, :], lhsT=Vt[:, kt, :], rhs=pT[i % 4][:], start=(kt == 0), stop=(kt == 31)),
                     reads=[("pT", i % 4)], writes=[("ps", 4)])
                P.op("pe", lambda e: e.matmul(ps[5][:, :], lhsT=ones[:], rhs=pT[i % 4][:], start=(kt == 0), stop=(kt == 31)),
                     reads=[("pT", i % 4)], writes=[("ps", 5)])

            def attn_norm(blk):
                h, qb = blk // 8, blk % 8
                bo, bd = 4, 5
                qs = slice(qb * 512, (qb + 1) * 512)
                P.op("act", lambda e: e.activation(out=rden[:], in_=ps[bd][:, :], func=AF.Ln), reads=[("ps", bd)], writes=["lnd"])
                P.op("act", lambda e: e.activation(out=rden[:], in_=rden[:], func=AF.Exp, scale=-1.0), reads=["lnd"], writes=["rden"])
                P.op("dve", lambda e: e.tensor_tensor(ot[:], ps[bo][:, :], rden[:], ALU.mult), reads=[("ps", bo), "rden"], writes=["ot"])
                ya = YA[blk % 2]
                P.op("dve", lambda e: e.tensor_tensor(ya[:], ot[:].rearrange("p (c s) -> p s c", s=8),
                                                     SG[:, h, qs].rearrange("p (c s) -> p s c", s=8), ALU.mult),
                     reads=["ot"], writes=[("YA", blk % 2)])
                P.dma("sp", EX[h * 128:(h + 1) * 128, :].rearrange("p (s c) -> p s c", s=8)[:, :, qb * 64:(qb + 1) * 64], ya[:], reads=[("YA", blk % 2)])

            def attn_step(n=8):
                if 'B' not in STAGES:
                    return
                for _ in range(n):
                    i = ait[0]
                    if i >= 512:
                        return
                    ait[0] += 1
                    blk, kt = divmod(i, 32)
                    if i == 0:
                        a_qk(0)
                    if i + 1 < 512:
                        a_qk(i + 1)
                    if kt == 0 and blk >= 1:
                        attn_norm(blk - 1)
                    a_pv(i)

            def attn_flush():
                if 'B' not in STAGES:
                    return
                attn_step(512)
                attn_norm(15)

            with ExitStack() as st:
              if 'C' in STAGES:
                  sb = lambda name, shape, dt=F32: st.enter_context(nc.sbuf_tensor(name, list(shape), dt))
                  spm = sb("spm", [128, 6, 16])
                  P.dma("sp", spm[:], sp_d, writes=["spm"])
                  rho = sb("rho", [128, 16]); ur = sb("ur", [128, 16]); ui = sb("ui", [128, 16])
                  WSr = sb("WSr", [128, 16, 128], BF16); WSi = sb("WSi", [128, 16, 128], BF16); M0 = sb("M0", [128, 16, 128], BF16)
                  E4 = [128, 16, 8, 16]
                  Cfr = sb("Cfr", E4, BF16); Cfi = sb("Cfi", E4, BF16); Cbr = sb("Cbr", E4, BF16); Cbi = sb("Cbi", E4, BF16)
                  fl = lambda a, g: a[:, g, :, :].rearrange("p s h -> p (s h)")

                  def dv(fn, reads, writes, eng="dve"):
                      P.op(eng, fn, reads=reads, writes=writes)

                  def tt(out, a, b, op, reads, writes):
                      dv(lambda e: e.tensor_tensor(out, a, b, op), reads, writes)

                  def cmul(or_, oi_, ar, ai, br, bi, reads, okr, oki, ta, tb, tk):
                      ka, kb = tk + "a", tk + "b"
                      tt(ta, ar, br, ALU.mult, reads, [ka])
                      tt(tb, ai, bi, ALU.mult, reads, [kb])
                      tt(or_, ta, tb, ALU.subtract, [ka, kb], [okr])
                      tt(ta, ar, bi, ALU.mult, reads + [okr], [ka])
                      tt(tb, ai, br, ALU.mult, reads + [okr], [kb])
                      tt(oi_, ta, tb, ALU.add, [ka, kb], [oki])

                  with ExitStack() as stg:
                      cnt = [0]

                      def tmp(shape, dt=F32):
                          cnt[0] += 1
                          return stg.enter_context(nc.sbuf_tensor("tmp%d" % cnt[0], list(shape), dt))

                      bcm = tmp([128, 4, 16, 16])
                      P.dma("sp", bcm[:], bc_d, writes=["bcm"])
                      are, aim, ldt, dtile = spm[:, 0, :], spm[:, 1, :], spm[:, 2, :], spm[:, 3, :]
                      mf, mb = spm[:, 4, 0:1], spm[:, 5, 0:1]
                      S = [128, 16]
                      lre = tmp(S); dt_ = tmp(S); al = tmp(S); th = tmp(S); mag = tmp(S)
                      sn = tmp(S); cs = tmp(S); a1 = tmp(S); a2 = tmp(S); a3 = tmp(S)
                      dv(lambda e: e.tensor_scalar(lre[:], are, -1e-4, None, ALU.min), ["spm"], ["lre"])
                      dv(lambda e: e.activation(out=dt_[:], in_=ldt, func=AF.Exp), ["spm"], ["dt"], "act")
                      tt(al[:], lre[:], dt_[:], ALU.mult, ["lre", "dt"], ["al"])
                      tt(th[:], aim, dt_[:], ALU.mult, ["spm", "dt"], ["th"])
                      dv(lambda e: e.activation(out=mag[:], in_=al[:], func=AF.Exp), ["al"], ["mag"], "act")
                      halfpi = tmp([128, 1])
                      dv(lambda e: e.memset(halfpi[:], math.pi / 2), [], ["halfpi"])
                      dv(lambda e: e.activation(out=sn[:], in_=th[:], func=AF.Sin, scale=1.0 / 32), ["th"], ["sn"], "act")
                      dv(lambda e: e.activation(out=cs[:], in_=th[:], func=AF.Sin, scale=1.0 / 32, bias=halfpi[:]), ["th", "halfpi"], ["cs"], "act")
                      for it in range(5):
                          tt(a1[:], sn[:], cs[:], ALU.mult, ["sn", "cs"], ["a1"])
                          tt(a2[:], cs[:], cs[:], ALU.mult, ["cs"], ["a2"])
                          tt(a3[:], sn[:], sn[:], ALU.mult, ["sn"], ["a3"])
                          tt(cs[:], a2[:], a3[:], ALU.subtract, ["a2", "a3"], ["cs"])
                          dv(lambda e: e.tensor_scalar(sn[:], a1[:], 2.0, None, ALU.mult), ["a1"], ["sn"])
                      attn_step()
                      lbr = tmp(S); lbi = tmp(S)
                      tt(lbr[:], mag[:], cs[:], ALU.mult, ["mag", "cs"], ["lbr"])
                      tt(lbi[:], mag[:], sn[:], ALU.mult, ["mag", "sn"], ["lbi"])
                      nr = tmp(S); den = tmp(S); bfr = tmp(S); bfi = tmp(S)
                      dv(lambda e: e.tensor_scalar(nr[:], lbr[:], -1.0, None, ALU.add), ["lbr"], ["nr"])
                      tt(a1[:], lre[:], lre[:], ALU.mult, ["lre"], ["a1"])
                      tt(a2[:], aim, aim, ALU.mult, ["spm"], ["a2"])
                      tt(den[:], a1[:], a2[:], ALU.add, ["a1", "a2"], ["den"])
                      dv(lambda e: e.reciprocal(den[:], den[:]), ["den"], ["den"])
                      tt(a1[:], nr[:], lre[:], ALU.mult, ["nr", "lre"], ["a1"])
                      tt(a2[:], lbi[:], aim, ALU.mult, ["lbi", "spm"], ["a2"])
                      tt(a3[:], a1[:], a2[:], ALU.add, ["a1", "a2"], ["a3"])
                      tt(bfr[:], a3[:], den[:], ALU.mult, ["a3", "den"], ["bfr"])
                      tt(a1[:], lbi[:], lre[:], ALU.mult, ["lbi", "lre"], ["a1"])
                      tt(a2[:], nr[:], aim, ALU.mult, ["nr", "spm"], ["a2"])
                      tt(a3[:], a1[:], a2[:], ALU.subtract, ["a1", "a2"], ["a3"])
                      tt(bfi[:], a3[:], den[:], ALU.mult, ["a3", "den"], ["bfi"])
                      Pwr = tmp([128, 16, 9]); Pwi = tmp([128, 16, 9])
                      dv(lambda e: e.memset(Pwr[:, :, 0], 1.0), [], ["Pw"])
                      dv(lambda e: e.memset(Pwi[:, :, 0], 0.0), ["Pw"], ["Pw"])
                      for k in range(1, 9):
                          cmul(Pwr[:, :, k], Pwi[:, :, k], Pwr[:, :, k - 1], Pwi[:, :, k - 1], lbr[:], lbi[:], ["lbr", "lbi", "Pw"], "Pw", "Pw", a1[:], a2[:], "a12")
                      rinv = tmp(S); i8r = tmp(S); i8i = tmp(S)
                      dv(lambda e: e.activation(out=rho[:], in_=al[:], func=AF.Exp, scale=8.0), ["al"], ["rho"], "act")
                      dv(lambda e: e.reciprocal(rinv[:], rho[:]), ["rho"], ["rinv"])
                      tt(ur[:], Pwr[:, :, 8], rinv[:], ALU.mult, ["Pw", "rinv"], ["ur"])
                      tt(ui[:], Pwi[:, :, 8], rinv[:], ALU.mult, ["Pw", "rinv"], ["ui"])
                      tt(i8r[:], ur[:], rinv[:], ALU.mult, ["ur", "rinv"], ["i8r"])
                      tt(a3[:], ui[:], rinv[:], ALU.mult, ["ui", "rinv"], ["a3"])
                      dv(lambda e: e.tensor_scalar(i8i[:], a3[:], -1.0, None, ALU.mult), ["a3"], ["i8i"])
                      PAr = tmp([128, 16, 8]); PAi = tmp([128, 16, 8]); PCr = tmp([128, 16, 8]); PCi = tmp([128, 16, 8])
                      for (dst, src) in ((PAr, Pwr), (PAi, Pwi)):
                          for k in range(8):
                              dv(lambda e, dst=dst, src=src, k=k: e.tensor_copy(dst[0:64, :, k:k + 1], src[0:64, :, 7 - k:8 - k]), ["Pw", "PA"], ["PA"])
                          dv(lambda e, dst=dst, src=src: e.tensor_copy(dst[64:128, :, :], src[64:128, :, 0:8]), ["Pw", "PA"], ["PA"])
                      for (dst, src) in ((PCr, Pwr), (PCi, Pwi)):
                          dv(lambda e, dst=dst, src=src: e.tensor_copy(dst[0:64, :, :], src[0:64, :, 1:9]), ["Pw", "PC"], ["PC"])
                          for k in range(8):
                              dv(lambda e, dst=dst, src=src, k=k: e.tensor_copy(dst[64:128, :, k:k + 1], src[64:128, :, 8 - k:9 - k]), ["Pw", "PC"], ["PC"])
                      PA2r = tmp([128, 16, 8]); PA2i = tmp([128, 16, 8]); t8a = tmp([128, 16, 8]); t8b = tmp([128, 16, 8])
                      bc8 = lambda a: a.unsqueeze(2).to_broadcast([128, 16, 8])
                      cmul(PA2r[:], PA2i[:], PAr[:], PAi[:], bc8(i8r[:]), bc8(i8i[:]), ["PA", "i8r", "i8i"], "PA2r", "PA2i", t8a[:], t8b[:], "t8")
                      Bbr = tmp([128, 16, 16]); Bbi = tmp([128, 16, 16]); t16a = tmp([128, 16, 16]); t16b = tmp([128, 16, 16])
                      bc16 = lambda a: a.unsqueeze(2).to_broadcast([128, 16, 16])
                      cmul(Bbr[:], Bbi[:], bcm[:, 0, :, :], bcm[:, 1, :, :], bc16(bfr[:]), bc16(bfi[:]), ["bcm", "bfr", "bfi"], "Bbr", "Bbi", t16a[:], t16b[:], "t16")
                      exa = tmp(E4); exb = tmp(E4); Er_ = tmp(E4); Ei_ = tmp(E4)

                      def cexp(pr, pi, mr, mi, reads):
                          for s_ in range(8):
                              b3 = lambda a, s_=s_: a[:, :, s_:s_ + 1].to_broadcast([128, 16, 16])
                              cmul(Er_[:, :, s_, :], Ei_[:, :, s_, :], b3(pr), b3(pi), mr, mi, reads + ["Er", "Ei"], "Er", "Ei", exa[:, :, s_, :], exb[:, :, s_, :], "ex")
                              if s_ % 2 == 1:
                                  attn_step()
                      Ar = tmp(E4, BF16); Ai = tmp(E4, BF16)
                      A2fr = tmp(E4, BF16); A2fi = tmp(E4, BF16); A2br = tmp(E4, BF16); A2bi = tmp(E4, BF16)
                      Ccr = tmp(E4, BF16); nCci = tmp(E4, BF16)
                      attn_step()
                      cexp(PAr, PAi, Bbr[:], Bbi[:], ["PA", "Bbr", "Bbi"])
                      attn_step()
                      dv(lambda e: e.tensor_copy(Ar[:], Er_[:]), ["Er"], ["Ar"])
                      dv(lambda e: e.tensor_copy(Ai[:], Ei_[:]), ["Ei"], ["Ai"])
                      cexp(PA2r, PA2i, Bbr[:], Bbi[:], ["PA2r", "PA2i", "Bbr", "Bbi", "Ar", "Ai"])
                      attn_step()
                      dv(lambda e: e.tensor_scalar(A2fr[:], Er_[:], mf, None, ALU.mult), ["Er", "spm"], ["A2fr"])
                      dv(lambda e: e.tensor_scalar(A2br[:], Er_[:], mb, None, ALU.mult), ["Er", "spm"], ["A2br"])
                      dv(lambda e: e.tensor_scalar(A2fi[:], Ei_[:], mf, None, ALU.mult), ["Ei", "spm"], ["A2fi"])
                      dv(lambda e: e.tensor_scalar(A2bi[:], Ei_[:], mb, None, ALU.mult), ["Ei", "spm"], ["A2bi"])
                      cexp(PCr, PCi, bcm[:, 2, :, :], bcm[:, 3, :, :], ["PC", "bcm", "A2fr", "A2br", "A2fi", "A2bi"])
                      attn_step()
                      dv(lambda e: e.tensor_copy(Ccr[:], Er_[:]), ["Er"], ["Ccr"])
                      dv(lambda e: e.tensor_scalar(nCci[:], Ei_[:], -1.0, None, ALU.mult), ["Ei"], ["nCci"])
                      dv(lambda e: e.tensor_scalar(Cfr[:], Er_[:], mf, None, ALU.mult), ["Er", "spm"], ["Cfr"])
                      dv(lambda e: e.tensor_scalar(Cbr[:], Er_[:], mb, None, ALU.mult), ["Er", "spm"], ["Cbr"])
                      dv(lambda e: e.tensor_scalar(Cfi[:], nCci[:], mf, None, ALU.mult), ["nCci", "spm"], ["Cfi"])
                      dv(lambda e: e.tensor_scalar(Cbi[:], nCci[:], mb, None, ALU.mult), ["nCci", "spm"], ["Cbi"])
                      mt1 = tmp([128, 128]); mt2 = tmp([128, 128])
                      for g in range(16):
                          attn_step()
                          P.op("pe", lambda e, g=g: e.matmul(ps[2][:, 0:128], lhsT=fl(Ar, g), rhs=identb[:], start=True, stop=True), reads=["Ar", "identb"], writes=[("ps", 2)])
                          P.op("pe", lambda e, g=g: e.matmul(ps[3][:, 0:128], lhsT=fl(Ai, g), rhs=identb[:], start=True, stop=True), reads=["Ai", "identb"], writes=[("ps", 3)])
                          P.op("act", lambda e, g=g: e.activation(out=WSr[:, g, :], in_=ps[2][:, 0:128], func=AF.Copy), reads=[("ps", 2)], writes=[("WSr", g)])
                          P.op("act", lambda e, g=g: e.activation(out=WSi[:, g, :], in_=ps[3][:, 0:128], func=AF.Copy), reads=[("ps", 3)], writes=[("WSi", g)])
                          P.op("pe", lambda e, g=g: e.matmul(ps[2][:, 0:128], lhsT=fl(A2fr, g), rhs=fl(Ccr, g), start=True, stop=False), reads=["A2fr", "Ccr"], writes=[("ps", 2)])
                          P.op("pe", lambda e, g=g: e.matmul(ps[2][:, 0:128], lhsT=fl(A2fi, g), rhs=fl(nCci, g), start=False, stop=True), reads=["A2fi", "nCci"], writes=[("ps", 2)])
                          P.op("pe", lambda e, g=g: e.matmul(ps[3][:, 0:128], lhsT=fl(A2br, g), rhs=fl(Ccr, g), start=True, stop=False), reads=["A2br", "Ccr"], writes=[("ps", 3)])
                          P.op("pe", lambda e, g=g: e.matmul(ps[3][:, 0:128], lhsT=fl(A2bi, g), rhs=fl(nCci, g), start=False, stop=True), reads=["A2bi", "nCci"], writes=[("ps", 3)])
                          tt(mt1[:], ps[2][:, 0:128], c128[:, 2, :], ALU.mult, [("ps", 2), "c128"], ["mt1"])
                          tt(mt2[:], ps[3][:, 0:128], c128[:, 3, :], ALU.mult, [("ps", 3), "c128"], ["mt2"])
                          tt(mt1[:], mt1[:], mt2[:], ALU.add, ["mt1", "mt2"], ["mt1"])
                          dv(lambda e, g=g: e.scalar_tensor_tensor(M0[:, g, :], c128[:, 1, :], dtile[:, g:g + 1], mt1[:], ALU.mult, ALU.add),
                             ["mt1", "c128", "spm"], [("M0", g)])
                      P.barrier()
                      P.emit()
                  if 'D' in STAGES:
                    U = sb("U", [128, 16, NCH], BF16)
                    for j in range(2):
                        for g8 in range(8):
                            for s in range(8):
                                P.dma("sp" if (g8 + s) % 2 == 0 else "pool", U[s * 16:(s + 1) * 16, j * 8 + g8, :],
                                      UT[g8 * 16:(g8 + 1) * 16, j, s * NCH:(s + 1) * NCH], writes=[("U", j * 8 + g8, s)])
                    Ukeys = lambda g: [("U", g, s) for s in range(8)]
                    Xr = sb("Xr", [128, 8, NCH + 2], BF16)
                    Xi = sb("Xi", [128, 8, NCH + 2], BF16)
                    YG = [sb("YG%d" % i, [128, NCH]) for i in range(2)]
                    Tr = sb("Tr", [128, 8, NCH]); Ti = sb("Ti", [128, 8, NCH])
                    Sr = sb("Sr", [128, 8, NCH], BF16); Si = sb("Si", [128, 8, NCH], BF16)
                    ta = sb("ta", [128, 8, 256]); tb_ = sb("tbb", [128, 8, 256])
                    ta2 = ta[:, 0:2, :].rearrange("p a b -> p (a b)"); tb2 = tb_[:, 0:2, :].rearrange("p a b -> p (a b)")
                    sgnv = sb("sgnv", [128, 1]); u2r = sb("u2r", [128, 8]); u2i = sb("u2i", [128, 8]); u3r = sb("u3r", [128, 8]); u3i = sb("u3i", [128, 8])
                    for h in range(2):
                        G8 = range(h * 8, h * 8 + 8)
                        gs = slice(h * 8, h * 8 + 8)
                        dv(lambda e: e.memset(Xr[:], 0.0), [("Xr", gl) for gl in range(8)], [("Xr", gl) for gl in range(8)])
                        dv(lambda e: e.memset(Xi[:], 0.0), [("Xi", gl) for gl in range(8)], [("Xi", gl) for gl in range(8)])
                        dv(lambda e: e.memset(Tr[:, :, 0:1], 1.0), ["T"], ["T"])
                        dv(lambda e: e.memset(Ti[:, :, 0:1], 0.0), ["T"], ["T"])
                        if h == 0:
                            tt(sgnv[:], spm[:, 4, 0:1], spm[:, 5, 0:1], ALU.subtract, ["spm"], ["sgnv"])
                        dv(lambda e, gs=gs: e.tensor_copy(u2r[:], ur[:, gs]), ["ur", "u2"], ["u2"])
                        dv(lambda e, gs=gs: e.tensor_scalar(u2i[:], ui[:, gs], sgnv[:, 0:1], None, ALU.mult), ["ui", "u2", "sgnv"], ["u2"])
                        m = 1
                        while m < NCH:
                            bq = lambda a, m=m: a[:, :].unsqueeze(2).to_broadcast([128, 8, m])
                            cmul(Tr[:, :, m:2 * m], Ti[:, :, m:2 * m], Tr[:, :, 0:m], Ti[:, :, 0:m], bq(u2r), bq(u2i), ["T", "u2"], "T", "T",
                                 ta[:, :, 0:m], tb_[:, :, 0:m], "tab")
                            cmul(u3r[:], u3i[:], u2r[:], u2i[:], u2r[:], u2i[:], ["u2", "T"], "u3", "u3", ta[:, :, 0], tb_[:, :, 0], "tab")
                            dv(lambda e: e.tensor_copy(u2r[:], u3r[:]), ["u3", "T"], ["u2"])
                            dv(lambda e: e.tensor_copy(u2i[:], u3i[:]), ["u3", "T", "u2"], ["u2"])
                            attn_step()
                            m *= 2
                        pa2 = ta[:, 2:4, :].rearrange("p a b -> p (a b)"); pb2 = tb_[:, 2:4, :].rearrange("p a b -> p (a b)")

                        def tp(out, a, b, op, reads, writes):
                            P.op("pool", lambda e: e.tensor_tensor(out, a, b, op), reads=reads, writes=writes)

                        def s_mm(g):
                            b_re, b_im = (2, 3) if g % 2 == 0 else (6, 7)
                            P.op("pe", lambda e: e.matmul(ps[b_re][:, :], lhsT=WSr[:, g, :], rhs=U[:, g, :], start=True, stop=True),
                                 reads=Ukeys(g), writes=[("ps", b_re)])
                            P.op("pe", lambda e: e.matmul(ps[b_im][:, :], lhsT=WSi[:, g, :], rhs=U[:, g, :], start=True, stop=True),
                                 reads=Ukeys(g), writes=[("ps", b_im)])

                        if h == 0:
                            s_mm(0)
                        for g in G8:
                            gl = g - h * 8
                            b_re, b_im = (2, 3) if g % 2 == 0 else (6, 7)
                            tt(ta2, Tr[:, gl, :], ps[b_re][:, :], ALU.mult, ["T", ("ps", b_re)], ["ta2"])
                            tt(tb2, Ti[:, gl, :], ps[b_im][:, :], ALU.mult, ["T", ("ps", b_im)], ["tb2"])
                            tt(Sr[:, gl, :], ta2, tb2, ALU.add, ["ta2", "tb2"], [("Sr", gl)])
                            tt(ta2, Tr[:, gl, :], ps[b_im][:, :], ALU.mult, ["T", ("ps", b_im)], ["ta2"])
                            tt(tb2, Ti[:, gl, :], ps[b_re][:, :], ALU.mult, ["T", ("ps", b_re)], ["tb2"])
                            tt(Si[:, gl, :], ta2, tb2, ALU.subtract, ["ta2", "tb2"], [("Si", gl)])
                            for (Sx, nm_) in ((Sr, "Sr"), (Si, "Si")):
                                dv(lambda e, Sx=Sx, g=g, gl=gl: e.tensor_tensor_scan(Sx[0:64, gl, :], rho[0:64, g:g + 1].to_broadcast([64, NCH]), Sx[0:64, gl, :], 0.0, ALU.mult, ALU.add),
                                   [(nm_, gl), "rho"], [(nm_, gl)])
                                dv(lambda e, Sx=Sx, g=g, gl=gl: e.tensor_tensor_scan(Sx[64:128, gl, ::-1], rho[64:128, g:g + 1].to_broadcast([64, NCH]), Sx[64:128, gl, ::-1], 0.0, ALU.mult, ALU.add),
                                   [(nm_, gl), "rho"], [(nm_, gl)])
                            tp(pa2, Tr[:, gl, :], Sr[:, gl, :], ALU.mult, ["T", ("Sr", gl)], ["pa2"])
                            tp(pb2, Ti[:, gl, :], Si[:, gl, :], ALU.mult, ["T", ("Si", gl)], ["pb2"])
                            tp(Xr[:, gl, 1:NCH + 1], pa2, pb2, ALU.subtract, ["pa2", "pb2"], [("Xr", gl)])
                            tp(pa2, Tr[:, gl, :], Si[:, gl, :], ALU.mult, ["T", ("Si", gl)], ["pa2"])
                            tp(pb2, Ti[:, gl, :], Sr[:, gl, :], ALU.mult, ["T", ("Sr", gl)], ["pb2"])
                            tp(Xi[:, gl, 1:NCH + 1], pa2, pb2, ALU.add, ["pa2", "pb2"], [("Xi", gl)])
                            if g + 1 < 16:
                                s_mm(g + 1)
                            attn_step()
                            pk = ("ps", b_re)
                            mm = [(M0[:, g, :], U[:, g, :], Ukeys(g)),
                                  (fl(Cfr, g), Xr[:, gl, 0:NCH], [("Xr", gl)]),
                                  (fl(Cfi, g), Xi[:, gl, 0:NCH], [("Xi", gl)]),
                                  (fl(Cbr, g), Xr[:, gl, 2:NCH + 2], [("Xr", gl)]),
                                  (fl(Cbi, g), Xi[:, gl, 2:NCH + 2], [("Xi", gl)])]
                            for i, (lt, rh, rk) in enumerate(mm):
                                P.op("pe", lambda e, lt=lt, rh=rh, i=i, b_re=b_re: e.matmul(ps[b_re][:, :], lhsT=lt, rhs=rh, start=(i == 0), stop=(i == 4)), reads=rk, writes=[pk])
                            yg = YG[g % 2]
                            P.op("act", lambda e, yg=yg, b_re=b_re: e.activation(out=yg[:], in_=ps[b_re][:, :], func=AF.Gelu_apprx_tanh), reads=[pk], writes=[("YG", g % 2)])
                            for t in range(8):
                                P.dma("sp" if t % 2 == 0 else "pool", EX[256 + g * 16:256 + (g + 1) * 16, t * NCH:(t + 1) * NCH], yg[t * 16:(t + 1) * 16, :], reads=[("YG", g % 2)])
                        if h == 1:
                            attn_flush()
                        if h == 0:
                            dv(lambda e: e.nop(), ["pa2", "pb2", "ta2", "tb2"], ["taba", "tabb"])
                    P.barrier()
                    P.emit()
    return nc


def build_phase2():
    nc = bass.Bass("TRN2", target_bir_lowering=False)
    mixin = _din(nc, "mixin", [128, 16, 1024])
    x2 = _din(nc, "x2", [2, 128, 16, 512])
    pTd = _din(nc, "pT", [128, 2, 1024])
    wgs = _din(nc, "wgs", [8, 128, 16, 128])
    wglu = _din(nc, "wglu", [16, 128, 8, 128])
    wout = _din(nc, "wout", [16, 128, 16, 128])
    wpg = _din(nc, "wpg", [16, 128, 16, 128])
    wpp = _din(nc, "wpp", [16, 128, 2, 128])
    vec_d = _din(nc, "vecs", [128, 4, 16])
    outT = nc.dram_tensor("outT", [128, 16, 1024], F32, kind="ExternalOutput").ap()
    with ExitStack() as st:
        P = Prog(nc, st)
        phase2_body(nc, P, st, dict(mixin=mixin, x2=x2, pTd=pTd, wgs=wgs, wglu=wglu, wout=wout, wpg=wpg, wpp=wpp, vec_d=vec_d, outT=outT), None, None)
    return nc


def phase2_body(nc, P, st, D, mixA, ysA):
    if True:
        sb = lambda name, shape, dt=F32: st.enter_context(nc.sbuf_tensor(name, list(shape), dt))
        ps = [st.enter_context(nc.psum_tensor("qs%d" % i, [128, 512], F32)) for i in range(8)]
        vecs = sb("vecss", [128, 4, 16])
        ones = sb("ones2", [128, 128], BF16)
        P.dma("sp", vecs[:], D["vec_d"], writes=["vecs"])
        P.op("dve", lambda e: e.memset(ones[:], 1.0), writes=["ones2"])
        NW = 4
        xn = sb("xn2", [128, 16, 1024], BF16)
        mix = sb("mix2", [128, 16, 1024], BF16)
        pTb = sb("pTb", [128, 2, 1024], BF16)
        rt = sb("rt2", [128, 512])
        rstd = sb("rstd2", [128, 1024])
        ga = sb("ga2", [128, 512]); sgb = sb("sgb2", [128, 512]); gsl = sb("gsl2", [128, 512])
        ot = [sb("ot2_0", [128, 512])] * 2
        xres = [sb("xres%d" % i, [128, 1024]) for i in range(2)]
        wcount = [0]
        casters = ["act", "dve"]
        WS = {}

        def load_w(src_tile, K):
            wst, wb = WS["wst"], WS["wb"]
            s = wcount[0] % len(wb)
            s2 = wcount[0] % len(wst)
            eng = casters[wcount[0] % len(casters)]
            wcount[0] += 1
            P.dma("sp", wst[s2][:, 0:K, :], src_tile, writes=[("wst", s2)])
            if eng == "act":
                P.op("act", lambda e: e.activation(out=wb[s][:, 0:K, :], in_=wst[s2][:, 0:K, :], func=AF.Copy), reads=[("wst", s2)], writes=[("wb", s)])
            else:
                P.op(eng, lambda e: e.tensor_copy(wb[s][:, 0:K, :], wst[s2][:, 0:K, :]), reads=[("wst", s2)], writes=[("wb", s)])
            return wb[s], ("wb", s)

        def rms_bcast(src_blk, key, sqt, blk):
            for hh in range(2):
                P.op("act", lambda e, hh=hh: e.activation(out=sqt[:], in_=src_blk[:, hh * 8:(hh + 1) * 8, :], func=AF.Square), reads=[key], writes=["sq"])
                for kc in range(8):
                    P.op("pe", lambda e, kc=kc, hh=hh: e.matmul(ps[7][:, :], lhsT=ones[:], rhs=sqt[:, kc, :], start=(hh == 0 and kc == 0), stop=(hh == 1 and kc == 7)),
                         reads=["sq", "ones2"], writes=[("ps", 7)])
            P.op("act", lambda e: e.activation(out=rt[:], in_=ps[7][:, :], func=AF.Sqrt, scale=1.0 / 2048, bias=EPS), reads=[("ps", 7)], writes=["rt"])
            P.op("dve", lambda e: e.reciprocal(rstd[:, blk * 512:(blk + 1) * 512], rt[:]), reads=["rt"], writes=[("rstd", blk)])

        def norm_to_bf16(dst_blk, src_blk, srckey, row, blk, dkey):
            for k in range(16):
                P.op("dve", lambda e, k=k: e.scalar_tensor_tensor(dst_blk[:, k, :], src_blk[:, k, :], vecs[:, row, k:k + 1], rstd[:, blk * 512:(blk + 1) * 512], ALU.mult, ALU.mult)
                     if True else None, reads=[srckey, ("rstd", blk), "vecs"], writes=[(dkey, blk, k)])

        sG = ExitStack()
        ys = sG.enter_context(nc.sbuf_tensor("ys2", [128, 8, 1024], BF16)) if ysA is None else ysA
        with ExitStack() as s1:
            sb1 = lambda name, shape, dt=F32: s1.enter_context(nc.sbuf_tensor(name, list(shape), dt))
            xb = sb1("xb2", [128, 16, 512])
            sq = sb1("sq2a", [128, 8, 512], BF16)
            mixf = sb1("mixf", [128, 8, 512])
            pTf = sb1("pTfs", [128, 2, 1024])
            P.dma("sp", pTf[:], D["pTd"], writes=["pTf"])
            P.op("dve", lambda e: e.tensor_copy(pTb[:], pTf[:]), reads=["pTf"], writes=["pTb"])
            for blk in range(2):
                tk = slice(blk * 512, (blk + 1) * 512)
                P.dma("sp", xb[:], D["x2"][blk], writes=["xb"])
                if mixA is None:
                    P.dma("sp", mixf[:], D["mixin"][:, 8:16, tk], writes=["mixf"])
                    P.op("act", lambda e, tk=tk: e.activation(out=ys[:, :, tk], in_=mixf[:], func=AF.Copy), reads=["mixf"], writes=[("ys", blk)])
                    P.dma("sp", mixf[:], D["mixin"][:, 0:8, tk], writes=["mixf"])
                    P.op("dve", lambda e, tk=tk: e.tensor_copy(mix[:, 0:8, tk], mixf[:]), reads=["mixf"], writes=[("mixa", blk)])
                rms_bcast(xb[:], "xb", sq, blk)
                for k in range(16):
                    P.op("dve", lambda e, k=k, tk=tk, blk=blk: e.scalar_tensor_tensor(xn[:, k, tk], xb[:, k, :], vecs[:, 0, k:k + 1], rstd[:, tk], ALU.mult, ALU.mult),
                         reads=["xb", ("rstd", blk), "vecs"], writes=[("xn", blk, k)])
            if mixA is not None:
                P.op("pool", lambda e: e.tensor_copy(mix[:, 0:8, :], mixA[:]), reads=[], writes=[("mixa", 0)])
            P.barrier()
            P.emit()
        WS["wst"] = [sG.enter_context(nc.sbuf_tensor("wstG%d" % i, [128, 16, 128], F32)) for i in range(6)]
        WS["wb"] = [sG.enter_context(nc.sbuf_tensor("wbG%d" % i, [128, 16, 128], BF16)) for i in range(6)]
        for m in range(8):
            wa, ka = load_w(D["wglu"][m], 8)
            wbb, kb = load_w(D["wglu"][8 + m], 8)
            wg, kg = load_w(D["wgs"][m], 16)
            for blk in range(2):
                tk = slice(blk * 512, (blk + 1) * 512)
                b0, b1, b2 = (0, 1, 2) if blk == 0 else (3, 4, 5)
                for k in range(8):
                    P.op("pe", lambda e, k=k, tk=tk, wa=wa, b0=b0: e.matmul(ps[b0][:, :], lhsT=wa[:, k, :], rhs=ys[:, k, tk], start=(k == 0), stop=(k == 7)), reads=[ka], writes=[("ps", b0)])
                for k in range(8):
                    P.op("pe", lambda e, k=k, tk=tk, wbb=wbb, b1=b1: e.matmul(ps[b1][:, :], lhsT=wbb[:, k, :], rhs=ys[:, k, tk], start=(k == 0), stop=(k == 7)), reads=[kb], writes=[("ps", b1)])
                for k in range(16):
                    P.op("pe", lambda e, k=k, tk=tk, wg=wg, b2=b2: e.matmul(ps[b2][:, :], lhsT=wg[:, k, :], rhs=xn[:, k, tk], start=(k == 0), stop=(k == 15)), reads=[kg], writes=[("ps", b2)])
                P.op("act", lambda e, m=m, b0=b0: e.activation(out=ga[:], in_=ps[b0][:, :], func=AF.Identity, bias=vecs[:, 3, m:m + 1]), reads=[("ps", b0), "vecs"], writes=["ga"])
                P.op("act", lambda e, m=m, b1=b1: e.activation(out=sgb[:], in_=ps[b1][:, :], func=AF.Sigmoid, bias=vecs[:, 3, 8 + m:9 + m]), reads=[("ps", b1), "vecs"], writes=["sgb"])
                P.op("act", lambda e, b2=b2: e.activation(out=gsl[:], in_=ps[b2][:, :], func=AF.Silu), reads=[("ps", b2)], writes=["gsl"])
                P.op("dve", lambda e: e.tensor_tensor(ga[:], ga[:], sgb[:], ALU.mult), reads=["ga", "sgb"], writes=["ga"])
                P.op("dve", lambda e, m=m, tk=tk: e.tensor_tensor(mix[:, 8 + m, tk], ga[:], gsl[:], ALU.mult), reads=["ga", "gsl"], writes=[("mixs", m, blk)])
        P.barrier()
        P.emit()
        sG.close()
        wcount[0] = 0
        H = sb("H2", [128, 16, 1024])
        sq = sb("sq2b", [128, 8, 512], BF16)
        WS["wst"] = [sb("wstO%d" % i, [128, 16, 128]) for i in range(4)]
        WS["wb"] = [sb("wbO%d" % i, [128, 16, 128], BF16) for i in range(3)]
        for m in range(16):
            wo, ko = load_w(D["wout"][m], 16)
            xr = xres[m % 2]
            P.dma("sp", xr[:].rearrange("p (b t) -> p b t", b=2), D["x2"][:, :, m, :].rearrange("b p t -> p b t"), writes=[("xres", m % 2)])
            for blk in range(2):
                tk = slice(blk * 512, (blk + 1) * 512)
                pb = ps[3 + (2 * m + blk) % 3]
                pk = ("ps", 3 + (2 * m + blk) % 3)
                for k in range(16):
                    P.op("pe", lambda e, k=k, tk=tk, pb=pb, wo=wo: e.matmul(pb[:, :], lhsT=wo[:, k, :], rhs=mix[:, k, tk], start=(k == 0), stop=(k == 15)),
                         reads=[ko] + ([("mixs", mm, blk) for mm in range(8)] if m == 0 else []), writes=[pk])
                P.op("dve", lambda e, m=m, tk=tk, pb=pb, xr=xr: e.tensor_tensor(H[:, m, tk], pb[:, :], xr[:, tk], ALU.add), reads=[pk, ("xres", m % 2)], writes=[("H", m, blk)])
        for blk in range(2):
            tk = slice(blk * 512, (blk + 1) * 512)
            P.op("dve", lambda e: e.nop(), reads=[("H", m, blk) for m in range(16)], writes=[("Hall", blk)])
            rms_bcast(H[:, :, tk], ("Hall", blk), sq, blk)
            for k in range(16):
                P.op("dve", lambda e, k=k, tk=tk: e.scalar_tensor_tensor(xn[:, k, tk], H[:, k, tk], vecs[:, 1, k:k + 1], rstd[:, tk], ALU.mult, ALU.mult),
                     reads=[("Hall", blk), ("rstd", blk), "vecs"], writes=[("hn", blk, k)])
        for m in range(16):
            wq, kq = load_w(D["wpg"][m], 16)
            wp_, kp = load_w(D["wpp"][m], 2)
            for blk in range(2):
                tk = slice(blk * 512, (blk + 1) * 512)
                pb = ps[(2 * m + blk) % 2]
                pk = ("ps", (2 * m + blk) % 2)
                for k in range(16):
                    P.op("pe", lambda e, k=k, tk=tk, pb=pb, wq=wq: e.matmul(pb[:, :], lhsT=wq[:, k, :], rhs=xn[:, k, tk], start=(k == 0), stop=(k == 15)),
                         reads=[kq] + ([("hn", blk, kk) for kk in range(16)] if m == 0 else []), writes=[pk])
                pb2 = ps[2 + (2 * m + blk) % 2]
                pk2 = ("ps", 2 + (2 * m + blk) % 2)
                for k in range(2):
                    P.op("pe", lambda e, k=k, tk=tk, pb2=pb2, wp_=wp_: e.matmul(pb2[:, :], lhsT=wp_[:, k, :], rhs=pTb[:, k, tk], start=(k == 0), stop=(k == 1)), reads=[kp, "pTb"], writes=[pk2])
                P.op("act", lambda e, pb=pb: e.activation(out=ga[:], in_=pb[:, :], func=AF.Sigmoid), reads=[pk], writes=["ga"])
                P.op("dve", lambda e, pb2=pb2: e.tensor_tensor(sgb[:], ga[:], pb2[:, :], ALU.mult), reads=["ga", pk2], writes=["sgb"])
                P.op("dve", lambda e, m=m, tk=tk: e.tensor_tensor(H[:, m, tk], H[:, m, tk], sgb[:], ALU.add),
                     reads=["sgb"] + [("hn", blk, kk) for kk in range(16)], writes=[("H2", m, blk)])
        for blk in range(2):
            tk = slice(blk * 512, (blk + 1) * 512)
            P.op("dve", lambda e: e.nop(), reads=[("H2", m, blk) for m in range(16)], writes=[("H2all", blk)])
            rms_bcast(H[:, :, tk], ("H2all", blk), sq, blk)
            obufs = [(ot[0], ("ot", 0)), (ga, "ga"), (sgb, "sgb"), (gsl, "gsl")]
            for m in range(16):
                o_, okey = obufs[m % 4]
                P.op("dve", lambda e, m=m, tk=tk, o_=o_: e.scalar_tensor_tensor(o_[:], H[:, m, tk], vecs[:, 2, m:m + 1], rstd[:, tk], ALU.mult, ALU.mult),
                     reads=[("H2all", blk), ("rstd", blk), "vecs"], writes=[okey])
                P.dma("sp" if m % 2 == 0 else "pool", D["outT"][:, m, tk], o_[:], reads=[okey], writes=[("out", m, blk)])
        P.barrier()
        P.emit()


_CACHE = {}


def _perm(r):
    return np.array([8 * (128 * r + cl) + t for t in range(8) for cl in range(128)], dtype=np.int64)


def _fm(a):
    F_, T_ = a.shape
    return np.ascontiguousarray(a.reshape(F_ // 128, 128, T_).transpose(1, 0, 2))


def kernel(x, p, norm_mix, w_in, q_norm, k_norm, ssm_a_re, ssm_a_im, ssm_log_dt,
           ssm_b_re, ssm_b_im, ssm_c_re, ssm_c_im, ssm_d, w_glu, b_glu, w_out,
           norm_ple, w_ple_gate, w_ple_proj, norm_final):
    f32 = np.float32
    x = np.asarray(x, f32); p = np.asarray(p, f32)
    if "p1" not in _CACHE:
        _CACHE["p1"] = build_phase1()
        _CACHE["p2"] = build_phase2()
    inv_freq = (np.float32(10000.0) ** (-np.arange(32, dtype=f32) / np.float32(32))).astype(f32)
    t = np.arange(L)
    rows = (t // 64).astype(f32); cols = (t % 64).astype(f32)
    cosT = np.zeros((128, L), f32); sinT = np.zeros((128, L), f32)
    for d in range(128):
        pos = rows if d < 64 else cols
        ang = (pos * inv_freq[d % 32]).astype(f32)
        cosT[d] = np.cos(ang); sinT[d] = np.sin(ang)
    csT = np.ascontiguousarray(np.stack([cosT, sinT], 1).reshape(128, 2, 16, 256).transpose(2, 0, 1, 3))
    Rm = np.zeros((128, 128), f32)
    for d in range(128):
        if d % 64 < 32:
            Rm[d + 32, d] = -1.0
        else:
            Rm[d - 32, d] = 1.0
    ident = np.eye(128, dtype=f32)
    si = np.arange(128) // 16
    maskF = (si[None, :] >= si[:, None]).astype(f32)
    maskB = (si[:, None] >= si[None, :]).astype(f32)
    c128 = np.ascontiguousarray(np.stack([Rm, ident, maskF, maskB], 1))
    nm = lambda v: np.ascontiguousarray(np.asarray(v, f32).reshape(-1, 128).T)
    w_in0 = np.asarray(w_in[0], f32)
    in1, in2 = [], []
    for c in range(8):
        b, r = c // 4, c % 4
        kv = r // 2
        G0 = 16 * r
        colsel = np.concatenate([np.arange(256 * r, 256 * r + 256), np.arange(1024 + 128 * kv, 1024 + 128 * kv + 128),
                                 np.arange(1280 + 128 * kv, 1280 + 128 * kv + 128), np.arange(1536 + 256 * r, 1536 + 256 * r + 256),
                                 np.arange(2560 + 256 * r, 2560 + 256 * r + 256)])
        w1 = np.ascontiguousarray(_fm(w_in0[:, colsel]).reshape(128, 16, 8, 128).transpose(2, 0, 1, 3))
        if r == 0:
            xTb = np.ascontiguousarray(_fm(np.ascontiguousarray(x[b].T)).reshape(128, 16, 16, 256).transpose(2, 0, 1, 3))
        qkn = np.ascontiguousarray(np.stack([q_norm[0], k_norm[0]], 1).astype(f32))
        dp = lambda a: np.ascontiguousarray(np.asarray(a[0], f32)[:, G0:G0 + 16, :].transpose(0, 2, 1).reshape(128, 16))
        are = dp(ssm_a_re); aim = dp(ssm_a_im)
        ldt = np.ascontiguousarray(np.repeat(np.asarray(ssm_log_dt[0], f32)[:, None, G0:G0 + 16], 64, 1).reshape(128, 16))
        dt_ = np.asarray(ssm_d[0], f32)[G0 * 16:(G0 + 16) * 16].reshape(16, 16)
        dtile = np.ascontiguousarray(np.tile(dt_.T, (8, 1)))
        mfv = np.zeros((128, 16), f32); mfv[:64] = 1.0
        mbv = np.zeros((128, 16), f32); mbv[64:] = 1.0
        ssmp = np.ascontiguousarray(np.stack([are, aim, ldt, dtile, mfv, mbv], 1))
        Bl = lambda a: np.asarray(a[0], f32)[:, G0:G0 + 16].transpose(0, 2, 1, 3).reshape(128, 16, 16)
        Cl = lambda a: np.asarray(a[0], f32)[:, G0:G0 + 16].transpose(0, 3, 1, 2).reshape(128, 16, 16)
        ssmbc = np.ascontiguousarray(np.stack([Bl(ssm_b_re), Bl(ssm_b_im), Cl(ssm_c_re), Cl(ssm_c_im)], 1))
        in1.append({"xT": xTb, "w1": w1, "nmix": nm(norm_mix[0]), "qkn": qkn, "csT": csT, "c128": c128,
                    "ssmp": ssmp, "ssmbc": ssmbc})
    res1 = run_bass_kernel_spmd(_CACHE["p1"], in1, core_ids=list(range(8)))
    EXs = [np.asarray(r["EX"], f32) for r in res1.results]
    if os.environ.get("KP1ONLY"):
        return EXs
    vecs = np.ascontiguousarray(np.stack([nm(norm_mix[0]), nm(norm_ple[0]), nm(norm_final), nm(b_glu[0])], 1))
    tl = lambda w: np.ascontiguousarray(_fm(w).reshape(128, w.shape[0] // 128, w.shape[1] // 128, 128).transpose(2, 0, 1, 3))
    wgs = tl(w_in0[:, 3584:4608]); wglu = tl(np.asarray(w_glu[0], f32)); wout = tl(np.asarray(w_out[0], f32))
    wpg = tl(np.asarray(w_ple_gate[0], f32)); wpp = tl(np.asarray(w_ple_proj[0], f32))
    for c in range(8):
        b, r = c // 4, c % 4
        pr = _perm(r)
        sel = np.concatenate([np.arange(tt * NCH + 128 * r, tt * NCH + 128 * r + 128) for tt in range(8)])
        attn = np.concatenate([EXs[b * 4 + q][0:256][:, sel] for q in range(4)], 0)
        ssm = np.concatenate([EXs[b * 4 + q][256:512][:, sel] for q in range(4)], 0)
        mixin = _fm(np.concatenate([attn, ssm], 0))
        x2 = np.ascontiguousarray(_fm(np.ascontiguousarray(x[b].T[:, pr])).reshape(128, 16, 2, 512).transpose(2, 0, 1, 3))
        pT = _fm(np.ascontiguousarray(p[0, b].T[:, pr]))
        in2.append({"mixin": mixin, "x2": x2, "pT": pT, "wgs": wgs, "wglu": wglu, "wout": wout, "wpg": wpg, "wpp": wpp, "vecs": vecs})
    res2 = run_bass_kernel_spmd(_CACHE["p2"], in2, core_ids=list(range(8)))
    out = np.zeros((2, L, 2048), f32)
    for c in range(8):
        b, r = c // 4, c % 4
        o = np.asarray(res2.results[c]["outT"], f32)
        o = o.transpose(1, 0, 2).reshape(2048, 1024)
        out[b, _perm(r), :] = o.T
    return out
```

```python
import math
import os
STAGES = os.environ.get('KSTAGES', 'ABCD')
from contextlib import ExitStack
import numpy as np
import concourse.bass as bass
import concourse.mybir as mybir
from concourse.bass_utils import run_bass_kernel_spmd

F32 = mybir.dt.float32
BF16 = mybir.dt.bfloat16
ALU = mybir.AluOpType
AF = mybir.ActivationFunctionType

NDMA = 12
EPS = 1e-6
L = 4096
NCH = 512


class Prog:
    ENGS = ("pe", "act", "dve", "pool", "sp")

    def __init__(self, nc, stack):
        self.nc = nc
        self.sem = {e: stack.enter_context(nc.semaphore("s_" + e)) for e in self.ENGS}
        self.dsem = {e: [stack.enter_context(nc.semaphore("d_%s%d" % (e, i))) for i in range(NDMA)]
                     for e in ("sp", "pool")}
        self.cnt = {e: 0 for e in self.ENGS}
        self.dcnt = {e: [0] * NDMA for e in self.dsem}
        self.dnext = {e: 0 for e in self.dsem}
        self.waited = {e: {} for e in self.ENGS}
        self._reset()

    def _reset(self):
        self.instrs = []
        self.last_writer = {}
        self.readers = {}
        self.last_idx = {}
        self.dmas = []

    def op(self, eng, fn, reads=(), writes=(), dma=False, extra=()):
        idx = len(self.instrs)
        deps = set(extra)
        for k in reads:
            if k in self.last_writer:
                deps.add(self.last_writer[k])
        for k in writes:
            if k in self.last_writer:
                deps.add(self.last_writer[k])
            for r in self.readers.get(k, ()):
                deps.add(r)
        self.instrs.append(dict(eng=eng, fn=fn, deps=deps, dma=dma))
        for k in reads:
            lst = self.readers.setdefault(k, [])
            if not dma:
                lst[:] = [r for r in lst if self.instrs[r]["dma"] or self.instrs[r]["eng"] != eng]
            lst.append(idx)
        for k in writes:
            self.last_writer[k] = idx
            self.readers[k] = []
        self.last_idx[eng] = idx
        if dma:
            self.dmas.append(idx)
        return idx

    def dma(self, eng, out, in_, reads=(), writes=()):
        return self.op(eng, lambda e: e.dma_start(out=out, in_=in_), reads, writes, dma=True)

    def barrier(self):
        deps = set(self.last_idx.values()) | set(self.dmas)
        for e in self.ENGS:
            self.op(e, lambda en: en.nop(), extra=deps)
        self.last_writer = {}
        self.readers = {}
        self.dmas = []

    def emit(self):
        nc = self.nc
        ins = self.instrs
        needed = set()
        for i, it in enumerate(ins):
            for d in it["deps"]:
                if ins[d]["eng"] == "pe" and it["eng"] == "pe" and not ins[d]["dma"]:
                    continue
                needed.add(d)
        for i, it in enumerate(ins):
            e = it["eng"]
            if it["dma"]:
                s = self.dnext[e]
                self.dnext[e] = (s + 1) % NDMA
                self.dcnt[e][s] += 16
                it["tok"] = (self.dsem[e][s], self.dcnt[e][s], "d_%s%d" % (e, s))
                it["inc"] = 16
            elif i in needed:
                self.cnt[e] += 1
                it["tok"] = (self.sem[e], self.cnt[e], "s_" + e)
                it["inc"] = 1
            else:
                it["tok"] = None
        per = {e: [] for e in self.ENGS}
        for i, it in enumerate(ins):
            per[it["eng"]].append(i)

        def replay(ename, eng):
            w = self.waited[ename]
            for i in per[ename]:
                it = ins[i]
                waits = {}
                for d in it["deps"]:
                    t = ins[d]["tok"]
                    if t is None:
                        continue
                    if w.get(t[2], 0) < t[1] and waits.get(t[2], (None, 0))[1] < t[1]:
                        waits[t[2]] = (t[0], t[1])
                if it["dma"]:
                    t = it["tok"]
                    prev = t[1] - 16
                    if prev > 0 and w.get(t[2], 0) < prev and waits.get(t[2], (None, 0))[1] < prev:
                        waits[t[2]] = (t[0], prev)
                for name, (s, v) in waits.items():
                    eng.wait_ge(s, v)
                    w[name] = v
                bi = it["fn"](eng)
                if it["tok"] is not None:
                    bi.then_inc(it["tok"][0], it["inc"])

        with nc.Block() as block:
            @block.tensor
            def _(e):
                replay("pe", e)

            @block.scalar
            def _(e):
                replay("act", e)

            @block.vector
            def _(e):
                replay("dve", e)

            @block.gpsimd
            def _(e):
                replay("pool", e)

            @block.sync
            def _(e):
                replay("sp", e)
        self._reset()


def _din(nc, name, shape):
    return nc.dram_tensor(name, list(shape), F32, kind="ExternalInput").ap()


def build_phase1():
    nc = bass.Bass("TRN2", target_bir_lowering=False)
    xT = _din(nc, "xT", [16, 128, 16, 256])
    w1 = _din(nc, "w1", [8, 128, 16, 128])
    nmix_d = _din(nc, "nmix", [128, 16])
    qkn_d = _din(nc, "qkn", [128, 2])
    cs_d = _din(nc, "csT", [16, 128, 2, 256])
    c128_d = _din(nc, "c128", [128, 4, 128])
    sp_d = _din(nc, "ssmp", [128, 6, 16])
    bc_d = _din(nc, "ssmbc", [128, 4, 16, 16])
    EX = nc.dram_tensor("EX", [512, L], F32, kind="ExternalOutput").ap()

    with ExitStack() as st0:
        P = Prog(nc, st0)
        sb0 = lambda name, shape, dt=F32: st0.enter_context(nc.sbuf_tensor(name, list(shape), dt))
        ps = [st0.enter_context(nc.psum_tensor("ps%d" % i, [128, 512], F32)) for i in range(8)]
        UT = sb0("UT", [128, 2, L], BF16)
        c128 = sb0("c128s", [128, 4, 128])
        Rm = sb0("Rm", [128, 128], BF16)
        identb = sb0("identb", [128, 128], BF16)
        ones = sb0("ones", [128, 128], BF16)
        qkn = sb0("qkns", [128, 2])
        P.dma("sp", c128[:], c128_d, writes=["c128"])
        P.dma("sp", qkn[:], qkn_d, writes=["qkn"])
        P.op("dve", lambda e: e.tensor_copy(Rm[:], c128[:, 0, :]), reads=["c128"], writes=["Rm"])
        P.op("dve", lambda e: e.tensor_copy(identb[:], c128[:, 1, :]), reads=["c128"], writes=["identb"])
        P.op("dve", lambda e: e.memset(ones[:], 1.0), writes=["ones"])

        with ExitStack() as st1:
            sb1 = lambda name, shape, dt=F32: st1.enter_context(nc.sbuf_tensor(name, list(shape), dt))
            QT = sb1("QT", [128, 2, L], BF16)
            KT = sb1("KT", [128, L], BF16)
            Vt = sb1("Vt", [128, 32, 128], BF16)
            SG = sb1("SG", [128, 2, L], BF16)
            with ExitStack() as st:
                sb = lambda name, shape, dt=F32: st.enter_context(nc.sbuf_tensor(name, list(shape), dt))
                W1b = sb("W1b", [128, 16, 1024], BF16)
                wst = [sb("wst%d" % i, [128, 16, 128]) for i in range(1)]
                xblk = [sb("xblk%d" % i, [128, 16, 256]) for i in range(2)]
                sq = sb("sq", [128, 16, 256], BF16)
                xn = [sb("xn%d" % i, [128, 16, 512], BF16) for i in range(2)]
                nmix = sb("nmixs", [128, 16])
                rt = [sb("rt%d" % i, [128, 256]) for i in range(2)]
                rstd = [sb("rstd%d" % i, [128, 256]) for i in range(2)]
                qw = [sb("qw%d" % i, [128, 512], BF16) for i in range(3)]
                qsq = [sb("qsq%d" % i, [128, 512], BF16) for i in range(3)]
                rq = [sb("rq%d" % i, [128, 512]) for i in range(3)]
                t1 = [sb("t1%d" % i, [128, 512]) for i in range(3)]
                t2 = [sb("t2s", [128, 512])] * 3
                csb = [sb("csbs", [128, 2, 512])] * 2
                vT = sb("vT", [128, 512], BF16)
                P.dma("sp", nmix[:], nmix_d, writes=["nmix"])
                for pc in range(8):
                    P.dma("sp", wst[0][:], w1[pc], writes=[("wst", 0)])
                    for kc in range(16):
                        if kc % 2 == 0:
                            P.op("dve", lambda e, kc=kc, pc=pc: e.tensor_scalar(
                                W1b[:, kc, pc * 128:(pc + 1) * 128], wst[0][:, kc, :], nmix[:, kc:kc + 1], None, ALU.mult),
                                reads=[("wst", 0), "nmix"], writes=[("W1b", pc, kc)])
                        else:
                            P.op("act", lambda e, kc=kc, pc=pc: e.activation(
                                out=W1b[:, kc, pc * 128:(pc + 1) * 128], in_=wst[0][:, kc, :], func=AF.Copy, scale=nmix[:, kc:kc + 1]),
                                reads=[("wst", 0), "nmix"], writes=[("W1b", pc, kc)])
                ctiles = [(0, "q", 0), (128, "q", 1), (256, "k", 2), (384, "v", 0), (512, "g", 0), (640, "g", 1), (768, "u", 0), (896, "u", 1)]
                sbank = [2, 6]

                def prep(tb):
                    s = tb % 2
                    pr = (tb // 2) % 2
                    hf = tb % 2
                    tok = slice(tb * 256, (tb + 1) * 256)
                    P.dma("sp", xblk[s][:, 0:8, :], xT[tb, :, 0:8, :], writes=[("xblk", s, 0)])
                    P.dma("sp", xblk[s][:, 8:16, :], xT[tb, :, 8:16, :], writes=[("xblk", s, 1)])
                    P.op("act", lambda e: e.activation(out=sq[:], in_=xblk[s][:], func=AF.Square),
                         reads=[("xblk", s, 0), ("xblk", s, 1)], writes=["sq"])
                    bk = ps[sbank[s]]
                    for kc in range(16):
                        P.op("pe", lambda e, kc=kc: e.matmul(bk[:, 0:256], lhsT=ones[:], rhs=sq[:, kc, :], start=(kc == 0), stop=(kc == 15)),
                             reads=["sq", "ones"], writes=[("ps", sbank[s])])
                    P.op("act", lambda e: e.activation(out=rt[s][:], in_=bk[:, 0:256], func=AF.Sqrt, scale=1.0 / 2048, bias=EPS),
                         reads=[("ps", sbank[s])], writes=[("rt", s)])
                    P.op("dve", lambda e: e.reciprocal(rstd[s][:], rt[s][:]), reads=[("rt", s)], writes=[("rstd", s)])
                    P.op("dve", lambda e: e.tensor_tensor(xn[pr][:, :, hf * 256:(hf + 1) * 256], xblk[s][:], rstd[s][:].unsqueeze(1).to_broadcast([128, 16, 256]), ALU.mult),
                         reads=[("xblk", s, 0), ("xblk", s, 1), ("rstd", s)], writes=[("xn", pr, hf)])

                def qk_post(pp, kind, idx, j):
                    pr = pp % 2
                    tok = slice(pp * 512, (pp + 1) * 512)
                    b1, b2 = [(3, 4), (7, 3), (4, 7)][j]
                    bq, br = ps[b1], ps[b2]
                    P.op("pe", lambda e: e.matmul(bq[:, :], lhsT=ones[:], rhs=qsq[j][:], start=True, stop=True),
                         reads=[("qsq", j), "ones"], writes=[("ps", b1)])
                    P.op("pe", lambda e: e.matmul(br[:, :], lhsT=Rm[:], rhs=qw[j][:], start=True, stop=True),
                         reads=[("qw", j), "Rm"], writes=[("ps", b2)])
                    P.op("act", lambda e: e.activation(out=rq[j][:], in_=bq[:, :], func=AF.Sqrt, scale=1.0 / 128, bias=EPS),
                         reads=[("ps", b1)], writes=[("rq", j)])
                    P.op("dve", lambda e: e.tensor_tensor(t1[j][:], qw[j][:], csb[pr][:, 0, :], ALU.mult),
                         reads=[("qw", j)] + [("csb", 0, 0, hf) for hf in range(2)], writes=[("t1", j)])
                    P.op("dve", lambda e: e.tensor_tensor(t2[j][:], br[:, :], csb[pr][:, 1, :], ALU.mult),
                         reads=[("ps", b2)] + [("csb", 0, 1, hf) for hf in range(2)], writes=[("t2", 0)])
                    P.op("dve", lambda e: e.tensor_tensor(t1[j][:], t1[j][:], t2[j][:], ALU.add), reads=[("t1", j), ("t2", 0)], writes=[("t1", j)])
                    dst = QT[:, idx, tok] if kind == "q" else KT[:, tok]
                    P.op("dve", lambda e: e.reciprocal(rq[j][:], rq[j][:]), reads=[("rq", j)], writes=[("rq", j)])
                    P.op("dve", lambda e: e.tensor_tensor(dst, t1[j][:], rq[j][:], ALU.mult),
                         reads=[("t1", j), ("rq", j)], writes=[("QK", kind, idx, pp)])

                def main(pp):
                    pr = pp % 2
                    tok = slice(pp * 512, (pp + 1) * 512)
                    pending = None
                    for hf in range(2):
                        P.dma("sp", csb[0][:, :, hf * 256:(hf + 1) * 256], cs_d[2 * pp + hf], writes=[("csb", 0, 0, hf), ("csb", 0, 1, hf)])
                    for ci, (c0, kind, idx) in enumerate(ctiles):
                        pb = ps[ci % 2]
                        pk = ("ps", ci % 2)
                        for kc in range(16):
                            P.op("pe", lambda e, pb=pb, kc=kc, c0=c0: e.matmul(pb[:, :], lhsT=W1b[:, kc, c0:c0 + 128], rhs=xn[pr][:, kc, :],
                                                                         start=(kc == 0), stop=(kc == 15)),
                                 reads=[("xn", pr, 0), ("xn", pr, 1)] + ([("W1b", c0 // 128, kc)] if pp == 0 else []), writes=[pk])
                        if pending is not None:
                            qk_post(*pending)
                            pending = None
                        if kind in ("q", "k"):
                            j = idx
                            col = 0 if kind == "q" else 1
                            P.op("act", lambda e, pb=pb, j=j, col=col: e.activation(out=qw[j][:], in_=pb[:, :], func=AF.Copy, scale=qkn[:, col:col + 1]),
                                 reads=[pk, "qkn"], writes=[("qw", j)])
                            P.op("act", lambda e, pb=pb, j=j: e.activation(out=qsq[j][:], in_=pb[:, :], func=AF.Square),
                                 reads=[pk], writes=[("qsq", j)])
                            pending = (pp, kind, idx, j)
                        elif kind == "v":
                            P.op("act", lambda e, pb=pb: e.activation(out=vT[:], in_=pb[:, :], func=AF.Copy), reads=[pk], writes=["vT"])
                            for q4 in range(4):
                                P.op("pe", lambda e, q4=q4: e.matmul(ps[5][:, q4 * 128:(q4 + 1) * 128], lhsT=vT[:, q4 * 128:(q4 + 1) * 128], rhs=identb[:], start=True, stop=True),
                                     reads=["vT", "identb"], writes=[("ps", 5)])
                            P.op("dve", lambda e: e.tensor_copy(Vt[:, pp * 4:pp * 4 + 4, :], ps[5][:, :].rearrange("p (a b) -> p a b", a=4)),
                                 reads=[("ps", 5)], writes=[("Vt", pp)])
                        elif kind == "g":
                            P.op("act", lambda e, pb=pb, idx=idx: e.activation(out=SG[:, idx, tok], in_=pb[:, :], func=AF.Silu),
                                 reads=[pk], writes=[("SG", idx, pp)])
                        else:
                            dstv = UT[:, idx, :].rearrange("p (s c) -> p s c", s=8)[:, :, pp * 64:(pp + 1) * 64]
                            srcv = pb[:, :].rearrange("p (c s) -> p s c", s=8)
                            P.op("dve", lambda e, dstv=dstv, srcv=srcv: e.tensor_copy(dstv, srcv), reads=[pk], writes=[("UT", idx, pp)])

                prep(0)
                prep(1)
                for pp in range(8):
                    if pp + 1 < 8:
                        prep(2 * pp + 2)
                        prep(2 * pp + 3)
                    main(pp)
                P.barrier()
                P.emit()

            stB = st1
            sbB = lambda name, shape, dt=F32: stB.enter_context(nc.sbuf_tensor(name, list(shape), dt))
            pT = [sbB("pT%d" % i, [128, 512], BF16) for i in range(4)]
            rden = sbB("rden", [128, 512])
            ot = sbB("ot", [128, 512])
            YA = [sbB("YA%d" % i, [128, 8, 64]) for i in range(2)]
            acnt = [0]

            ait = [0]
            scale_att = 128 ** -0.5

            def a_qk(i):
                blk, kt = divmod(i, 32)
                h, qb = blk // 8, blk % 8
                sp_ = ps[i % 2]
                P.op("pe", lambda e: e.matmul(sp_[:, :], lhsT=KT[:, kt * 128:(kt + 1) * 128], rhs=QT[:, h, qb * 512:(qb + 1) * 512], start=True, stop=True),
                     reads=[], writes=[("ps", i % 2)])
                P.op("act", lambda e: e.activation(out=pT[i % 4][:], in_=sp_[:, :], func=AF.Exp, scale=scale_att),
                     reads=[("ps", i % 2)], writes=[("pT", i % 4)])

            def a_pv(i):
                blk, kt = divmod(i, 32)
                P.op("pe", lambda e: e.matmul(ps[4][:, :], lhsT=Vt[:, kt, :], rhs=pT[i % 4][:], start=(kt == 0), stop=(kt == 31)),
                     reads=[("pT", i % 4)], writes=[("ps", 4)])
                P.op("pe", lambda e: e.matmul(ps[5][:, :], lhsT=ones[:], rhs=pT[i % 4][:], start=(kt == 0), stop=(kt == 31)),
                     reads=[("pT", i % 4)], writes=[("ps", 5)])

            def attn_norm(blk):
                h, qb = blk // 8, blk % 8
                bo, bd = 4, 5
                qs = slice(qb * 512, (qb + 1) * 512)
                P.op("act", lambda e: e.activation(out=rden[:], in_=ps[bd][:, :], func=AF.Ln), reads=[("ps", bd)], writes=["lnd"])
                P.op("act", lambda e: e.activation(out=rden[:], in_=rden[:], func=AF.Exp, scale=-1.0), reads=["lnd"], writes=["rden"])
                P.op("dve", lambda e: e.tensor_tensor(ot[:], ps[bo][:, :], rden[:], ALU.mult), reads=[("ps", bo), "rden"], writes=["ot"])
                ya = YA[blk % 2]
                P.op("dve", lambda e: e.tensor_tensor(ya[:], ot[:].rearrange("p (c s) -> p s c", s=8),
                                                     SG[:, h, qs].rearrange("p (c s) -> p s c", s=8), ALU.mult),
                     reads=["ot"], writes=[("YA", blk % 2)])
                P.dma("sp", EX[h * 128:(h + 1) * 128, :].rearrange("p (s c) -> p s c", s=8)[:, :, qb * 64:(qb + 1) * 64], ya[:], reads=[("YA", blk % 2)])

            def attn_step(n=8):
                if 'B' not in STAGES:
                    return
                for _ in range(n):
                    i = ait[0]
                    if i >= 512:
                        return
                    ait[0] += 1
                    blk, kt = divmod(i, 32)
                    if i == 0:
                        a_qk(0)
                    if i + 1 < 512:
                        a_qk(i + 1)
                    if kt == 0 and blk >= 1:
                        attn_norm(blk - 1)
                    a_pv(i)

            def attn_flush():
                if 'B' not in STAGES:
                    return
                attn_step(512)
                attn_norm(15)

            with ExitStack() as st:
              if 'C' in STAGES:
                  sb = lambda name, shape, dt=F32: st.enter_context(nc.sbuf_tensor(name, list(shape), dt))
                  spm = sb("spm", [128, 6, 16])
                  P.dma("sp", spm[:], sp_d, writes=["spm"])
                  rho = sb("rho", [128, 16]); ur = sb("ur", [128, 16]); ui = sb("ui", [128, 16])
                  WSr = sb("WSr", [128, 16, 128], BF16); WSi = sb("WSi", [128, 16, 128], BF16); M0 = sb("M0", [128, 16, 128], BF16)
                  E4 = [128, 16, 8, 16]
                  Cfr = sb("Cfr", E4, BF16); Cfi = sb("Cfi", E4, BF16); Cbr = sb("Cbr", E4, BF16); Cbi = sb("Cbi", E4, BF16)
                  fl = lambda a, g: a[:, g, :, :].rearrange("p s h -> p (s h)")

                  def dv(fn, reads, writes, eng="dve"):
                      P.op(eng, fn, reads=reads, writes=writes)

                  def tt(out, a, b, op, reads, writes):
                      dv(lambda e: e.tensor_tensor(out, a, b, op), reads, writes)

                  def cmul(or_, oi_, ar, ai, br, bi, reads, okr, oki, ta, tb, tk):
                      ka, kb = tk + "a", tk + "b"
                      tt(ta, ar, br, ALU.mult, reads, [ka])
                      tt(tb, ai, bi, ALU.mult, reads, [kb])
                      tt(or_, ta, tb, ALU.subtract, [ka, kb], [okr])
                      tt(ta, ar, bi, ALU.mult, reads + [okr], [ka])
                      tt(tb, ai, br, ALU.mult, reads + [okr], [kb])
                      tt(oi_, ta, tb, ALU.add, [ka, kb], [oki])

                  with ExitStack() as stg:
                      cnt = [0]

                      def tmp(shape, dt=F32):
                          cnt[0] += 1
                          return stg.enter_context(nc.sbuf_tensor("tmp%d" % cnt[0], list(shape), dt))

                      bcm = tmp([128, 4, 16, 16])
                      P.dma("sp", bcm[:], bc_d, writes=["bcm"])
                      are, aim, ldt, dtile = spm[:, 0, :], spm[:, 1, :], spm[:, 2, :], spm[:, 3, :]
                      mf, mb = spm[:, 4, 0:1], spm[:, 5, 0:1]
                      S = [128, 16]
                      lre = tmp(S); dt_ = tmp(S); al = tmp(S); th = tmp(S); mag = tmp(S)
                      sn = tmp(S); cs = tmp(S); a1 = tmp(S); a2 = tmp(S); a3 = tmp(S)
                      dv(lambda e: e.tensor_scalar(lre[:], are, -1e-4, None, ALU.min), ["spm"], ["lre"])
                      dv(lambda e: e.activation(out=dt_[:], in_=ldt, func=AF.Exp), ["spm"], ["dt"], "act")
                      tt(al[:], lre[:], dt_[:], ALU.mult, ["lre", "dt"], ["al"])
                      tt(th[:], aim, dt_[:], ALU.mult, ["spm", "dt"], ["th"])
                      dv(lambda e: e.activation(out=mag[:], in_=al[:], func=AF.Exp), ["al"], ["mag"], "act")
                      halfpi = tmp([128, 1])
                      dv(lambda e: e.memset(halfpi[:], math.pi / 2), [], ["halfpi"])
                      dv(lambda e: e.activation(out=sn[:], in_=th[:], func=AF.Sin, scale=1.0 / 32), ["th"], ["sn"], "act")
                      dv(lambda e: e.activation(out=cs[:], in_=th[:], func=AF.Sin, scale=1.0 / 32, bias=halfpi[:]), ["th", "halfpi"], ["cs"], "act")
                      for it in range(5):
                          tt(a1[:], sn[:], cs[:], ALU.mult, ["sn", "cs"], ["a1"])
                          tt(a2[:], cs[:], cs[:], ALU.mult, ["cs"], ["a2"])
                          tt(a3[:], sn[:], sn[:], ALU.mult, ["sn"], ["a3"])
                          tt(cs[:], a2[:], a3[:], ALU.subtract, ["a2", "a3"], ["cs"])
                          dv(lambda e: e.tensor_scalar(sn[:], a1[:], 2.0, None, ALU.mult), ["a1"], ["sn"])
                      attn_step()
                      lbr = tmp(S); lbi = tmp(S)
                      tt(lbr[:], mag[:], cs[:], ALU.mult, ["mag", "cs"], ["lbr"])
                      tt(lbi[:], mag[:], sn[:], ALU.mult, ["mag", "sn"], ["lbi"])
                      nr = tmp(S); den = tmp(S); bfr = tmp(S); bfi = tmp(S)
                      dv(lambda e: e.tensor_scalar(nr[:], lbr[:], -1.0, None, ALU.add), ["lbr"], ["nr"])
                      tt(a1[:], lre[:], lre[:], ALU.mult, ["lre"], ["a1"])
                      tt(a2[:], aim, aim, ALU.mult, ["spm"], ["a2"])
                      tt(den[:], a1[:], a2[:], ALU.add, ["a1", "a2"], ["den"])
                      dv(lambda e: e.reciprocal(den[:], den[:]), ["den"], ["den"])
                      tt(a1[:], nr[:], lre[:], ALU.mult, ["nr", "lre"], ["a1"])
                      tt(a2[:], lbi[:], aim, ALU.mult, ["lbi", "spm"], ["a2"])
                      tt(a3[:], a1[:], a2[:], ALU.add, ["a1", "a2"], ["a3"])
                      tt(bfr[:], a3[:], den[:], ALU.mult, ["a3", "den"], ["bfr"])
                      tt(a1[:], lbi[:], lre[:], ALU.mult, ["lbi", "lre"], ["a1"])
                      tt(a2[:], nr[:], aim, ALU.mult, ["nr", "spm"], ["a2"])
                      tt(a3[:], a1[:], a2[:], ALU.subtract, ["a1", "a2"], ["a3"])
                      tt(bfi[:], a3[:], den[:], ALU.mult, ["a3", "den"], ["bfi"])
                      Pwr = tmp([128, 16, 9]); Pwi = tmp([128, 16, 9])
                      dv(lambda e: e.memset(Pwr[:, :, 0], 1.0), [], ["Pw"])
                      dv(lambda e: e.memset(Pwi[:, :, 0], 0.0), ["Pw"], ["Pw"])
                      for k in range(1, 9):
                          cmul(Pwr[:, :, k], Pwi[:, :, k], Pwr[:, :, k - 1], Pwi[:, :, k - 1], lbr[:], lbi[:], ["lbr", "lbi", "Pw"], "Pw", "Pw", a1[:], a2[:], "a12")
                      rinv = tmp(S); i8r = tmp(S); i8i = tmp(S)
                      dv(lambda e: e.activation(out=rho[:], in_=al[:], func=AF.Exp, scale=8.0), ["al"], ["rho"], "act")
                      dv(lambda e: e.reciprocal(rinv[:], rho[:]), ["rho"], ["rinv"])
                      tt(ur[:], Pwr[:, :, 8], rinv[:], ALU.mult, ["Pw", "rinv"], ["ur"])
                      tt(ui[:], Pwi[:, :, 8], rinv[:], ALU.mult, ["Pw", "rinv"], ["ui"])
                      tt(i8r[:], ur[:], rinv[:], ALU.mult, ["ur", "rinv"], ["i8r"])
                      tt(a3[:], ui[:], rinv[:], ALU.mult, ["ui", "rinv"], ["a3"])
                      dv(lambda e: e.tensor_scalar(i8i[:], a3[:], -1.0, None, ALU.mult), ["a3"], ["i8i"])
                      PAr = tmp([128, 16, 8]); PAi = tmp([128, 16, 8]); PCr = tmp([128, 16, 8]); PCi = tmp([128, 16, 8])
                      for (dst, src) in ((PAr, Pwr), (PAi, Pwi)):
                          for k in range(8):
                              dv(lambda e, dst=dst, src=src, k=k: e.tensor_copy(dst[0:64, :, k:k + 1], src[0:64, :, 7 - k:8 - k]), ["Pw", "PA"], ["PA"])
                          dv(lambda e, dst=dst, src=src: e.tensor_copy(dst[64:128, :, :], src[64:128, :, 0:8]), ["Pw", "PA"], ["PA"])
                      for (dst, src) in ((PCr, Pwr), (PCi, Pwi)):
                          dv(lambda e, dst=dst, src=src: e.tensor_copy(dst[0:64, :, :], src[0:64, :, 1:9]), ["Pw", "PC"], ["PC"])
                          for k in range(8):
                              dv(lambda e, dst=dst, src=src, k=k: e.tensor_copy(dst[64:128, :, k:k + 1], src[64:128, :, 8 - k:9 - k]), ["Pw", "PC"], ["PC"])
                      PA2r = tmp([128, 16, 8]); PA2i = tmp([128, 16, 8]); t8a = tmp([128, 16, 8]); t8b = tmp([128, 16, 8])
                      bc8 = lambda a: a.unsqueeze(2).to_broadcast([128, 16, 8])
                      cmul(PA2r[:], PA2i[:], PAr[:], PAi[:], bc8(i8r[:]), bc8(i8i[:]), ["PA", "i8r", "i8i"], "PA2r", "PA2i", t8a[:], t8b[:], "t8")
                      Bbr = tmp([128, 16, 16]); Bbi = tmp([128, 16, 16]); t16a = tmp([128, 16, 16]); t16b = tmp([128, 16, 16])
                      bc16 = lambda a: a.unsqueeze(2).to_broadcast([128, 16, 16])
                      cmul(Bbr[:], Bbi[:], bcm[:, 0, :, :], bcm[:, 1, :, :], bc16(bfr[:]), bc16(bfi[:]), ["bcm", "bfr", "bfi"], "Bbr", "Bbi", t16a[:], t16b[:], "t16")
                      exa = tmp(E4); exb = tmp(E4); Er_ = tmp(E4); Ei_ = tmp(E4)

                      def cexp(pr, pi, mr, mi, reads):
                          for s_ in range(8):
                              b3 = lambda a, s_=s_: a[:, :, s_:s_ + 1].to_broadcast([128, 16, 16])
                              cmul(Er_[:, :, s_, :], Ei_[:, :, s_, :], b3(pr), b3(pi), mr, mi, reads + ["Er", "Ei"], "Er", "Ei", exa[:, :, s_, :], exb[:, :, s_, :], "ex")
                              if s_ % 2 == 1:
                                  attn_step()
                      Ar = tmp(E4, BF16); Ai = tmp(E4, BF16)
                      A2fr = tmp(E4, BF16); A2fi = tmp(E4, BF16); A2br = tmp(E4, BF16); A2bi = tmp(E4, BF16)
                      Ccr = tmp(E4, BF16); nCci = tmp(E4, BF16)
                      attn_step()
                      cexp(PAr, PAi, Bbr[:], Bbi[:], ["PA", "Bbr", "Bbi"])
                      attn_step()
                      dv(lambda e: e.tensor_copy(Ar[:], Er_[:]), ["Er"], ["Ar"])
                      dv(lambda e: e.tensor_copy(Ai[:], Ei_[:]), ["Ei"], ["Ai"])
                      cexp(PA2r, PA2i, Bbr[:], Bbi[:], ["PA2r", "PA2i", "Bbr", "Bbi", "Ar", "Ai"])
                      attn_step()
                      dv(lambda e: e.tensor_scalar(A2fr[:], Er_[:], mf, None, ALU.mult), ["Er", "spm"], ["A2fr"])
                      dv(lambda e: e.tensor_scalar(A2br[:], Er_[:], mb, None, ALU.mult), ["Er", "spm"], ["A2br"])
                      dv(lambda e: e.tensor_scalar(A2fi[:], Ei_[:], mf, None, ALU.mult), ["Ei", "spm"], ["A2fi"])
                      dv(lambda e: e.tensor_scalar(A2bi[:], Ei_[:], mb, None, ALU.mult), ["Ei", "spm"], ["A2bi"])
                      cexp(PCr, PCi, bcm[:, 2, :, :], bcm[:, 3, :, :], ["PC", "bcm", "A2fr", "A2br", "A2fi", "A2bi"])
                      attn_step()
                      dv(lambda e: e.tensor_copy(Ccr[:], Er_[:]), ["Er"], ["Ccr"])
                      dv(lambda e: e.tensor_scalar(nCci[:], Ei_[:], -1.0, None, ALU.mult), ["Ei"], ["nCci"])
                      dv(lambda e: e.tensor_scalar(Cfr[:], Er_[:], mf, None, ALU.mult), ["Er", "spm"], ["Cfr"])
                      dv(lambda e: e.tensor_scalar(Cbr[:], Er_[:], mb, None, ALU.mult), ["Er", "spm"], ["Cbr"])
                      dv(lambda e: e.tensor_scalar(Cfi[:], nCci[:], mf, None, ALU.mult), ["nCci", "spm"], ["Cfi"])
                      dv(lambda e: e.tensor_scalar(Cbi[:], nCci[:], mb, None, ALU.mult), ["nCci", "spm"], ["Cbi"])
                      mt1 = tmp([128, 128]); mt2 = tmp([128, 128])
                      for g in range(16):
                          attn_step()
                          P.op("pe", lambda e, g=g: e.matmul(ps[2][:, 0:128], lhsT=fl(Ar, g), rhs=identb[:], start=True, stop=True), reads=["Ar", "identb"], writes=[("ps", 2)])
                          P.op("pe", lambda e, g=g: e.matmul(ps[3][:, 0:128], lhsT=fl(Ai, g), rhs=identb[:], start=True, stop=True), reads=["Ai", "identb"], writes=[("ps", 3)])
                          P.op("act", lambda e, g=g: e.activation(out=WSr[:, g, :], in_=ps[2][:, 0:128], func=AF.Copy), reads=[("ps", 2)], writes=[("WSr", g)])
                          P.op("act", lambda e, g=g: e.activation(out=WSi[:, g, :], in_=ps[3][:, 0:128], func=AF.Copy), reads=[("ps", 3)], writes=[("WSi", g)])
                          P.op("pe", lambda e, g=g: e.matmul(ps[2][:, 0:128], lhsT=fl(A2fr, g), rhs=fl(Ccr, g), start=True, stop=False), reads=["A2fr", "Ccr"], writes=[("ps", 2)])
                          P.op("pe", lambda e, g=g: e.matmul(ps[2][:, 0:128], lhsT=fl(A2fi, g), rhs=fl(nCci, g), start=False, stop=True), reads=["A2fi", "nCci"], writes=[("ps", 2)])
                          P.op("pe", lambda e, g=g: e.matmul(ps[3][:, 0:128], lhsT=fl(A2br, g), rhs=fl(Ccr, g), start=True, stop=False), reads=["A2br", "Ccr"], writes=[("ps", 3)])
                          P.op("pe", lambda e, g=g: e.matmul(ps[3][:, 0:128], lhsT=fl(A2bi, g), rhs=fl(nCci, g), start=False, stop=True), reads=["A2bi", "nCci"], writes=[("ps", 3)])
                          tt(mt1[:], ps[2][:, 0:128], c128[:, 2, :], ALU.mult, [("ps", 2), "c128"], ["mt1"])
                          tt(mt2[:], ps[3][:, 0:128], c128[:, 3, :], ALU.mult, [("ps", 3), "c128"], ["mt2"])
                          tt(mt1[:], mt1[:], mt2[:], ALU.add, ["mt1", "mt2"], ["mt1"])
                          dv(lambda e, g=g: e.scalar_tensor_tensor(M0[:, g, :], c128[:, 1, :], dtile[:, g:g + 1], mt1[:], ALU.mult, ALU.add),
                             ["mt1", "c128", "spm"], [("M0", g)])
                      P.barrier()
                      P.emit()
                  if 'D' in STAGES:
                    U = sb("U", [128, 16, NCH], BF16)
                    for j in range(2):
                        for g8 in range(8):
                            for s in range(8):
                                P.dma("sp" if (g8 + s) % 2 == 0 else "pool", U[s * 16:(s + 1) * 16, j * 8 + g8, :],
                                      UT[g8 * 16:(g8 + 1) * 16, j, s * NCH:(s + 1) * NCH], writes=[("U", j * 8 + g8, s)])
                    Ukeys = lambda g: [("U", g, s) for s in range(8)]
                    Xr = sb("Xr", [128, 8, NCH + 2], BF16)
                    Xi = sb("Xi", [128, 8, NCH + 2], BF16)
                    YG = [sb("YG%d" % i, [128, NCH]) for i in range(2)]
                    Tr = sb("Tr", [128, 8, NCH]); Ti = sb("Ti", [128, 8, NCH])
                    Sr = sb("Sr", [128, 8, NCH], BF16); Si = sb("Si", [128, 8, NCH], BF16)
                    ta = sb("ta", [128, 8, 256]); tb_ = sb("tbb", [128, 8, 256])
                    ta2 = ta[:, 0:2, :].rearrange("p a b -> p (a b)"); tb2 = tb_[:, 0:2, :].rearrange("p a b -> p (a b)")
                    sgnv = sb("sgnv", [128, 1]); u2r = sb("u2r", [128, 8]); u2i = sb("u2i", [128, 8]); u3r = sb("u3r", [128, 8]); u3i = sb("u3i", [128, 8])
                    for h in range(2):
                        G8 = range(h * 8, h * 8 + 8)
                        gs = slice(h * 8, h * 8 + 8)
                        dv(lambda e: e.memset(Xr[:], 0.0), [("Xr", gl) for gl in range(8)], [("Xr", gl) for gl in range(8)])
                        dv(lambda e: e.memset(Xi[:], 0.0), [("Xi", gl) for gl in range(8)], [("Xi", gl) for gl in range(8)])
                        dv(lambda e: e.memset(Tr[:, :, 0:1], 1.0), ["T"], ["T"])
                        dv(lambda e: e.memset(Ti[:, :, 0:1], 0.0), ["T"], ["T"])
                        if h == 0:
                            tt(sgnv[:], spm[:, 4, 0:1], spm[:, 5, 0:1], ALU.subtract, ["spm"], ["sgnv"])
                        dv(lambda e, gs=gs: e.tensor_copy(u2r[:], ur[:, gs]), ["ur", "u2"], ["u2"])
                        dv(lambda e, gs=gs: e.tensor_scalar(u2i[:], ui[:, gs], sgnv[:, 0:1], None, ALU.mult), ["ui", "u2", "sgnv"], ["u2"])
                        m = 1
                        while m < NCH:
                            bq = lambda a, m=m: a[:, :].unsqueeze(2).to_broadcast([128, 8, m])
                            cmul(Tr[:, :, m:2 * m], Ti[:, :, m:2 * m], Tr[:, :, 0:m], Ti[:, :, 0:m], bq(u2r), bq(u2i), ["T", "u2"], "T", "T",
                                 ta[:, :, 0:m], tb_[:, :, 0:m], "tab")
                            cmul(u3r[:], u3i[:], u2r[:], u2i[:], u2r[:], u2i[:], ["u2", "T"], "u3", "u3", ta[:, :, 0], tb_[:, :, 0], "tab")
                            dv(lambda e: e.tensor_copy(u2r[:], u3r[:]), ["u3", "T"], ["u2"])
                            dv(lambda e: e.tensor_copy(u2i[:], u3i[:]), ["u3", "T", "u2"], ["u2"])
                            attn_step()
                            m *= 2
                        pa2 = ta[:, 2:4, :].rearrange("p a b -> p (a b)"); pb2 = tb_[:, 2:4, :].rearrange("p a b -> p (a b)")

                        def tp(out, a, b, op, reads, writes):
                            P.op("pool", lambda e: e.tensor_tensor(out, a, b, op), reads=reads, writes=writes)

                        def s_mm(g):
                            b_re, b_im = (2, 3) if g % 2 == 0 else (6, 7)
                            P.op("pe", lambda e: e.matmul(ps[b_re][:, :], lhsT=WSr[:, g, :], rhs=U[:, g, :], start=True, stop=True),
                                 reads=Ukeys(g), writes=[("ps", b_re)])
                            P.op("pe", lambda e: e.matmul(ps[b_im][:, :], lhsT=WSi[:, g, :], rhs=U[:, g, :], start=True, stop=True),
                                 reads=Ukeys(g), writes=[("ps", b_im)])

                        if h == 0:
                            s_mm(0)
                        for g in G8:
                            gl = g - h * 8
                            b_re, b_im = (2, 3) if g % 2 == 0 else (6, 7)
                            tt(ta2, Tr[:, gl, :], ps[b_re][:, :], ALU.mult, ["T", ("ps", b_re)], ["ta2"])
                            tt(tb2, Ti[:, gl, :], ps[b_im][:, :], ALU.mult, ["T", ("ps", b_im)], ["tb2"])
                            tt(Sr[:, gl, :], ta2, tb2, ALU.add, ["ta2", "tb2"], [("Sr", gl)])
                            tt(ta2, Tr[:, gl, :], ps[b_im][:, :], ALU.mult, ["T", ("ps", b_im)], ["ta2"])
                            tt(tb2, Ti[:, gl, :], ps[b_re][:, :], ALU.mult, ["T", ("ps", b_re)], ["tb2"])
                            tt(Si[:, gl, :], ta2, tb2, ALU.subtract, ["ta2", "tb2"], [("Si", gl)])
                            for (Sx, nm_) in ((Sr, "Sr"), (Si, "Si")):
                                dv(lambda e, Sx=Sx, g=g, gl=gl: e.tensor_tensor_scan(Sx[0:64, gl, :], rho[0:64, g:g + 1].to_broadcast([64, NCH]), Sx[0:64, gl, :], 0.0, ALU.mult, ALU.add),
                                   [(nm_, gl), "rho"], [(nm_, gl)])
                                dv(lambda e, Sx=Sx, g=g, gl=gl: e.tensor_tensor_scan(Sx[64:128, gl, ::-1], rho[64:128, g:g + 1].to_broadcast([64, NCH]), Sx[64:128, gl, ::-1], 0.0, ALU.mult, ALU.add),
                                   [(nm_, gl), "rho"], [(nm_, gl)])
                            tp(pa2, Tr[:, gl, :], Sr[:, gl, :], ALU.mult, ["T", ("Sr", gl)], ["pa2"])
                            tp(pb2, Ti[:, gl, :], Si[:, gl, :], ALU.mult, ["T", ("Si", gl)], ["pb2"])
                            tp(Xr[:, gl, 1:NCH + 1], pa2, pb2, ALU.subtract, ["pa2", "pb2"], [("Xr", gl)])
                            tp(pa2, Tr[:, gl, :], Si[:, gl, :], ALU.mult, ["T", ("Si", gl)], ["pa2"])
                            tp(pb2, Ti[:, gl, :], Sr[:, gl, :], ALU.mult, ["T", ("Sr", gl)], ["pb2"])
                            tp(Xi[:, gl, 1:NCH + 1], pa2, pb2, ALU.add, ["pa2", "pb2"], [("Xi", gl)])
                            if g + 1 < 16:
                                s_mm(g + 1)
                            attn_step()
                            pk = ("ps", b_re)
                            mm = [(M0[:, g, :], U[:, g, :], Ukeys(g)),
                                  (fl(Cfr, g), Xr[:, gl, 0:NCH], [("Xr", gl)]),
                                  (fl(Cfi, g), Xi[:, gl, 0:NCH], [("Xi", gl)]),
                                  (fl(Cbr, g), Xr[:, gl, 2:NCH + 2], [("Xr", gl)]),
                                  (fl(Cbi, g), Xi[:, gl, 2:NCH + 2], [("Xi", gl)])]
                            for i, (lt, rh, rk) in enumerate(mm):
                                P.op("pe", lambda e, lt=lt, rh=rh, i=i, b_re=b_re: e.matmul(ps[b_re][:, :], lhsT=lt, rhs=rh, start=(i == 0), stop=(i == 4)), reads=rk, writes=[pk])
                            yg = YG[g % 2]
                            P.op("act", lambda e, yg=yg, b_re=b_re: e.activation(out=yg[:], in_=ps[b_re][:, :], func=AF.Gelu_apprx_tanh), reads=[pk], writes=[("YG", g % 2)])
                            for t in range(8):
                                P.dma("sp" if t % 2 == 0 else "pool", EX[256 + g * 16:256 + (g + 1) * 16, t * NCH:(t + 1) * NCH], yg[t * 16:(t + 1) * 16, :], reads=[("YG", g % 2)])
                        if h == 1:
                            attn_flush()
                        if h == 0:
                            dv(lambda e: e.nop(), ["pa2", "pb2", "ta2", "tb2"], ["taba", "tabb"])
                    P.barrier()
                    P.emit()
    return nc


def build_phase2():
    nc = bass.Bass("TRN2", target_bir_lowering=False)
    mixin = _din(nc, "mixin", [128, 16, 1024])
    x2 = _din(nc, "x2", [2, 128, 16, 512])
    pTd = _din(nc, "pT", [128, 2, 1024])
    wgs = _din(nc, "wgs", [8, 128, 16, 128])
    wglu = _din(nc, "wglu", [16, 128, 8, 128])
    wout = _din(nc, "wout", [16, 128, 16, 128])
    wpg = _din(nc, "wpg", [16, 128, 16, 128])
    wpp = _din(nc, "wpp", [16, 128, 2, 128])
    vec_d = _din(nc, "vecs", [128, 4, 16])
    outT = nc.dram_tensor("outT", [128, 16, 1024], F32, kind="ExternalOutput").ap()
    with ExitStack() as st:
        P = Prog(nc, st)
        phase2_body(nc, P, st, dict(mixin=mixin, x2=x2, pTd=pTd, wgs=wgs, wglu=wglu, wout=wout, wpg=wpg, wpp=wpp, vec_d=vec_d, outT=outT), None, None)
    return nc


def phase2_body(nc, P, st, D, mixA, ysA):
    if True:
        sb = lambda name, shape, dt=F32: st.enter_context(nc.sbuf_tensor(name, list(shape), dt))
        ps = [st.enter_context(nc.psum_tensor("qs%d" % i, [128, 512], F32)) for i in range(8)]
        vecs = sb("vecss", [128, 4, 16])
        ones = sb("ones2", [128, 128], BF16)
        P.dma("sp", vecs[:], D["vec_d"], writes=["vecs"])
        P.op("dve", lambda e: e.memset(ones[:], 1.0), writes=["ones2"])
        NW = 4
        xn = sb("xn2", [128, 16, 1024], BF16)
        mix = sb("mix2", [128, 16, 1024], BF16)
        pTb = sb("pTb", [128, 2, 1024], BF16)
        rt = sb("rt2", [128, 512])
        rstd = sb("rstd2", [128, 1024])
        ga = sb("ga2", [128, 512]); sgb = sb("sgb2", [128, 512]); gsl = sb("gsl2", [128, 512])
        ot = [sb("ot2_0", [128, 512])] * 2
        xres = [sb("xres%d" % i, [128, 1024]) for i in range(2)]
        wcount = [0]
        casters = ["act", "dve"]
        WS = {}

        def load_w(src_tile, K):
            wst, wb = WS["wst"], WS["wb"]
            s = wcount[0] % len(wb)
            s2 = wcount[0] % len(wst)
            eng = casters[wcount[0] % len(casters)]
            wcount[0] += 1
            P.dma("sp", wst[s2][:, 0:K, :], src_tile, writes=[("wst", s2)])
            if eng == "act":
                P.op("act", lambda e: e.activation(out=wb[s][:, 0:K, :], in_=wst[s2][:, 0:K, :], func=AF.Copy), reads=[("wst", s2)], writes=[("wb", s)])
            else:
                P.op(eng, lambda e: e.tensor_copy(wb[s][:, 0:K, :], wst[s2][:, 0:K, :]), reads=[("wst", s2)], writes=[("wb", s)])
            return wb[s], ("wb", s)

        def rms_bcast(src_blk, key, sqt, blk):
            for hh in range(2):
                P.op("act", lambda e, hh=hh: e.activation(out=sqt[:], in_=src_blk[:, hh * 8:(hh + 1) * 8, :], func=AF.Square), reads=[key], writes=["sq"])
                for kc in range(8):
                    P.op("pe", lambda e, kc=kc, hh=hh: e.matmul(ps[7][:, :], lhsT=ones[:], rhs=sqt[:, kc, :], start=(hh == 0 and kc == 0), stop=(hh == 1 and kc == 7)),
                         reads=["sq", "ones2"], writes=[("ps", 7)])
            P.op("act", lambda e: e.activation(out=rt[:], in_=ps[7][:, :], func=AF.Sqrt, scale=1.0 / 2048, bias=EPS), reads=[("ps", 7)], writes=["rt"])
            P.op("dve", lambda e: e.reciprocal(rstd[:, blk * 512:(blk + 1) * 512], rt[:]), reads=["rt"], writes=[("rstd", blk)])

        def norm_to_bf16(dst_blk, src_blk, srckey, row, blk, dkey):
            for k in range(16):
                P.op("dve", lambda e, k=k: e.scalar_tensor_tensor(dst_blk[:, k, :], src_blk[:, k, :], vecs[:, row, k:k + 1], rstd[:, blk * 512:(blk + 1) * 512], ALU.mult, ALU.mult)
                     if True else None, reads=[srckey, ("rstd", blk), "vecs"], writes=[(dkey, blk, k)])

        sG = ExitStack()
        ys = sG.enter_context(nc.sbuf_tensor("ys2", [128, 8, 1024], BF16)) if ysA is None else ysA
        with ExitStack() as s1:
            sb1 = lambda name, shape, dt=F32: s1.enter_context(nc.sbuf_tensor(name, list(shape), dt))
            xb = sb1("xb2", [128, 16, 512])
            sq = sb1("sq2a", [128, 8, 512], BF16)
            mixf = sb1("mixf", [128, 8, 512])
            pTf = sb1("pTfs", [128, 2, 1024])
            P.dma("sp", pTf[:], D["pTd"], writes=["pTf"])
            P.op("dve", lambda e: e.tensor_copy(pTb[:], pTf[:]), reads=["pTf"], writes=["pTb"])
            for blk in range(2):
                tk = slice(blk * 512, (blk + 1) * 512)
                P.dma("sp", xb[:], D["x2"][blk], writes=["xb"])
                if mixA is None:
                    P.dma("pool", mixf[:], D["mixin"][:, 8:16, tk], writes=["mixf"])
                    P.op("act", lambda e, tk=tk: e.activation(out=ys[:, :, tk], in_=mixf[:], func=AF.Copy), reads=["mixf"], writes=[("ys", blk)])
                    P.dma("pool", mixf[:], D["mixin"][:, 0:8, tk], writes=["mixf"])
                    P.op("dve", lambda e, tk=tk: e.tensor_copy(mix[:, 0:8, tk], mixf[:]), reads=["mixf"], writes=[("mixa", blk)])
                rms_bcast(xb[:], "xb", sq, blk)
                for k in range(16):
                    P.op("dve", lambda e, k=k, tk=tk, blk=blk: e.scalar_tensor_tensor(xn[:, k, tk], xb[:, k, :], vecs[:, 0, k:k + 1], rstd[:, tk], ALU.mult, ALU.mult),
                         reads=["xb", ("rstd", blk), "vecs"], writes=[("xn", blk, k)])
            if mixA is not None:
                P.op("pool", lambda e: e.tensor_copy(mix[:, 0:8, :], mixA[:]), reads=[], writes=[("mixa", 0)])
            P.barrier()
            P.emit()
        WS["wst"] = [sG.enter_context(nc.sbuf_tensor("wstG%d" % i, [128, 16, 128], F32)) for i in range(6)]
        WS["wb"] = [sG.enter_context(nc.sbuf_tensor("wbG%d" % i, [128, 16, 128], BF16)) for i in range(6)]
        for m in range(8):
            wa, ka = load_w(D["wglu"][m], 8)
            wbb, kb = load_w(D["wglu"][8 + m], 8)
            wg, kg = load_w(D["wgs"][m], 16)
            for blk in range(2):
                tk = slice(blk * 512, (blk + 1) * 512)
                b0, b1, b2 = (0, 1, 2) if blk == 0 else (3, 4, 5)
                for k in range(8):
                    P.op("pe", lambda e, k=k, tk=tk, wa=wa, b0=b0: e.matmul(ps[b0][:, :], lhsT=wa[:, k, :], rhs=ys[:, k, tk], start=(k == 0), stop=(k == 7)), reads=[ka], writes=[("ps", b0)])
                for k in range(8):
                    P.op("pe", lambda e, k=k, tk=tk, wbb=wbb, b1=b1: e.matmul(ps[b1][:, :], lhsT=wbb[:, k, :], rhs=ys[:, k, tk], start=(k == 0), stop=(k == 7)), reads=[kb], writes=[("ps", b1)])
                for k in range(16):
                    P.op("pe", lambda e, k=k, tk=tk, wg=wg, b2=b2: e.matmul(ps[b2][:, :], lhsT=wg[:, k, :], rhs=xn[:, k, tk], start=(k == 0), stop=(k == 15)), reads=[kg], writes=[("ps", b2)])
                P.op("act", lambda e, m=m, b0=b0: e.activation(out=ga[:], in_=ps[b0][:, :], func=AF.Identity, bias=vecs[:, 3, m:m + 1]), reads=[("ps", b0), "vecs"], writes=["ga"])
                P.op("act", lambda e, m=m, b1=b1: e.activation(out=sgb[:], in_=ps[b1][:, :], func=AF.Sigmoid, bias=vecs[:, 3, 8 + m:9 + m]), reads=[("ps", b1), "vecs"], writes=["sgb"])
                P.op("act", lambda e, b2=b2: e.activation(out=gsl[:], in_=ps[b2][:, :], func=AF.Silu), reads=[("ps", b2)], writes=["gsl"])
                P.op("dve", lambda e: e.tensor_tensor(ga[:], ga[:], sgb[:], ALU.mult), reads=["ga", "sgb"], writes=["ga"])
                P.op("dve", lambda e, m=m, tk=tk: e.tensor_tensor(mix[:, 8 + m, tk], ga[:], gsl[:], ALU.mult), reads=["ga", "gsl"], writes=[("mixs", m, blk)])
        P.barrier()
        P.emit()
        sG.close()
        wcount[0] = 0
        H = sb("H2", [128, 16, 1024])
        sq = sb("sq2b", [128, 8, 512], BF16)
        WS["wst"] = [sb("wstO%d" % i, [128, 16, 128]) for i in range(4)]
        WS["wb"] = [sb("wbO%d" % i, [128, 16, 128], BF16) for i in range(3)]
        for m in range(16):
            wo, ko = load_w(D["wout"][m], 16)
            xr = xres[m % 2]
            P.dma("sp", xr[:].rearrange("p (b t) -> p b t", b=2), D["x2"][:, :, m, :].rearrange("b p t -> p b t"), writes=[("xres", m % 2)])
            for blk in range(2):
                tk = slice(blk * 512, (blk + 1) * 512)
                pb = ps[3 + (2 * m + blk) % 3]
                pk = ("ps", 3 + (2 * m + blk) % 3)
                for k in range(16):
                    P.op("pe", lambda e, k=k, tk=tk, pb=pb, wo=wo: e.matmul(pb[:, :], lhsT=wo[:, k, :], rhs=mix[:, k, tk], start=(k == 0), stop=(k == 15)),
                         reads=[ko] + ([("mixs", mm, blk) for mm in range(8)] if m == 0 else []), writes=[pk])
                P.op("dve", lambda e, m=m, tk=tk, pb=pb, xr=xr: e.tensor_tensor(H[:, m, tk], pb[:, :], xr[:, tk], ALU.add), reads=[pk, ("xres", m % 2)], writes=[("H", m, blk)])
        for blk in range(2):
            tk = slice(blk * 512, (blk + 1) * 512)
            P.op("dve", lambda e: e.nop(), reads=[("H", m, blk) for m in range(16)], writes=[("Hall", blk)])
            rms_bcast(H[:, :, tk], ("Hall", blk), sq, blk)
            for k in range(16):
                P.op("dve", lambda e, k=k, tk=tk: e.scalar_tensor_tensor(xn[:, k, tk], H[:, k, tk], vecs[:, 1, k:k + 1], rstd[:, tk], ALU.mult, ALU.mult),
                     reads=[("Hall", blk), ("rstd", blk), "vecs"], writes=[("hn", blk, k)])
        for m in range(16):
            wq, kq = load_w(D["wpg"][m], 16)
            wp_, kp = load_w(D["wpp"][m], 2)
            for blk in range(2):
                tk = slice(blk * 512, (blk + 1) * 512)
                pb = ps[(2 * m + blk) % 2]
                pk = ("ps", (2 * m + blk) % 2)
                for k in range(16):
                    P.op("pe", lambda e, k=k, tk=tk, pb=pb, wq=wq: e.matmul(pb[:, :], lhsT=wq[:, k, :], rhs=xn[:, k, tk], start=(k == 0), stop=(k == 15)),
                         reads=[kq] + ([("hn", blk, kk) for kk in range(16)] if m == 0 else []), writes=[pk])
                pb2 = ps[2 + (2 * m + blk) % 2]
                pk2 = ("ps", 2 + (2 * m + blk) % 2)
                for k in range(2):
                    P.op("pe", lambda e, k=k, tk=tk, pb2=pb2, wp_=wp_: e.matmul(pb2[:, :], lhsT=wp_[:, k, :], rhs=pTb[:, k, tk], start=(k == 0), stop=(k == 1)), reads=[kp, "pTb"], writes=[pk2])
                P.op("act", lambda e, pb=pb: e.activation(out=ga[:], in_=pb[:, :], func=AF.Sigmoid), reads=[pk], writes=["ga"])
                P.op("dve", lambda e, pb2=pb2: e.tensor_tensor(sgb[:], ga[:], pb2[:, :], ALU.mult), reads=["ga", pk2], writes=["sgb"])
                P.op("dve", lambda e, m=m, tk=tk: e.tensor_tensor(H[:, m, tk], H[:, m, tk], sgb[:], ALU.add),
                     reads=["sgb"] + [("hn", blk, kk) for kk in range(16)], writes=[("H2", m, blk)])
        for blk in range(2):
            tk = slice(blk * 512, (blk + 1) * 512)
            P.op("dve", lambda e: e.nop(), reads=[("H2", m, blk) for m in range(16)], writes=[("H2all", blk)])
            rms_bcast(H[:, :, tk], ("H2all", blk), sq, blk)
            obufs = [(ot[0], ("ot", 0)), (ga, "ga"), (sgb, "sgb"), (gsl, "gsl")]
            for m in range(16):
                o_, okey = obufs[m % 4]
                P.op("dve", lambda e, m=m, tk=tk, o_=o_: e.scalar_tensor_tensor(o_[:], H[:, m, tk], vecs[:, 2, m:m + 1], rstd[:, tk], ALU.mult, ALU.mult),
                     reads=[("H2all", blk), ("rstd", blk), "vecs"], writes=[okey])
                P.dma("sp" if m % 2 == 0 else "pool", D["outT"][:, m, tk], o_[:], reads=[okey], writes=[("out", m, blk)])
        P.barrier()
        P.emit()


_CACHE = {}


def _perm(r):
    return np.array([8 * (128 * r + cl) + t for t in range(8) for cl in range(128)], dtype=np.int64)


def _fm(a):
    F_, T_ = a.shape
    return np.ascontiguousarray(a.reshape(F_ // 128, 128, T_).transpose(1, 0, 2))


def kernel(x, p, norm_mix, w_in, q_norm, k_norm, ssm_a_re, ssm_a_im, ssm_log_dt,
           ssm_b_re, ssm_b_im, ssm_c_re, ssm_c_im, ssm_d, w_glu, b_glu, w_out,
           norm_ple, w_ple_gate, w_ple_proj, norm_final):
    f32 = np.float32
    x = np.asarray(x, f32); p = np.asarray(p, f32)
    if "p1" not in _CACHE:
        _CACHE["p1"] = build_phase1()
        _CACHE["p2"] = build_phase2()
    inv_freq = (np.float32(10000.0) ** (-np.arange(32, dtype=f32) / np.float32(32))).astype(f32)
    t = np.arange(L)
    rows = (t // 64).astype(f32); cols = (t % 64).astype(f32)
    cosT = np.zeros((128, L), f32); sinT = np.zeros((128, L), f32)
    for d in range(128):
        pos = rows if d < 64 else cols
        ang = (pos * inv_freq[d % 32]).astype(f32)
        cosT[d] = np.cos(ang); sinT[d] = np.sin(ang)
    csT = np.ascontiguousarray(np.stack([cosT, sinT], 1).reshape(128, 2, 16, 256).transpose(2, 0, 1, 3))
    Rm = np.zeros((128, 128), f32)
    for d in range(128):
        if d % 64 < 32:
            Rm[d + 32, d] = -1.0
        else:
            Rm[d - 32, d] = 1.0
    ident = np.eye(128, dtype=f32)
    si = np.arange(128) // 16
    maskF = (si[None, :] >= si[:, None]).astype(f32)
    maskB = (si[:, None] >= si[None, :]).astype(f32)
    c128 = np.ascontiguousarray(np.stack([Rm, ident, maskF, maskB], 1))
    nm = lambda v: np.ascontiguousarray(np.asarray(v, f32).reshape(-1, 128).T)
    w_in0 = np.asarray(w_in[0], f32)
    in1, in2 = [], []
    for c in range(8):
        b, r = c // 4, c % 4
        kv = r // 2
        G0 = 16 * r
        colsel = np.concatenate([np.arange(256 * r, 256 * r + 256), np.arange(1024 + 128 * kv, 1024 + 128 * kv + 128),
                                 np.arange(1280 + 128 * kv, 1280 + 128 * kv + 128), np.arange(1536 + 256 * r, 1536 + 256 * r + 256),
                                 np.arange(2560 + 256 * r, 2560 + 256 * r + 256)])
        w1 = np.ascontiguousarray(_fm(w_in0[:, colsel]).reshape(128, 16, 8, 128).transpose(2, 0, 1, 3))
        if r == 0:
            xTb = np.ascontiguousarray(_fm(np.ascontiguousarray(x[b].T)).reshape(128, 16, 16, 256).transpose(2, 0, 1, 3))
        qkn = np.ascontiguousarray(np.stack([q_norm[0], k_norm[0]], 1).astype(f32))
        dp = lambda a: np.ascontiguousarray(np.asarray(a[0], f32)[:, G0:G0 + 16, :].transpose(0, 2, 1).reshape(128, 16))
        are = dp(ssm_a_re); aim = dp(ssm_a_im)
        ldt = np.ascontiguousarray(np.repeat(np.asarray(ssm_log_dt[0], f32)[:, None, G0:G0 + 16], 64, 1).reshape(128, 16))
        dt_ = np.asarray(ssm_d[0], f32)[G0 * 16:(G0 + 16) * 16].reshape(16, 16)
        dtile = np.ascontiguousarray(np.tile(dt_.T, (8, 1)))
        mfv = np.zeros((128, 16), f32); mfv[:64] = 1.0
        mbv = np.zeros((128, 16), f32); mbv[64:] = 1.0
        ssmp = np.ascontiguousarray(np.stack([are, aim, ldt, dtile, mfv, mbv], 1))
        Bl = lambda a: np.asarray(a[0], f32)[:, G0:G0 + 16].transpose(0, 2, 1, 3).reshape(128, 16, 16)
        Cl = lambda a: np.asarray(a[0], f32)[:, G0:G0 + 16].transpose(0, 3, 1, 2).reshape(128, 16, 16)
        ssmbc = np.ascontiguousarray(np.stack([Bl(ssm_b_re), Bl(ssm_b_im), Cl(ssm_c_re), Cl(ssm_c_im)], 1))
        in1.append({"xT": xTb, "w1": w1, "nmix": nm(norm_mix[0]), "qkn": qkn, "csT": csT, "c128": c128,
                    "ssmp": ssmp, "ssmbc": ssmbc})
    res1 = run_bass_kernel_spmd(_CACHE["p1"], in1, core_ids=list(range(8)))
    EXs = [np.asarray(r["EX"], f32) for r in res1.results]
    if os.environ.get("KP1ONLY"):
        return EXs
    vecs = np.ascontiguousarray(np.stack([nm(norm_mix[0]), nm(norm_ple[0]), nm(norm_final), nm(b_glu[0])], 1))
    tl = lambda w: np.ascontiguousarray(_fm(w).reshape(128, w.shape[0] // 128, w.shape[1] // 128, 128).transpose(2, 0, 1, 3))
    wgs = tl(w_in0[:, 3584:4608]); wglu = tl(np.asarray(w_glu[0], f32)); wout = tl(np.asarray(w_out[0], f32))
    wpg = tl(np.asarray(w_ple_gate[0], f32)); wpp = tl(np.asarray(w_ple_proj[0], f32))
    for c in range(8):
        b, r = c // 4, c % 4
        pr = _perm(r)
        sel = np.concatenate([np.arange(tt * NCH + 128 * r, tt * NCH + 128 * r + 128) for tt in range(8)])
        attn = np.concatenate([EXs[b * 4 + q][0:256][:, sel] for q in range(4)], 0)
        ssm = np.concatenate([EXs[b * 4 + q][256:512][:, sel] for q in range(4)], 0)
        mixin = _fm(np.concatenate([attn, ssm], 0))
        x2 = np.ascontiguousarray(_fm(np.ascontiguousarray(x[b].T[:, pr])).reshape(128, 16, 2, 512).transpose(2, 0, 1, 3))
        pT = _fm(np.ascontiguousarray(p[0, b].T[:, pr]))
        in2.append({"mixin": mixin, "x2": x2, "pT": pT, "wgs": wgs, "wglu": wglu, "wout": wout, "wpg": wpg, "wpp": wpp, "vecs": vecs})
    res2 = run_bass_kernel_spmd(_CACHE["p2"], in2, core_ids=list(range(8)))
    out = np.zeros((2, L, 2048), f32)
    for c in range(8):
        b, r = c // 4, c % 4
        o = np.asarray(res2.results[c]["outT"], f32)
        o = o.transpose(1, 0, 2).reshape(2048, 1024)
        out[b, _perm(r), :] = o.T
    return out
```
